# Optimizing a Trainium2 kernel written in Bass

```python
import jax, jax.numpy as jnp
from jax import lax
import numpy as np

D_MODEL = 1024
BATCH = 4
SEQ = 4096
DEPTH = 1

MLSTM_HEADS = 4
MLSTM_QK_DIM = 128
MLSTM_V_DIM = D_MODEL // MLSTM_HEADS
MLSTM_CHUNK = 64
CONV_WIDTH = 4
MLA_HEADS = 8
MLA_NOPE_DIM = 128
MLA_ROPE_DIM = 64
MLA_V_DIM = D_MODEL // MLA_HEADS
MLA_Q_RANK = 384
MLA_KV_RANK = 256
ROPE_THETA = 10000.0
ATTN_BLOCK = 128
D_FF = 4 * D_MODEL
EPS = 1e-6
N_MOD = 6

MA_QK = MLSTM_HEADS * MLSTM_QK_DIM
MA_V = MLSTM_HEADS * MLSTM_V_DIM
SPLIT_SIZES = (MA_QK, MA_QK, MA_V, MA_V, MLSTM_HEADS, MLSTM_HEADS,
               MLA_Q_RANK, MLA_KV_RANK, MLA_ROPE_DIM, D_MODEL, D_MODEL)
D_IN = sum(SPLIT_SIZES)
SPLIT_IDX = tuple(int(v) for v in np.cumsum(SPLIT_SIZES)[:-1])

kernel_name = 'hybrid_mlstm_mla_block'


def rmsnorm(x, w):
    xf = x.astype(jnp.float32)
    y = xf * lax.rsqrt(jnp.mean(xf * xf, axis=-1, keepdims=True) + EPS)
    return (y * w.astype(jnp.float32)).astype(x.dtype)


def causal_dwconv(x, w, b):
    ch = x.shape[-1]
    y = lax.conv_general_dilated(x, w[:, None, :].astype(x.dtype), window_strides=(1,),
                                 padding=[(CONV_WIDTH - 1, 0)],
                                 dimension_numbers=('NWC', 'WIO', 'NWC'),
                                 feature_group_count=ch)
    return y + b.astype(x.dtype)


def rope(x, positions):
    half = x.shape[-1] // 2
    inv_freq = ROPE_THETA ** (-jnp.arange(half, dtype=jnp.float32) / half)
    ang = positions.astype(jnp.float32)[:, None] * inv_freq[None, :]
    cos, sin = jnp.cos(ang), jnp.sin(ang)
    x1, x2 = x[..., :half], x[..., half:]
    return jnp.concatenate([x1 * cos - x2 * sin, x2 * cos + x1 * sin], axis=-1)


def mlstm_branch(q, k, v, o_pre, i_pre, f_pre, head_norm_w):
    bsz, seq, _ = q.shape
    H, L = MLSTM_HEADS, MLSTM_CHUNK
    nc = seq // L
    f32 = jnp.float32

    def chunks(t, d):
        return t.astype(f32).reshape(bsz, nc, L, H, d).transpose(1, 0, 3, 2, 4)

    def gchunks(t):
        return t.astype(f32).reshape(bsz, nc, L, H).transpose(1, 0, 3, 2)

    qc = chunks(q, MLSTM_QK_DIM)
    kc = chunks(k, MLSTM_QK_DIM) * (MLSTM_QK_DIM ** -0.5)
    vc = chunks(v, MLSTM_V_DIM)
    igc = gchunks(i_pre)
    lfc = jax.nn.log_sigmoid(gchunks(f_pre))
    causal = jnp.tril(jnp.ones((L, L), dtype=bool))

    def step(carry, xs):
        C, n, m = carry
        qb, kb, vb, ib, lb = xs
        b = jnp.cumsum(lb, axis=-1)
        a = b + m[..., None]
        D = jnp.where(causal, b[..., :, None] - b[..., None, :] + ib[..., None, :], -jnp.inf)
        m_out = jnp.maximum(a, jnp.max(D, axis=-1))
        s = jnp.einsum('bhtd,bhsd->bhts', qb, kb) * jnp.exp(D - m_out[..., None])
        w_inter = jnp.exp(a - m_out)
        num = (jnp.einsum('bhts,bhsv->bhtv', s, vb)
               + w_inter[..., None] * jnp.einsum('bhvd,bhtd->bhtv', C, qb))
        den = jnp.sum(s, axis=-1) + w_inter * jnp.einsum('bhd,bhtd->bht', n, qb)
        h = num / jnp.maximum(jnp.abs(den), jnp.exp(-m_out))[..., None]
        g_prev = b[..., -1] + m
        g = b[..., -1:] - b + ib
        m_new = jnp.maximum(g_prev, jnp.max(g, axis=-1))
        wk = jnp.exp(g - m_new[..., None])
        decay = jnp.exp(g_prev - m_new)
        C_new = decay[..., None, None] * C + jnp.einsum('bhs,bhsv,bhsd->bhvd', wk, vb, kb)
        n_new = decay[..., None] * n + jnp.einsum('bhs,bhsd->bhd', wk, kb)
        return (C_new, n_new, m_new), h

    init = (jnp.zeros((bsz, H, MLSTM_V_DIM, MLSTM_QK_DIM), f32),
            jnp.zeros((bsz, H, MLSTM_QK_DIM), f32),
            jnp.zeros((bsz, H), f32))
    _, h = lax.scan(step, init, (qc, kc, vc, igc, lfc))
    h = h.transpose(1, 0, 3, 2, 4).reshape(bsz, seq, H, MLSTM_V_DIM)
    h = h * lax.rsqrt(jnp.mean(h * h, axis=-1, keepdims=True) + EPS)
    h = h.reshape(bsz, seq, MA_V) * head_norm_w.astype(f32)
    y = jax.nn.sigmoid(o_pre.astype(f32)) * h
    return y.astype(q.dtype)


def mla_branch(c_q, c_kv, k_rope, q_norm_w, kv_norm_w, w_uq, w_ukv):
    bsz, seq, _ = c_q.shape
    H = MLA_HEADS
    f32 = jnp.float32
    positions = jnp.arange(seq, dtype=jnp.int32)
    q = (rmsnorm(c_q, q_norm_w) @ w_uq).reshape(bsz, seq, H, MLA_NOPE_DIM + MLA_ROPE_DIM)
    kv = (rmsnorm(c_kv, kv_norm_w) @ w_ukv).reshape(bsz, seq, H, MLA_NOPE_DIM + MLA_V_DIM)
    q = q.astype(f32).transpose(0, 2, 1, 3)
    kv = kv.astype(f32).transpose(0, 2, 1, 3)
    q_nope, q_pe = q[..., :MLA_NOPE_DIM], rope(q[..., MLA_NOPE_DIM:], positions)
    k_nope, v = kv[..., :MLA_NOPE_DIM], kv[..., MLA_NOPE_DIM:]
    k_pe = rope(k_rope.astype(f32), positions)
    scale = (MLA_NOPE_DIM + MLA_ROPE_DIM) ** -0.5
    outs = []
    for i in range(seq // ATTN_BLOCK):
        start, end = i * ATTN_BLOCK, (i + 1) * ATTN_BLOCK
        sc = (jnp.einsum('bhqd,bhkd->bhqk', q_nope[:, :, start:end], k_nope[:, :, :end])
              + jnp.einsum('bhqr,bkr->bhqk', q_pe[:, :, start:end], k_pe[:, :end])) * scale
        mask = (start + jnp.arange(ATTN_BLOCK))[:, None] >= jnp.arange(end)[None, :]
        p = jax.nn.softmax(jnp.where(mask, sc, -jnp.inf), axis=-1)
        outs.append(jnp.einsum('bhqk,bhkv->bhqv', p, v[:, :, :end]))
    o = jnp.concatenate(outs, axis=2)
    return o.transpose(0, 2, 1, 3).reshape(bsz, seq, H * MLA_V_DIM).astype(c_q.dtype)


def token_mixer(h, w_in, conv_w, conv_b, gate_b, head_norm_w, q_norm_w, kv_norm_w,
                w_uq, w_ukv, w_out):
    (q_a, k_a, v_a, o_a, i_a, f_a, c_q, c_kv, k_pe, g_a, g_b) = jnp.split(h @ w_in, SPLIT_IDX, axis=-1)
    qk = jax.nn.silu(causal_dwconv(jnp.concatenate([q_a, k_a], axis=-1), conv_w, conv_b))
    q_a, k_a = qk[..., :MA_QK], qk[..., MA_QK:]
    i_a = i_a + gate_b[:MLSTM_HEADS].astype(h.dtype)
    f_a = f_a + gate_b[MLSTM_HEADS:].astype(h.dtype)
    y_a = mlstm_branch(q_a, k_a, v_a, o_a, i_a, f_a, head_norm_w)
    y_b = mla_branch(c_q, c_kv, k_pe, q_norm_w, kv_norm_w, w_uq, w_ukv)
    y = jax.nn.sigmoid(g_a) * y_a + jax.nn.sigmoid(g_b) * y_b
    return y @ w_out


def setup_inputs(seed: int = 0) -> dict:
    key = jax.random.key(seed)
    ks = jax.random.split(key, 24)
    nrm = lambda k, shape, s: jax.random.normal(k, shape, jnp.float32) * s
    gain = lambda k, n: 1.0 + nrm(k, (DEPTH, n), 0.05)
    gate_b = jnp.concatenate([nrm(ks[10], (DEPTH, MLSTM_HEADS), 0.1),
                              3.0 + nrm(ks[11], (DEPTH, MLSTM_HEADS), 0.5)], axis=-1)
    return {
        'x': nrm(ks[0], (BATCH, SEQ, D_MODEL), 1.0),
        'c': nrm(ks[1], (BATCH, D_MODEL), 1.0),
        'w_ada': nrm(ks[2], (DEPTH, D_MODEL, N_MOD * D_MODEL), 0.3 * D_MODEL ** -0.5),
        'b_ada': nrm(ks[3], (DEPTH, N_MOD * D_MODEL), 0.02),
        'norm_pre_mix': gain(ks[4], D_MODEL),
        'norm_post_mix': gain(ks[5], D_MODEL),
        'norm_pre_mlp': gain(ks[6], D_MODEL),
        'norm_post_mlp': gain(ks[7], D_MODEL),
        'w_in': nrm(ks[8], (DEPTH, D_MODEL, D_IN), D_MODEL ** -0.5),
        'mlstm_conv_w': nrm(ks[9], (DEPTH, CONV_WIDTH, 2 * MA_QK), CONV_WIDTH ** -0.5),
        'mlstm_conv_b': nrm(ks[12], (DEPTH, 2 * MA_QK), 0.02),
        'mlstm_gate_b': gate_b,
        'mlstm_head_norm': gain(ks[13], MA_V),
        'mla_q_norm': gain(ks[14], MLA_Q_RANK),
        'mla_kv_norm': gain(ks[15], MLA_KV_RANK),
        'w_uq': nrm(ks[16], (DEPTH, MLA_Q_RANK, MLA_HEADS * (MLA_NOPE_DIM + MLA_ROPE_DIM)), MLA_Q_RANK ** -0.5),
        'w_ukv': nrm(ks[17], (DEPTH, MLA_KV_RANK, MLA_HEADS * (MLA_NOPE_DIM + MLA_V_DIM)), MLA_KV_RANK ** -0.5),
        'w_out': nrm(ks[18], (DEPTH, D_MODEL, D_MODEL), D_MODEL ** -0.5),
        'w_ff1': nrm(ks[19], (DEPTH, D_MODEL, D_FF), D_MODEL ** -0.5),
        'w_ff2': nrm(ks[20], (DEPTH, D_FF, D_MODEL), D_FF ** -0.5),
    }


def reference(x, c, w_ada, b_ada, norm_pre_mix, norm_post_mix, norm_pre_mlp, norm_post_mlp,
              w_in, mlstm_conv_w, mlstm_conv_b, mlstm_gate_b, mlstm_head_norm,
              mla_q_norm, mla_kv_norm, w_uq, w_ukv, w_out, w_ff1, w_ff2):
    for l in range(DEPTH):
        mod = jax.nn.silu(c) @ w_ada[l] + b_ada[l]
        sh_m, sc_m, gt_m, sh_f, sc_f, gt_f = [t[:, None, :] for t in jnp.split(mod, N_MOD, axis=-1)]
        h = rmsnorm(x, norm_pre_mix[l]) * (1.0 + sc_m) + sh_m
        y = token_mixer(h, w_in[l], mlstm_conv_w[l], mlstm_conv_b[l], mlstm_gate_b[l],
                        mlstm_head_norm[l], mla_q_norm[l], mla_kv_norm[l],
                        w_uq[l], w_ukv[l], w_out[l])
        x = x + gt_m * rmsnorm(y, norm_post_mix[l])
        h = rmsnorm(x, norm_pre_mlp[l]) * (1.0 + sc_f) + sh_f
        y = jnp.square(jax.nn.relu(h @ w_ff1[l])) @ w_ff2[l]
        x = x + gt_f * rmsnorm(y, norm_post_mlp[l])
    return x
```

```python
import contextlib
import os
import numpy as np
import concourse.bass as bass
import concourse.mybir as mybir
from concourse.bass_utils import run_bass_kernel_spmd

F32 = mybir.dt.float32
BF16 = mybir.dt.bfloat16
AF = mybir.ActivationFunctionType
ALU = mybir.AluOpType
AX = mybir.AxisListType
AP = bass.AP

SAME_ENGINE_SYNC = True
SYNC_ALL = False
EPS = 1e-6
S = 4096
D = 1024
NOWN = 2048


class Res:
    __slots__ = ("name", "writer", "readers", "small")

    def __init__(self, name="", small=False):
        self.name = name
        self.writer = None
        self.readers = {}
        self.small = small


class Prog:
    ENGS = ("pe", "act", "dve", "pool", "sp")

    def __init__(self, nc):
        self.nc = nc
        self.q = {e: [] for e in self.ENGS}
        self.count = {}
        self.waited = {e: {} for e in self.ENGS}
        self.fence = {}

    def fence_all(self):
        self.fence = dict(self.count)

    def _add(self, queue, track, fn, reads, writes):
        idx = self.count.get(track, 0) + 1
        self.count[track] = idx
        waits = {}

        def need(dep, res=None):
            t2, i2 = dep
            if t2 == track:
                if track == "pe" or track.startswith("dma") or not SAME_ENGINE_SYNC:
                    return
                if res is None or not (res.small or SYNC_ALL):
                    return
            if self.waited[queue].get(t2, 0) >= i2:
                return
            waits[t2] = max(waits.get(t2, 0), i2)

        for t2, i2 in self.fence.items():
            if t2 != track:
                need((t2, i2))
        for r in reads:
            if r.writer is not None:
                need(r.writer, r)
        for w in writes:
            if w.writer is not None:
                need(w.writer, w)
            for t2, i2 in w.readers.items():
                need((t2, i2), w)
        for t2, i2 in waits.items():
            self.waited[queue][t2] = i2
        for r in reads:
            r.readers[track] = idx
        for w in writes:
            w.writer = (track, idx)
            w.readers = {}
        self.q[queue].append((fn, waits, track, idx))

    def op(self, eng, fn, reads=(), writes=()):
        self._add(eng, eng, fn, reads, writes)

    def dma(self, queue, key, fn, reads=(), writes=()):
        self._add(queue, "dma:" + key, fn, reads, writes)

    def emit(self, final_waits=()):
        nc = self.nc
        needed = {}
        for e in self.ENGS:
            for fn, waits, track, idx in self.q[e]:
                for t2, i2 in waits.items():
                    needed.setdefault(t2, set()).add(i2)
        fin = {}
        for r in final_waits:
            if r.writer is not None:
                t2, i2 = r.writer
                fin[t2] = max(fin.get(t2, 0), i2)
                needed.setdefault(t2, set()).add(i2)
        rank = {}
        for t, s_ in needed.items():
            if t.startswith("dma"):
                rank[t] = {i: i for i in range(1, self.count[t] + 1)}
            else:
                rank[t] = {i: k + 1 for k, i in enumerate(sorted(s_))}
        tracks = sorted(needed.keys())
        with contextlib.ExitStack() as st:
            sems = {t: st.enter_context(nc.semaphore("s_" + t.replace(":", "_"))) for t in tracks}
            block = st.enter_context(nc.Block())

            def run(ename, e):
                for fn, waits, track, idx in self.q[ename]:
                    for t2, i2 in waits.items():
                        e.wait_ge(sems[t2], rank[t2][i2] * (16 if t2.startswith("dma") else 1))
                    ins = fn(e)
                    if track in rank and idx in rank[track]:
                        ins.then_inc(sems[track], 16 if track.startswith("dma") else 1)
                if ename == "sp":
                    for t2, i2 in fin.items():
                        e.wait_ge(sems[t2], rank[t2][i2] * (16 if t2.startswith("dma") else 1))

            block.sync(lambda e: run("sp", e))
            block.scalar(lambda e: run("act", e))
            block.vector(lambda e: run("dve", e))
            block.gpsimd(lambda e: run("pool", e))
            block.tensor(lambda e: run("pe", e))


def bc_mid(ap, n):
    a = [list(x) for x in ap.ap]
    return AP(ap.tensor, ap.offset, [a[0], [0, n]] + a[1:])


def bc_part(ap, n=128):
    a = [list(x) for x in ap.ap]
    return AP(ap.tensor, ap.offset, [[0, n]] + a[1:])


A_QK, A_V, A_G, A_CKV, A_KPE = 0, 1024, 2048, 2056, 2312
A_COLS = 2440
B_O, B_GA, B_GB, B_CQ = 0, 1024, 2048, 3072
B_COLS = 3456


def build_program(debug=False):
    nc = bass.Bass("TRN2", target_bir_lowering=False)
    dt_in = lambda name, shape: nc.dram_tensor(name, shape, F32, kind="ExternalInput").ap()
    x_all = dt_in("x_all", [S, D])
    x_own = dt_in("x_own", [NOWN, D])
    c_t = dt_in("c_t", [128, 8])
    w_ada = dt_in("w_ada", [D, 6 * D])
    b_ada = dt_in("b_ada", [1, 6 * D])
    wA = dt_in("wA", [D, A_COLS])
    wB = dt_in("wB", [D, B_COLS])
    w_uq2 = dt_in("w_uq2", [384, 2048])
    w_ukv2 = dt_in("w_ukv2", [256, 2048])
    w_out = dt_in("w_out", [D, D])
    w_ff1b = dt_in("w_ff1b", [8, 128, 8 * 512])
    w_ff2b = dt_in("w_ff2b", [2, 128, 32 * 512])
    wBg_b = dt_in("wBg_b", [6, 128, 8 * 512])
    npre_mix_t = dt_in("npre_mix_t", [128, 8])
    npre_mlp_t = dt_in("npre_mlp_t", [128, 8])
    npost_mix = dt_in("npost_mix", [1, D])
    npost_mlp = dt_in("npost_mlp", [1, D])
    conv_w_t = dt_in("conv_w_t", [128, 8, 4])
    conv_b_t = dt_in("conv_b_t", [128, 8])
    gate_b4 = dt_in("gate_b4", [1, 32])
    head_norm = dt_in("head_norm", [1, D])
    qn_t = dt_in("qn_t", [128, 3])
    kvn_t = dt_in("kvn_t", [128, 2])
    rope_all = dt_in("rope_all", [128, S])
    rope_own = dt_in("rope_own", [128, NOWN])
    amask = dt_in("amask", [128, 2, 128])
    selm = dt_in("selm", [128, 2])
    out = nc.dram_tensor("out", [NOWN, D], F32, kind="ExternalOutput").ap()
    skind = "ExternalOutput" if debug else "Internal"
    KT = nc.dram_tensor("KT", [8, 128, S], BF16, kind=skind).ap()
    VT = nc.dram_tensor("VT", [32, 128, 8 * 128], BF16, kind=skind).ap()
    KPE = nc.dram_tensor("KPE", [64, S], BF16, kind=skind).ap()
    YA = nc.dram_tensor("YA", [16, 128, D], F32, kind=skind).ap()
    X1 = nc.dram_tensor("X1", [NOWN, D], F32, kind=skind).ap()
    if debug:
        DBG = nc.dram_tensor("DBG", [128, 6 * D], F32, kind="ExternalOutput").ap()
        DBG2 = nc.dram_tensor("DBG2", [16, 128, D], BF16, kind="ExternalOutput").ap()
        D_hT = nc.dram_tensor("D_hT", [128, 8, 512], BF16, kind="ExternalOutput").ap()
        D_qT = nc.dram_tensor("D_qT", [128, 4, 512], BF16, kind="ExternalOutput").ap()
        D_kT = nc.dram_tensor("D_kT", [128, 4, 512], BF16, kind="ExternalOutput").ap()
        D_g = nc.dram_tensor("D_g", [128, 4, 8], F32, kind="ExternalOutput").ap()
        D_ev = nc.dram_tensor("D_ev", [128, 3, 16], F32, kind="ExternalOutput").ap()
        D_ckvn = nc.dram_tensor("D_ckvn", [128, 2, 512], BF16, kind="ExternalOutput").ap()
        D_ktmp = nc.dram_tensor("D_ktmp", [128, 512], BF16, kind="ExternalOutput").ap()
        D_Vt = nc.dram_tensor("D_Vt", [128, 4, 4, 257], BF16, kind="ExternalOutput").ap()
        D_ps = nc.dram_tensor("D_ps", [128, 4, 257], F32, kind="ExternalOutput").ap()
        D_sc4 = nc.dram_tensor("D_sc4", [128, 8, 4], F32, kind="ExternalOutput").ap()
        D_kpst = nc.dram_tensor("D_kpst", [64, 512], BF16, kind="ExternalOutput").ap()

    P = Prog(nc)
    with contextlib.ExitStack() as top:
        def sbuf(st, name, shape, dt):
            return st.enter_context(nc.sbuf_tensor(name, shape, dt))

        psg = [top.enter_context(nc.psum_tensor("psg%d" % i, [128, 512], F32)) for i in range(4)]
        psg_r = [Res("psg%d" % i) for i in range(4)]
        psN = top.enter_context(nc.psum_tensor("psN", [128, 4, 512], F32))
        psN_r = [Res("psN%d" % i) for i in range(4)]
        pctr = [0]

        def bank():
            i = pctr[0] % 4
            pctr[0] += 1
            return psg[i], psg_r[i]

        ident_bf = sbuf(top, "ident_bf", [128, 128], BF16)
        ident_f = sbuf(top, "ident_f", [128, 128], F32)
        ones_f = sbuf(top, "ones_f", [128, 128], F32)
        ones_bf = sbuf(top, "ones_bf", [128, 128], BF16)
        tri_f = sbuf(top, "tri_f", [128, 128], F32)
        maskS = sbuf(top, "maskS", [128, 128], F32)
        sel = sbuf(top, "sel", [128, 64], BF16)
        sel2 = sbuf(top, "sel2", [128, 64], BF16)
        sel128 = sbuf(top, "sel128", [128, 128], BF16)
        amask_s = sbuf(top, "amask_s", [128, 2, 128], F32)
        amask_b = sbuf(top, "amask_b", [128, 2, 128], BF16)
        selm_s = sbuf(top, "selm_s", [128, 2], F32)
        r_const = Res("const", small=True)
        _cr = {}

        def pl(fn, name, last=False):
            r = _cr.setdefault(name, Res("c_" + name, small=True))
            P.op("pool", fn, reads=[r], writes=[r] + ([r_const] if last else []))

        def sel_eq(t, ncols, base):
            return lambda e: e.affine_select(out=t[:], in_=t[:], pattern=[[-1, ncols]], compare_op=ALU.is_equal, fill=0.0,
                                             base=base, channel_multiplier=1)

        def sel_ge(t):
            return lambda e: e.affine_select(out=t[:], in_=t[:], pattern=[[1, 128]], compare_op=ALU.is_ge, fill=0.0,
                                             base=0, channel_multiplier=-1)

        pl(lambda e: e.memset(ident_bf[:], 1.0), "ident_bf")
        pl(sel_eq(ident_bf, 128, 0), "ident_bf")
        pl(lambda e: e.memset(ident_f[:], 1.0), "ident_f")
        pl(sel_eq(ident_f, 128, 0), "ident_f")
        pl(lambda e: e.memset(ones_f[:], 1.0), "ones_f")
        pl(lambda e: e.memset(ones_bf[:], 1.0), "ones_bf")
        pl(lambda e: e.memset(tri_f[:], 1.0), "tri_f")
        pl(sel_ge(tri_f), "tri_f")
        pl(lambda e: e.memset(maskS[:], 128.0 ** -0.5), "maskS")
        pl(sel_ge(maskS), "maskS")
        pl(lambda e: e.memset(sel[:], 1.0), "sel")
        pl(sel_eq(sel, 64, 0), "sel")
        pl(lambda e: e.memset(sel2[:], 1.0), "sel2")
        pl(sel_eq(sel2, 64, -64), "sel2")
        P.op("pool", lambda e: e.tensor_tensor(out=sel[:], in0=sel[:], in1=sel2[:], op=ALU.add), reads=[_cr["sel"], _cr["sel2"]], writes=[_cr["sel"]])
        P.op("pool", lambda e: e.tensor_copy(out=sel128[:, 0:64], in_=sel[:]), reads=[_cr["sel"]], writes=[r_const])
        P.op("pool", lambda e: e.tensor_copy(out=sel128[:, 64:128], in_=sel[:]), reads=[_cr["sel"]] + list(_cr.values()), writes=[r_const])
        r_am = Res("amask", small=True)
        r_cd = [Res("cd%d" % i) for i in range(5)]
        P.dma("sp", "const1", lambda e: e.dma_start(out=amask_s[:], in_=amask[:, :, :]), writes=[r_cd[0]])
        P.dma("sp", "const2", lambda e: e.dma_start(out=selm_s[:], in_=selm[:, :]), writes=[r_cd[1]])

        npre_mix_s = sbuf(top, "npre_mix_s", [128, 8], F32)
        npre_mlp_s = sbuf(top, "npre_mlp_s", [128, 8], F32)
        cw_s = sbuf(top, "cw_s", [128, 8, 4], F32)
        cb_s = sbuf(top, "cb_s", [128, 8], F32)
        gb_s = sbuf(top, "gb_s", [128, 32], F32)
        qn_s = sbuf(top, "qn_s", [128, 3], F32)
        kvn_s = sbuf(top, "kvn_s", [128, 2], F32)
        r_small = Res("small", small=True)
        for dst, src in ((npre_mix_s, npre_mix_t), (npre_mlp_s, npre_mlp_t), (cb_s, conv_b_t),
                         (qn_s, qn_t), (kvn_s, kvn_t)):
            P.dma("sp", "const3", (lambda dst, src: lambda e: e.dma_start(out=dst[:], in_=src[:, :]))(dst, src),
                  writes=[r_cd[2]])
        P.dma("sp", "const4", lambda e: e.dma_start(out=cw_s[:], in_=conv_w_t[:, :, :]), writes=[r_cd[3]])
        P.dma("sp", "const5", lambda e: e.dma_start(out=gb_s[:], in_=bc_part(gate_b4[0:1, :])), writes=[r_cd[4]])
        P.op("dve", lambda e: e.tensor_copy(out=amask_b[:], in_=amask_s[:]), reads=r_cd, writes=[r_am, r_small])

        GM = sbuf(top, "GM", [128, D], F32)
        GF = sbuf(top, "GF", [128, D], F32)
        AM = sbuf(top, "AM", [128, 8], F32)
        BM = sbuf(top, "BM", [128, 8], F32)
        AFm = sbuf(top, "AFm", [128, 8], F32)
        BFm = sbuf(top, "BFm", [128, 8], F32)
        r_mod = Res("mod", small=True)

        def _phase0():
            with contextlib.ExitStack() as st:
                MODB = sbuf(st, "MODB", [128, 6 * D], F32)
                badaB = sbuf(st, "badaB", [128, 6 * D], F32)
                wa = [sbuf(st, "wa%d" % i, [128, 8, 512], BF16) for i in range(2)]
                wa_r = [Res("wa0"), Res("wa1")]
                cs = sbuf(st, "cs", [128, 8], F32)
                scs = sbuf(st, "scs", [128, 8], F32)
                screp = sbuf(st, "screp", [128, 8, 128], BF16)
                npmB = sbuf(st, "npmB", [128, D], F32)
                npfB = sbuf(st, "npfB", [128, D], F32)
                dtmp = sbuf(st, "dtmp", [128, 16, 128], F32)
                dg = sbuf(st, "dg", [128, 32], F32)
                r_cs, r_screp, r_bada, r_modb, r_np = Res(small=True), Res(), Res(), Res(small=True), Res()
                P.dma("sp", "p0a", lambda e: e.dma_start(out=cs[:], in_=c_t[:, :]), writes=[r_cs])
                P.dma("sp", "p0b", lambda e: e.dma_start(out=badaB[:], in_=bc_part(b_ada[0:1, :])), writes=[r_bada])
                P.dma("sp", "p0c", lambda e: e.dma_start(out=npmB[:], in_=bc_part(npost_mix[0:1, :])), writes=[r_np])
                P.dma("sp", "p0c", lambda e: e.dma_start(out=npfB[:], in_=bc_part(npost_mlp[0:1, :])), writes=[r_np])
                P.op("act", lambda e: e.activation(out=scs[:], in_=cs[:], func=AF.Silu), reads=[r_cs], writes=[r_cs])
                for k in range(8):
                    P.op("dve", (lambda k: lambda e: e.tensor_scalar(out=screp[:, k, :], in0=ones_f[:], scalar1=scs[:, k:k + 1],
                                                                      scalar2=None, op0=ALU.mult))(k),
                         reads=[r_cs, r_const], writes=[r_screp])
                wadv = w_ada.rearrange("(k p) n -> p k n", p=128)
                for ng in range(12):
                    sl = ng % 2
                    P.dma("pool", "wa%d" % sl, (lambda ng, sl: lambda e: e.dma_start(out=wa[sl][:], in_=wadv[:, :, ng * 512:(ng + 1) * 512]))(ng, sl),
                          writes=[wa_r[sl]])
                    ps, pr = bank()
                    for k in range(8):
                        P.op("pe", (lambda ps, k, sl: lambda e: e.matmul(ps[:], lhsT=screp[:, k, :], rhs=wa[sl][:, k, :],
                                                                         start=(k == 0), stop=(k == 7)))(ps, k, sl),
                             reads=[r_screp, wa_r[sl]], writes=[pr])
                    P.op("dve", (lambda ps, ng: lambda e: e.tensor_tensor(out=MODB[:, ng * 512:(ng + 1) * 512], in0=ps[:],
                                                                          in1=badaB[:, ng * 512:(ng + 1) * 512], op=ALU.add))(ps, ng),
                         reads=[pr, r_bada], writes=[r_modb])
                if debug:
                    P.dma("sp", "dbg", lambda e: e.dma_start(out=DBG[:, :], in_=MODB[:]), reads=[r_modb])
                P.op("dve", lambda e: e.tensor_tensor(out=GM[:], in0=MODB[:, 2 * D:3 * D], in1=npmB[:], op=ALU.mult),
                     reads=[r_modb, r_np], writes=[r_mod])
                P.op("dve", lambda e: e.tensor_tensor(out=GF[:], in0=MODB[:, 5 * D:6 * D], in1=npfB[:], op=ALU.mult),
                     reads=[r_modb, r_np], writes=[r_mod])
                for half, off in ((0, 0), (1, 3 * D)):
                    P.op("dve", (lambda off: lambda e: e.tensor_tensor(
                        out=dtmp[:], in0=MODB[:, off:off + 2 * D].rearrange("p (a b) -> p a b", b=128),
                        in1=bc_mid(ident_f[:], 16), op=ALU.mult))(off), reads=[r_modb, r_const], writes=[r_modb])
                    P.op("dve", (lambda half: lambda e: e.tensor_reduce(out=dg[:, half * 16:(half + 1) * 16], in_=dtmp[:],
                                                                         axis=AX.X, op=ALU.add))(half),
                         reads=[r_modb], writes=[r_modb])
                P.op("dve", lambda e: e.scalar_tensor_tensor(out=AM[:], in0=dg[:, 8:16], scalar=1.0, in1=npre_mix_s[:],
                                                             op0=ALU.add, op1=ALU.mult), reads=[r_modb, r_small], writes=[r_mod])
                P.op("dve", lambda e: e.tensor_copy(out=BM[:], in_=dg[:, 0:8]), reads=[r_modb], writes=[r_mod])
                P.op("dve", lambda e: e.scalar_tensor_tensor(out=AFm[:], in0=dg[:, 24:32], scalar=1.0, in1=npre_mlp_s[:],
                                                             op0=ALU.add, op1=ALU.mult), reads=[r_modb, r_small], writes=[r_mod])
                P.op("dve", lambda e: e.tensor_copy(out=BFm[:], in_=dg[:, 16:24]), reads=[r_modb], writes=[r_mod])

        _phase0()
        P.fence_all()
        def norm_transpose(src_ap_blocks, nb, xt, xt_r, xn, xn_r, hT, hT_r, Asc, Bsc, junk, junk_r, stat, stat_r, dkey, dma_reads=(), part="all"):
            if part == "dma":
                P.dma("sp", dkey, lambda e: e.dma_start(out=xt[:, 0:nb, :], in_=src_ap_blocks), reads=list(dma_reads), writes=[xt_r])
                return
            if part in ("all", "pre", "pre_nodma"):
                _nt_pre(src_ap_blocks, nb, xt, xt_r, xn, xn_r, junk, junk_r, stat, stat_r, dkey, dma_reads, do_dma=(part != "pre_nodma"))
            if part in ("all", "tr"):
                _nt_tr(nb, xn, xn_r, hT, hT_r, Asc, Bsc)

        def _nt_pre(src_ap_blocks, nb, xt, xt_r, xn, xn_r, junk, junk_r, stat, stat_r, dkey, dma_reads, do_dma=True):
            if do_dma:
                P.dma("sp", dkey, lambda e: e.dma_start(out=xt[:, 0:nb, :], in_=src_ap_blocks), reads=list(dma_reads), writes=[xt_r])
            P.op("dve", lambda e: e.memset(stat[:, 0:4], 0.0), writes=[stat_r])
            for b in range(nb):
                P.op("act", (lambda b: lambda e: e.activation(out=junk[:], in_=xt[:, b, :], func=AF.Square,
                                                              accum_out=stat[:, b:b + 1]))(b),
                     reads=[xt_r], writes=[junk_r, stat_r])
            P.op("act", lambda e: e.activation(out=stat[:, 4:4 + nb], in_=stat[:, 0:nb], func=AF.Sqrt, bias=EPS, scale=1.0 / D),
                 reads=[stat_r], writes=[stat_r])
            P.op("dve", lambda e: e.reciprocal(out=stat[:, 8:8 + nb], in_=stat[:, 4:4 + nb]),
                 reads=[stat_r], writes=[stat_r])
            for b in range(nb):
                P.op("dve", (lambda b: lambda e: e.tensor_scalar(out=xn[:, b, :], in0=xt[:, b, :], scalar1=stat[:, 8 + b:9 + b],
                                                                 scalar2=None, op0=ALU.mult))(b),
                     reads=[xt_r, stat_r], writes=[xn_r])

        def _nt_tr(nb, xn, xn_r, hT, hT_r, Asc, Bsc):
            for k in range(8):
                ps, pr = bank()
                for b in range(nb):
                    P.op("pe", (lambda ps, b, k: lambda e: e.matmul(ps[:, b * 128:(b + 1) * 128], lhsT=xn[:, b, k * 128:(k + 1) * 128],
                                                                    rhs=ident_bf[:], start=True, stop=True))(ps, b, k),
                         reads=[xn_r, r_const], writes=[pr])
                P.op("act", (lambda ps, k: lambda e: e.activation(out=hT[:, k, 0:nb * 128], in_=ps[:, 0:nb * 128], func=AF.Identity,
                                                                  bias=Bsc[:, k:k + 1], scale=Asc[:, k:k + 1]))(ps, k),
                     reads=[pr, r_mod], writes=[hT_r])

        def rms_feature_major(ps_list, pr_list, nj, wsc, outT, out_r, sq, sq_r, rstd, rstd_r, n_feat):
            for j in range(nj):
                P.op("act", (lambda j: lambda e: e.activation(out=sq[:, j, :], in_=ps_list[j][:], func=AF.Square))(j),
                     reads=[pr_list[j]], writes=[sq_r])
            ps, pr = bank()
            for j in range(nj):
                P.op("pe", (lambda ps, j: lambda e: e.matmul(ps[:], lhsT=ones_bf[:], rhs=sq[:, j, :], start=(j == 0), stop=(j == nj - 1)))(ps, j),
                     reads=[sq_r, r_const], writes=[pr])
            P.op("act", (lambda ps: lambda e: e.activation(out=rstd[:], in_=ps[:], func=AF.Sqrt, bias=EPS, scale=1.0 / n_feat))(ps),
                 reads=[pr], writes=[rstd_r])
            P.op("dve", lambda e: e.reciprocal(out=rstd[:], in_=rstd[:]), reads=[rstd_r], writes=[rstd_r])
            for j in range(nj):
                P.op("dve", (lambda j: lambda e: e.scalar_tensor_tensor(out=outT[:, j, :], in0=ps_list[j][:], scalar=wsc[:, j:j + 1],
                                                                        in1=rstd[:], op0=ALU.mult, op1=ALU.mult))(j),
                     reads=[pr_list[j], rstd_r, r_small], writes=[out_r])

        def _phase1():
            with contextlib.ExitStack() as st:
                WA = sbuf(st, "WA", [128, 8, A_COLS], BF16)
                WUKV = sbuf(st, "WUKV", [128, 2, 2048], BF16)
                r_WA, r_WUKV = Res("WA"), Res("WUKV")
                wAv = wA.rearrange("(k p) n -> p k n", p=128)
                for k in range(8):
                    P.dma("pool", "WA", (lambda k: lambda e: e.dma_start(out=WA[:, k, :], in_=wAv[:, k, :]))(k), writes=[r_WA])
                P.dma("pool", "WUKV", lambda e: e.dma_start(out=WUKV[:], in_=w_ukv2.rearrange("(k p) n -> p k n", p=128)), writes=[r_WUKV])
                xt = [sbuf(st, "xt%d" % i, [128, 4, D], F32) for i in range(2)]
                xt_r = [Res(), Res()]
                xn = sbuf(st, "xn", [128, 4, D], BF16); xn_r = Res()
                hT = sbuf(st, "hT", [128, 8, 512], BF16); hT_r = Res()
                stat = sbuf(st, "stat", [128, 12], F32); stat_r = Res(small=True)
                xqk = sbuf(st, "xqk", [128, 2, 515], F32); xqk_r = [Res(), Res()]
                hal = sbuf(st, "hal", [128, 8, 3], F32); hal_r = Res(small=True)
                junkM = sbuf(st, "junkM", [128, 256], BF16); junkM_r = Res()
                acc = sbuf(st, "acc", [128, 512], F32); acc_r = Res()
                qT2 = [sbuf(st, "qT%d" % i, [128, 4, 512], BF16) for i in range(2)]; qT2_r = [Res(), Res()]
                kT2 = [sbuf(st, "kT%d" % i, [128, 4, 512], BF16) for i in range(2)]; kT2_r = [Res(), Res()]
                Ktok2 = [sbuf(st, "Ktok%d" % i, [128, 4, 512], BF16) for i in range(2)]; Ktok2_r = [Res(), Res()]
                g = sbuf(st, "g", [128, 4, 8], F32); g_r = Res(small=True)
                spv = sbuf(st, "spv", [128, 4, 4], F32); sp_r = Res(small=True)
                wv = sbuf(st, "wv", [128, 4, 4], F32); wv_r = Res(small=True)
                ez = sbuf(st, "ez", [128, 4, 4], F32); ez_r = Res(small=True)
                ev2 = [sbuf(st, "ev%d" % i, [128, 4, 4], F32) for i in range(2)]
                eb2 = [sbuf(st, "eb%d" % i, [128, 4, 4], F32) for i in range(2)]
                eL2 = [sbuf(st, "eL%d" % i, [128, 4, 4], F32) for i in range(2)]
                ebi2 = [sbuf(st, "ebi%d" % i, [128, 4, 4], F32) for i in range(2)]
                gate2_r = [Res(small=True), Res(small=True)]
                Vt2 = [sbuf(st, "Vt%d" % i, [128, 4, 4, 257], BF16) for i in range(2)]; Vt2_r = [Res(), Res()]
                sq = sbuf(st, "sq", [128, 2, 512], BF16); sq_r = Res()
                rstd = sbuf(st, "rstd", [128, 512], F32); rstd_r = Res()
                ckvn = sbuf(st, "ckvn", [128, 2, 512], BF16); ckvn_r = Res()
                kst = sbuf(st, "kst", [128, 8, 512], BF16); kst_r = Res()
                vst = sbuf(st, "vst", [128, 4, 8, 128], BF16); vst_r = Res()
                ropeA2 = [sbuf(st, "ropeA%d" % i, [128, 512], F32) for i in range(2)]; ropeA2_r = [Res(), Res()]
                ktmp = sbuf(st, "ktmp", [128, 512], BF16); ktmp_r = Res()
                kpst = sbuf(st, "kpst", [64, 512], BF16); kpst_r = Res()
                PT = sbuf(st, "PT", [128, 4, 128], BF16); PT_r = Res()
                C32 = sbuf(st, "C32", [128, 4, 257], F32); Cbf = sbuf(st, "Cbf", [128, 4, 257], BF16)
                C32_r = Res(small=True); Cbf_r = Res(small=True)
                t1 = sbuf(st, "t1", [128, 257], F32); t1_r = Res(small=True)
                Nsb = [sbuf(st, "Nsb%d" % i, [128, 4, 257], F32) for i in range(2)]; Nsb_r = [Res(small=True), Res(small=True)]
                scs_ = [sbuf(st, "scs_%d" % i, [128, 8, 4], F32) for i in range(2)]; scs_r = [Res(small=True), Res(small=True)]
                yraw = [sbuf(st, "yraw%d" % i, [128, D], F32) for i in range(2)]; yraw_r = [Res(), Res()]
                hnB = None
                P.op("pool", lambda e: e.memset(hal[:], 0.0), writes=[hal_r])
                P.op("pool", lambda e: e.memset(vst[:], 1.0), writes=[vst_r])
                P.op("pool", lambda e: e.memset(C32[:], 0.0), writes=[C32_r])
                P.op("pool", lambda e: e.memset(Cbf[:], 0.0), writes=[Cbf_r])
                xav = x_all.rearrange("(t b p) d -> t p b d", b=4, p=128)
                def make_tile(ti):
                    sl = ti % 2
                    c0 = ti * 512
                    s_ = ti % 2
                    qT, qT_r = qT2[s_], qT2_r[s_]
                    kT, kT_r = kT2[s_], kT2_r[s_]
                    Ktok, Ktok_r = Ktok2[s_], Ktok2_r[s_]
                    Vt, Vt_r = Vt2[s_], Vt2_r[s_]
                    ev, eb, eL, ebi = ev2[s_], eb2[s_], eL2[s_], ebi2[s_]
                    gate_r = gate2_r[s_]
                    ropeA, ropeA_r = ropeA2[s_], ropeA2_r[s_]
                    def sec_loads():
                        P.dma("sp", "ropeA%d" % s_, (lambda c0, ropeA: lambda e: e.dma_start(out=ropeA[:], in_=rope_all[:, c0:c0 + 512]))(c0, ropeA), writes=[ropeA_r])
                        norm_transpose(xav[ti], 4, xt[sl], xt_r[sl], xn, xn_r, hT, hT_r, AM, BM, xn[:, 3, :], xn_r, stat, stat_r, "xt%d" % sl, part="dma")
                    def sec_nt_pre():
                        norm_transpose(xav[ti], 4, xt[sl], xt_r[sl], xn, xn_r, hT, hT_r, AM, BM, xn[:, 3, :], xn_r, stat, stat_r, "xt%d" % sl, part="pre_nodma")
                    def sec_nt_tr():
                        norm_transpose(xav[ti], 4, xt[sl], xt_r[sl], xn, xn_r, hT, hT_r, AM, BM, xn[:, 3, :], xn_r, stat, stat_r, "xt%d" % sl, part="tr")
                    def sec_gates():
                        ps, pr = bank()
                        for b in range(4):
                            for k in range(8):
                                P.op("pe", (lambda ps, b, k: lambda e: e.matmul(ps[:, b * 8:(b + 1) * 8], lhsT=hT[:, k, b * 128:(b + 1) * 128],
                                                                                rhs=WA[:, k, A_G:A_G + 8], start=(k == 0), stop=(k == 7)))(ps, b, k),
                                     reads=[hT_r, r_WA], writes=[pr])
                        P.op("dve", (lambda ps: lambda e: e.tensor_tensor(out=g[:], in0=ps[:, 0:32].rearrange("p (b c) -> p b c", c=8),
                                                                          in1=gb_s[:].rearrange("p (b c) -> p b c", c=8), op=ALU.add))(ps),
                             reads=[pr, r_small], writes=[g_r])
                        P.op("act", lambda e: e.activation(out=spv[:], in_=g[:, :, 4:8], func=AF.Exp, scale=-1.0), reads=[g_r], writes=[sp_r])
                        P.op("dve", lambda e: e.tensor_scalar(out=wv[:], in0=spv[:], scalar1=1.0, scalar2=None, op0=ALU.add), reads=[sp_r], writes=[wv_r])
                        P.op("act", lambda e: e.activation(out=spv[:], in_=wv[:], func=AF.Ln), reads=[wv_r], writes=[sp_r])
                        for _it in range(2):
                            P.op("act", lambda e: e.activation(out=ez[:], in_=spv[:], func=AF.Exp, scale=-1.0), reads=[sp_r], writes=[ez_r])
                            P.op("dve", lambda e: e.tensor_tensor(out=ez[:], in0=ez[:], in1=wv[:], op=ALU.mult), reads=[ez_r, wv_r], writes=[ez_r])
                            P.op("dve", lambda e: e.scalar_tensor_tensor(out=spv[:], in0=spv[:], scalar=-1.0, in1=ez[:], op0=ALU.add, op1=ALU.add),
                                 reads=[sp_r, ez_r], writes=[sp_r])
                        psb, prb = bank()
                        P.op("pe", (lambda psb: lambda e: e.matmul(psb[:, 0:16], lhsT=tri_f[:], rhs=spv[:].rearrange("p b c -> p (b c)"),
                                                                   start=True, stop=True))(psb), reads=[sp_r, r_const], writes=[prb])
                        P.op("pe", (lambda psb: lambda e: e.matmul(psb[:, 16:32], lhsT=ones_f[:], rhs=spv[:].rearrange("p b c -> p (b c)"),
                                                                   start=True, stop=True))(psb), reads=[sp_r, r_const], writes=[prb])
                        P.op("dve", (lambda psb: lambda e: e.tensor_tensor(out=ev[:], in0=g[:, :, 0:4],
                                                                           in1=psb[:, 0:16].rearrange("p (b c) -> p b c", c=4), op=ALU.add))(psb),
                             reads=[prb, g_r], writes=[gate_r])
                        P.op("act", lambda e: e.activation(out=ev[:], in_=ev[:], func=AF.Exp), reads=[gate_r], writes=[gate_r])
                        P.op("act", (lambda psb: lambda e: e.activation(out=eb[:], in_=psb[:, 0:16].rearrange("p (b c) -> p b c", c=4),
                                                                        func=AF.Exp, scale=-1.0))(psb), reads=[prb], writes=[gate_r])
                        P.op("act", (lambda psb: lambda e: e.activation(out=ebi[:], in_=psb[:, 0:16].rearrange("p (b c) -> p b c", c=4),
                                                                        func=AF.Exp))(psb), reads=[prb], writes=[gate_r])
                        P.op("act", (lambda psb: lambda e: e.activation(out=eL[:], in_=psb[:, 16:32].rearrange("p (b c) -> p b c", c=4),
                                                                        func=AF.Exp, scale=-1.0))(psb), reads=[prb], writes=[gate_r])
                    def sec_qk(nsel):
                        for n in nsel:
                            ps, pr = bank()
                            for k in range(8):
                                P.op("pe", (lambda ps, n, k: lambda e: e.matmul(ps[:], lhsT=WA[:, k, A_QK + n * 128:A_QK + (n + 1) * 128],
                                                                                rhs=hT[:, k, :], start=(k == 0), stop=(k == 7)))(ps, n, k),
                                     reads=[hT_r, r_WA], writes=[pr])
                            P.op("dve", (lambda n: lambda e: e.tensor_copy(out=xqk[:, n % 2, 0:3], in_=hal[:, n, :]))(n),
                                 reads=[hal_r], writes=[xqk_r[n % 2]])
                            P.op("act", (lambda ps, n: lambda e: e.activation(out=xqk[:, n % 2, 3:515], in_=ps[:], func=AF.Copy))(ps, n),
                                 reads=[pr], writes=[xqk_r[n % 2]])
                            P.op("dve", (lambda n: lambda e: e.tensor_scalar(out=acc[:], in0=xqk[:, n % 2, 3:515], scalar1=cw_s[:, n, 3:4],
                                                                             scalar2=None, op0=ALU.mult))(n),
                                 reads=[xqk_r[n % 2], r_small], writes=[acc_r])
                            for jj in range(3):
                                P.op("dve", (lambda n, jj: lambda e: e.scalar_tensor_tensor(out=acc[:], in0=xqk[:, n % 2, jj:jj + 512],
                                                                                             scalar=cw_s[:, n, jj:jj + 1], in1=acc[:],
                                                                                             op0=ALU.mult, op1=ALU.add))(n, jj),
                                     reads=[xqk_r[n % 2], acc_r], writes=[acc_r])
                            P.op("dve", (lambda n: lambda e: e.tensor_copy(out=hal[:, n, :], in_=xqk[:, n % 2, 512:515]))(n),
                                 reads=[xqk_r[n % 2]], writes=[hal_r])
                            if n < 4:
                                P.op("act", (lambda n: lambda e: e.activation(out=qT[:, n, :], in_=acc[:], func=AF.Silu, bias=cb_s[:, n:n + 1]))(n),
                                     reads=[acc_r, r_small], writes=[qT_r])
                            else:
                                P.op("act", (lambda n: lambda e: e.activation(out=kT[:, n - 4, :], in_=acc[:], func=AF.Silu, bias=cb_s[:, n:n + 1]))(n),
                                     reads=[acc_r, r_small], writes=[kT_r])
                    def sec_ktr(bsel):
                        for b in bsel:
                            ps, pr = bank()
                            for h in range(4):
                                P.op("pe", (lambda ps, b, h: lambda e: e.matmul(ps[:, h * 128:(h + 1) * 128], lhsT=kT[:, h, b * 128:(b + 1) * 128],
                                                                                rhs=ident_bf[:], start=True, stop=True))(ps, b, h),
                                     reads=[kT_r, r_const], writes=[pr])
                            P.op("act", (lambda ps, b: lambda e: e.activation(out=Ktok[:, b, :], in_=ps[:], func=AF.Copy))(ps, b),
                                 reads=[pr], writes=[Ktok_r])
                    def sec_v(bsel):
                        for b in bsel:
                            for half in range(2):
                                ps, pr = bank()
                                for k in range(8):
                                    P.op("pe", (lambda ps, b, half, k: lambda e: e.matmul(ps[:], lhsT=hT[:, k, b * 128:(b + 1) * 128],
                                                                                          rhs=WA[:, k, A_V + half * 512:A_V + (half + 1) * 512],
                                                                                          start=(k == 0), stop=(k == 7)))(ps, b, half, k),
                                         reads=[hT_r, r_WA], writes=[pr])
                                for hh in range(2):
                                    h = half * 2 + hh
                                    eng = "act" if hh == 0 else "dve"
                                    if eng == "act":
                                        P.op("act", (lambda ps, b, h, hh: lambda e: e.activation(out=Vt[:, b, h, 0:256], in_=ps[:, hh * 256:(hh + 1) * 256],
                                                                                                 func=AF.Copy, scale=ev[:, b, h:h + 1]))(ps, b, h, hh),
                                             reads=[pr, gate_r], writes=[Vt_r])
                                    else:
                                        P.op("dve", (lambda ps, b, h, hh: lambda e: e.tensor_scalar(out=Vt[:, b, h, 0:256], in0=ps[:, hh * 256:(hh + 1) * 256],
                                                                                                    scalar1=ev[:, b, h:h + 1], scalar2=None, op0=ALU.mult))(ps, b, h, hh),
                                             reads=[pr, gate_r], writes=[Vt_r])
                        if 3 in bsel:
                            P.op("dve", lambda e: e.tensor_copy(out=Vt[:, :, :, 256], in_=ev[:]), reads=[gate_r], writes=[Vt_r])
                    def sec_ckv():
                        pss, prs = [], []
                        for j in range(2):
                            ps, pr = bank()
                            for k in range(8):
                                P.op("pe", (lambda ps, j, k: lambda e: e.matmul(ps[:], lhsT=WA[:, k, A_CKV + j * 128:A_CKV + (j + 1) * 128],
                                                                                rhs=hT[:, k, :], start=(k == 0), stop=(k == 7)))(ps, j, k),
                                     reads=[hT_r, r_WA], writes=[pr])
                            pss.append(ps); prs.append(pr)
                        rms_feature_major(pss, prs, 2, kvn_s, ckvn, ckvn_r, sq, sq_r, rstd, rstd_r, 256)
                    def sec_kup(hsel):
                        for h in hsel:
                            ps, pr = bank()
                            for j in range(2):
                                P.op("pe", (lambda ps, h, j: lambda e: e.matmul(ps[:], lhsT=WUKV[:, j, h * 128:(h + 1) * 128], rhs=ckvn[:, j, :],
                                                                                start=(j == 0), stop=(j == 1)))(ps, h, j),
                                     reads=[ckvn_r, r_WUKV], writes=[pr])
                            if h % 2 == 0:
                                P.op("act", (lambda ps, h: lambda e: e.activation(out=kst[:, h, :], in_=ps[:], func=AF.Copy))(ps, h), reads=[pr], writes=[kst_r])
                            else:
                                P.op("dve", (lambda ps, h: lambda e: e.tensor_copy(out=kst[:, h, :], in_=ps[:]))(ps, h), reads=[pr], writes=[kst_r])
                        if 7 in hsel:
                            P.dma("sp", "kst", (lambda c0: lambda e: e.dma_start(out=KT[:, :, c0:c0 + 512].rearrange("h d t -> d h t"), in_=kst[:]))(c0),
                                  reads=[kst_r])
                    def sec_vup(bsel):
                        for b in bsel:
                            for half in range(2):
                                ps, pr = bank()
                                for j in range(2):
                                    P.op("pe", (lambda ps, b, half, j: lambda e: e.matmul(ps[:], lhsT=ckvn[:, j, b * 128:(b + 1) * 128],
                                                                                          rhs=WUKV[:, j, 1024 + half * 512:1024 + (half + 1) * 512],
                                                                                          start=(j == 0), stop=(j == 1)))(ps, b, half, j),
                                         reads=[ckvn_r, r_WUKV], writes=[pr])
                                if half == 0:
                                    P.op("act", (lambda ps, b, half: lambda e: e.activation(out=vst[:, b, 4 * half:4 * half + 4, 0:128],
                                                                                            in_=ps[:].rearrange("p (h v) -> p h v", v=128), func=AF.Copy))(ps, b, half),
                                         reads=[pr], writes=[vst_r])
                                else:
                                    P.op("dve", (lambda ps, b, half: lambda e: e.tensor_copy(out=vst[:, b, 4 * half:4 * half + 4, 0:128],
                                                                                             in_=ps[:].rearrange("p (h v) -> p h v", v=128)))(ps, b, half),
                                         reads=[pr], writes=[vst_r])
                        if 3 in bsel:
                            P.dma("sp", "vst", (lambda ti: lambda e: e.dma_start(out=VT[ti * 4:ti * 4 + 4, :, :].rearrange("b p f -> p b f"),
                                                                                   in_=vst[:].rearrange("p b h f -> p b (h f)")))(ti), reads=[vst_r])
                    def sec_kpe():
                        ps, pr = bank()
                        for k in range(8):
                            P.op("pe", (lambda ps, k: lambda e: e.matmul(ps[:], lhsT=WA[:, k, A_KPE:A_KPE + 128], rhs=hT[:, k, :],
                                                                         start=(k == 0), stop=(k == 7)))(ps, k), reads=[hT_r, r_WA], writes=[pr])
                        P.op("dve", (lambda ps: lambda e: e.tensor_tensor(out=ktmp[:], in0=ps[:], in1=ropeA[:], op=ALU.mult))(ps),
                             reads=[pr, ropeA_r], writes=[ktmp_r])
                        ps, pr = bank()
                        P.op("pe", (lambda ps: lambda e: e.matmul(ps[0:64, :], lhsT=sel[:], rhs=ktmp[:], start=True, stop=True))(ps),
                             reads=[ktmp_r, r_const], writes=[pr])
                        P.op("act", (lambda ps: lambda e: e.activation(out=kpst[:], in_=ps[0:64, :], func=AF.Copy))(ps), reads=[pr], writes=[kpst_r])
                        P.dma("sp", "kpst", (lambda c0: lambda e: e.dma_start(out=KPE[:, c0:c0 + 512], in_=kpst[:]))(c0), reads=[kpst_r])
                    def sec_dbg():
                        if debug and ti == 0:
                            P.dma("sp", "d1", lambda e: e.dma_start(out=D_hT[:, :, :], in_=hT[:]), reads=[hT_r])
                            P.dma("sp", "d2", lambda e: e.dma_start(out=D_qT[:, :, :], in_=qT[:]), reads=[qT_r])
                            P.dma("sp", "d3", lambda e: e.dma_start(out=D_kT[:, :, :], in_=kT[:]), reads=[kT_r])
                            P.dma("sp", "d4", lambda e: e.dma_start(out=D_g[:, :, :], in_=g[:]), reads=[g_r])
                            P.dma("sp", "d5", lambda e: e.dma_start(out=D_ev[:, 0, :], in_=ev[:].rearrange("p b c -> p (b c)")), reads=[gate_r])
                            P.dma("sp", "d5", lambda e: e.dma_start(out=D_ev[:, 1, :], in_=eb[:].rearrange("p b c -> p (b c)")), reads=[gate_r])
                            P.dma("sp", "d5", lambda e: e.dma_start(out=D_ev[:, 2, :], in_=eL[:].rearrange("p b c -> p (b c)")), reads=[gate_r])
                            P.dma("sp", "d6", lambda e: e.dma_start(out=D_ckvn[:, :, :], in_=ckvn[:]), reads=[ckvn_r])
                            P.dma("sp", "d7", lambda e: e.dma_start(out=D_ktmp[:, :], in_=ktmp[:]), reads=[ktmp_r])
                            P.dma("sp", "d8", lambda e: e.dma_start(out=D_Vt[:, :, :, :], in_=Vt[:]), reads=[Vt_r])
                    pscs = {}
                    def mc1(b):
                        gb = ti * 4 + b
                        par = gb % 2
                        lb = gb // 2
                        yr, yr_r = yraw[lb % 2], yraw_r[lb % 2]
                        bs = slice(b * 128, (b + 1) * 128)
                        nsl = gb % 2
                        Nb, Nb_r, sc, sc_r = Nsb[nsl], Nsb_r[nsl], scs_[nsl], scs_r[nsl]
                        ps, pr = bank(); pscs[("S", b)] = (ps, pr)
                        for h in range(4):
                            P.op("pe", (lambda ps, h, bs: lambda e: e.matmul(ps[:, h * 128:(h + 1) * 128], lhsT=kT[:, h, bs], rhs=qT[:, h, bs],
                                                                             start=True, stop=True))(ps, h, bs), reads=[kT_r, qT_r], writes=[pr])
                    def mc2(b):
                        gb = ti * 4 + b
                        par = gb % 2
                        lb = gb // 2
                        yr, yr_r = yraw[lb % 2], yraw_r[lb % 2]
                        bs = slice(b * 128, (b + 1) * 128)
                        nsl = gb % 2
                        Nb, Nb_r, sc, sc_r = Nsb[nsl], Nsb_r[nsl], scs_[nsl], scs_r[nsl]
                        ps, pr = pscs[("S", b)]
                        P.op("dve", (lambda ps: lambda e: e.tensor_tensor(out=PT[:], in0=ps[:].rearrange("p (h t) -> p h t", t=128),
                                                                          in1=bc_mid(maskS[:], 4), op=ALU.mult))(ps),
                             reads=[pr, r_const], writes=[PT_r])
                    def mc3(b):
                        gb = ti * 4 + b
                        par = gb % 2
                        lb = gb // 2
                        yr, yr_r = yraw[lb % 2], yraw_r[lb % 2]
                        bs = slice(b * 128, (b + 1) * 128)
                        nsl = gb % 2
                        Nb, Nb_r, sc, sc_r = Nsb[nsl], Nsb_r[nsl], scs_[nsl], scs_r[nsl]
                        psc = []; pscs[("C", b)] = psc
                        for h in range(4):
                            P.op("pe", (lambda h, b: lambda e: e.matmul(psN[:, h, 0:257], lhsT=PT[:, h, :], rhs=Vt[:, b, h, :], start=True, stop=False))(h, b),
                                 reads=[PT_r, Vt_r], writes=[psN_r[h]])
                            P.op("pe", (lambda h, bs: lambda e: e.matmul(psN[:, h, 0:257], lhsT=qT[:, h, bs], rhs=Cbf[:, h, :], start=False, stop=True))(h, bs),
                                 reads=[qT_r, Cbf_r], writes=[psN_r[h]])
                        for h in range(4):
                            ps, pr = bank(); psc.append((ps, pr))
                            P.op("pe", (lambda ps, h, b: lambda e: e.matmul(ps[:, 0:257], lhsT=Ktok[:, b, h * 128:(h + 1) * 128], rhs=Vt[:, b, h, :],
                                                                            start=True, stop=True))(ps, h, b), reads=[Ktok_r, Vt_r], writes=[pr])
                    def mc4(b):
                        gb = ti * 4 + b
                        par = gb % 2
                        lb = gb // 2
                        yr, yr_r = yraw[lb % 2], yraw_r[lb % 2]
                        bs = slice(b * 128, (b + 1) * 128)
                        nsl = gb % 2
                        Nb, Nb_r, sc, sc_r = Nsb[nsl], Nsb_r[nsl], scs_[nsl], scs_r[nsl]
                        psc = pscs[("C", b)]
                        for h in range(4):
                            ps, pr = psc[h]
                            P.op("dve", (lambda ps, h, b: lambda e: e.tensor_scalar(out=t1[:], in0=ps[:, 0:257], scalar1=eL[:, b, h:h + 1], scalar2=None,
                                                                                    op0=ALU.mult))(ps, h, b), reads=[pr, gate_r], writes=[t1_r])
                            P.op("dve", (lambda h, b: lambda e: e.scalar_tensor_tensor(out=C32[:, h, :], in0=C32[:, h, :], scalar=eL[:, b, h:h + 1],
                                                                                       in1=t1[:], op0=ALU.mult, op1=ALU.add))(h, b),
                                 reads=[t1_r, gate_r, C32_r], writes=[C32_r])
                            P.op("act", (lambda h: lambda e: e.activation(out=Cbf[:, h, :], in_=C32[:, h, :], func=AF.Copy, scale=128.0 ** -0.5))(h),
                                 reads=[C32_r], writes=[Cbf_r])
                    def mc5(b):
                        gb = ti * 4 + b
                        par = gb % 2
                        lb = gb // 2
                        yr, yr_r = yraw[lb % 2], yraw_r[lb % 2]
                        bs = slice(b * 128, (b + 1) * 128)
                        nsl = gb % 2
                        Nb, Nb_r, sc, sc_r = Nsb[nsl], Nsb_r[nsl], scs_[nsl], scs_r[nsl]
                        for h in range(4):
                            P.op("act", (lambda h, Nb: lambda e: e.activation(out=Nb[:, h, :], in_=psN[:, h, 0:257], func=AF.Copy))(h, Nb),
                                 reads=[psN_r[h]], writes=[Nb_r])
                    def mo1(b):
                        gb = ti * 4 + b
                        par = gb % 2
                        lb = gb // 2
                        yr, yr_r = yraw[lb % 2], yraw_r[lb % 2]
                        bs = slice(b * 128, (b + 1) * 128)
                        nsl = gb % 2
                        Nb, Nb_r, sc, sc_r = Nsb[nsl], Nsb_r[nsl], scs_[nsl], scs_r[nsl]
                        P.op("dve", (lambda sc: lambda e: e.memset(sc[:, 0, :], 0.0))(sc), writes=[sc_r])
                        for h in range(4):
                            P.op("act", (lambda h, Nb, sc: lambda e: e.activation(out=junkM[:, 0:256], in_=Nb[:, h, 0:256], func=AF.Square,
                                                                                  accum_out=sc[:, 0, h:h + 1]))(h, Nb, sc), reads=[Nb_r], writes=[junkM_r, sc_r])
                    def mo2(b):
                        gb = ti * 4 + b
                        par = gb % 2
                        lb = gb // 2
                        yr, yr_r = yraw[lb % 2], yraw_r[lb % 2]
                        bs = slice(b * 128, (b + 1) * 128)
                        nsl = gb % 2
                        Nb, Nb_r, sc, sc_r = Nsb[nsl], Nsb_r[nsl], scs_[nsl], scs_r[nsl]
                        P.op("dve", (lambda Nb, sc: lambda e: e.scalar_tensor_tensor(out=sc[:, 1, :], in0=Nb[:, :, 256], scalar=-1.0, in1=Nb[:, :, 256],
                                                                                     op0=ALU.mult, op1=ALU.max))(Nb, sc), reads=[Nb_r], writes=[sc_r])
                        P.op("dve", (lambda b, sc: lambda e: e.tensor_tensor(out=sc[:, 1, :], in0=sc[:, 1, :], in1=ebi[:, b, :], op=ALU.max))(b, sc),
                             reads=[sc_r, gate_r], writes=[sc_r])
                        P.op("dve", (lambda sc: lambda e: e.tensor_tensor(out=sc[:, 2, :], in0=sc[:, 1, :], in1=sc[:, 1, :], op=ALU.mult))(sc),
                             reads=[sc_r], writes=[sc_r])
                        P.op("dve", (lambda sc: lambda e: e.scalar_tensor_tensor(out=sc[:, 3, :], in0=sc[:, 2, :], scalar=256.0 * EPS, in1=sc[:, 0, :],
                                                                                 op0=ALU.mult, op1=ALU.add))(sc), reads=[sc_r], writes=[sc_r])
                    def mo3(b):
                        gb = ti * 4 + b
                        par = gb % 2
                        lb = gb // 2
                        yr, yr_r = yraw[lb % 2], yraw_r[lb % 2]
                        bs = slice(b * 128, (b + 1) * 128)
                        nsl = gb % 2
                        Nb, Nb_r, sc, sc_r = Nsb[nsl], Nsb_r[nsl], scs_[nsl], scs_r[nsl]
                        P.op("act", (lambda sc: lambda e: e.activation(out=sc[:, 4, :], in_=sc[:, 3, :], func=AF.Sqrt, scale=1.0 / 256))(sc),
                             reads=[sc_r], writes=[sc_r])
                    def mo4(b):
                        gb = ti * 4 + b
                        par = gb % 2
                        lb = gb // 2
                        yr, yr_r = yraw[lb % 2], yraw_r[lb % 2]
                        bs = slice(b * 128, (b + 1) * 128)
                        nsl = gb % 2
                        Nb, Nb_r, sc, sc_r = Nsb[nsl], Nsb_r[nsl], scs_[nsl], scs_r[nsl]
                        P.op("dve", (lambda sc: lambda e: e.reciprocal(out=sc[:, 5, :], in_=sc[:, 4, :]))(sc), reads=[sc_r], writes=[sc_r])
                        P.op("dve", (lambda par, sc: lambda e: e.tensor_scalar(out=sc[:, 5, :], in0=sc[:, 5, :], scalar1=selm_s[:, par:par + 1],
                                                                               scalar2=None, op0=ALU.mult))(par, sc), reads=[sc_r, r_am], writes=[sc_r])
                    def mo5(b):
                        gb = ti * 4 + b
                        par = gb % 2
                        lb = gb // 2
                        yr, yr_r = yraw[lb % 2], yraw_r[lb % 2]
                        bs = slice(b * 128, (b + 1) * 128)
                        nsl = gb % 2
                        Nb, Nb_r, sc, sc_r = Nsb[nsl], Nsb_r[nsl], scs_[nsl], scs_r[nsl]
                        for h in range(4):
                            if par == 0:
                                P.op("act", (lambda h, yr, Nb, sc: lambda e: e.activation(out=yr[:, h * 256:(h + 1) * 256], in_=Nb[:, h, 0:256], func=AF.Copy,
                                                                                          scale=sc[:, 5, h:h + 1]))(h, yr, Nb, sc), reads=[Nb_r, sc_r], writes=[yr_r])
                            else:
                                P.op("dve", (lambda h, yr, Nb, sc: lambda e: e.scalar_tensor_tensor(out=yr[:, h * 256:(h + 1) * 256], in0=Nb[:, h, 0:256],
                                                                                                    scalar=sc[:, 5, h:h + 1], in1=yr[:, h * 256:(h + 1) * 256],
                                                                                                    op0=ALU.mult, op1=ALU.add))(h, yr, Nb, sc),
                                     reads=[Nb_r, sc_r, yr_r], writes=[yr_r])
                        if par == 1:
                            P.dma("sp", "yraw%d" % (lb % 2), (lambda lb, yr: lambda e: e.dma_start(out=YA[lb, :, :], in_=yr[:]))(lb, yr), reads=[yr_r])

                    xu = [sec_nt_pre, sec_nt_tr, sec_gates]
                    xu += [(lambda n: lambda: sec_qk([n]))(n) for n in range(8)]
                    xu += [(lambda b: lambda: sec_ktr([b]))(b) for b in range(4)]
                    xu += [(lambda b: lambda: sec_v([b]))(b) for b in range(4)]
                    xu += [sec_ckv]
                    xu += [(lambda h: lambda: sec_kup([h, h + 1]))(h) for h in range(0, 8, 2)]
                    xu += [(lambda b: lambda: sec_vup([b]))(b) for b in range(4)]
                    xu += [sec_kpe, sec_dbg]
                    mu = []
                    def mc12(b):
                        mc1(b); mc2(b)
                    def mc34(b):
                        mc3(b); mc4(b)
                    for b in range(4):
                        seq = [(mc12, b), (mo1, b - 1), (mc34, b), (mo2, b - 1), (mo3, b - 1), (mc5, b), (mo4, b - 1), (mo5, b - 1)]
                        for f, bb in seq:
                            if bb >= 0:
                                mu.append((lambda f, bb: lambda: f(bb))(f, bb))
                    for j in range(5):
                        mu.append((lambda f: lambda: f(3))([mo1, mo2, mo3, mo4, mo5][j]))
                    return xu, mu, sec_loads

                tiles = [make_tile(ti) for ti in range(8)]
                for ti in range(8):
                    if ti + 1 < 8:
                        tiles[ti][0].insert(2, tiles[ti + 1][2])
                tiles[0][2]()

                def run_merged(a, b):
                    ia = ib = 0
                    while ia < len(a) or ib < len(b):
                        fa = (ia + 1) / len(a) if ia < len(a) else 2.0
                        fb = (ib + 1) / len(b) if ib < len(b) else 2.0
                        if fa <= fb:
                            a[ia](); ia += 1
                        else:
                            b[ib](); ib += 1

                for u in tiles[0][0]:
                    u()
                for ti in range(8):
                    if ti + 1 < 8:
                        run_merged(tiles[ti][1], tiles[ti + 1][0])
                    else:
                        for u in tiles[ti][1]:
                            u()

        _phase1()
        P.fence_all()
        r_scr = Res("scratch")
        scr_deps = []
        for key in ("kst", "vst", "kpst", "yraw0", "yraw1"):
            rr = Res(key)
            rr.writer = ("dma:" + key, P.count["dma:" + key])
            scr_deps.append(rr)

        with contextlib.ExitStack() as st2:
            YB = sbuf(st2, "YB", [128, 16, D], BF16); YB_r = [Res() for _ in range(16)]
            with contextlib.ExitStack() as st:
                QT = sbuf(st, "QT", [128, 8, NOWN], BF16); QT_r = Res()
                QPE = sbuf(st, "QPE", [128, 4, NOWN], BF16); QPE_r = Res()
                def _phase2():
                    with contextlib.ExitStack() as sa:
                        WCQ = sbuf(sa, "WCQ", [128, 8, 384], BF16); r_WB = Res()
                        WUQ = sbuf(sa, "WUQ", [128, 3, 2048], BF16); r_WUQ = Res()
                        P.dma("pool", "WCQ", lambda e: e.dma_start(out=WCQ[:], in_=wB.rearrange("(k p) n -> p k n", p=128)[:, :, B_CQ:B_CQ + 384]), writes=[r_WB])
                        P.dma("pool", "WUQ", lambda e: e.dma_start(out=WUQ[:], in_=w_uq2.rearrange("(k p) n -> p k n", p=128)), writes=[r_WUQ])
                        xt = [sbuf(sa, "xo%d" % i, [128, 4, D], F32) for i in range(2)]; xt_r = [Res(), Res()]
                        xn = sbuf(sa, "xn2", [128, 4, D], BF16); xn_r = Res()
                        hT = sbuf(sa, "hT2", [128, 8, 512], BF16); hT_r = Res()
                        junk = sbuf(sa, "junk2", [128, D], BF16); junk_r = Res()
                        stat = sbuf(sa, "stat2", [128, 12], F32); stat_r = Res(small=True)
                        sq = sbuf(sa, "sq2", [128, 3, 512], BF16); sq_r = Res()
                        rstd = sbuf(sa, "rstd2", [128, 512], F32); rstd_r = Res()
                        cqn = sbuf(sa, "cqn", [128, 3, 512], BF16); cqn_r = Res()
                        ropeO = sbuf(sa, "ropeO", [128, 512], F32); ropeO_r = Res()
                        qtmp = sbuf(sa, "qtmp", [128, 512], BF16); qtmp_r = Res()
                        xov = x_own.rearrange("(t b p) d -> t p b d", b=4, p=128)
                        def p2a_nt(ti_, part):
                            norm_transpose(xov[ti_], 4, xt[ti_ % 2], xt_r[ti_ % 2], xn, xn_r, hT, hT_r, AM, BM, junk, junk_r, stat, stat_r, "xo%d" % (ti_ % 2), part=part)
                        p2a_nt(0, "all")
                        for ti in range(4):
                            sl = ti % 2
                            c0 = ti * 512
                            P.dma("sp", "ropeO", (lambda c0: lambda e: e.dma_start(out=ropeO[:], in_=rope_own[:, c0:c0 + 512]))(c0), writes=[ropeO_r])
                            pss, prs = [], []
                            for j in range(3):
                                ps, pr = bank()
                                for k in range(8):
                                    P.op("pe", (lambda ps, j, k: lambda e: e.matmul(ps[:], lhsT=WCQ[:, k, j * 128:(j + 1) * 128],
                                                                                    rhs=hT[:, k, :], start=(k == 0), stop=(k == 7)))(ps, j, k),
                                         reads=[hT_r, r_WB], writes=[pr])
                                pss.append(ps); prs.append(pr)
                            rms_feature_major(pss, prs, 3, qn_s, cqn, cqn_r, sq, sq_r, rstd, rstd_r, 384)
                            if ti + 1 < 4:
                                p2a_nt(ti + 1, "pre")
                            for h in range(8):
                                if h == 4 and ti + 1 < 4:
                                    p2a_nt(ti + 1, "tr")
                                ps, pr = bank()
                                for j in range(3):
                                    P.op("pe", (lambda ps, h, j: lambda e: e.matmul(ps[:], lhsT=WUQ[:, j, h * 128:(h + 1) * 128], rhs=cqn[:, j, :],
                                                                                    start=(j == 0), stop=(j == 2)))(ps, h, j),
                                         reads=[cqn_r, r_WUQ], writes=[pr])
                                P.op("act", (lambda ps, h, c0: lambda e: e.activation(out=QT[:, h, c0:c0 + 512], in_=ps[:], func=AF.Copy))(ps, h, c0),
                                     reads=[pr], writes=[QT_r])
                                ps, pr = bank()
                                for j in range(3):
                                    P.op("pe", (lambda ps, h, j: lambda e: e.matmul(ps[:], lhsT=WUQ[:, j, 1024 + h * 128:1024 + (h + 1) * 128], rhs=cqn[:, j, :],
                                                                                    start=(j == 0), stop=(j == 2)))(ps, h, j),
                                         reads=[cqn_r, r_WUQ], writes=[pr])
                                P.op("dve", (lambda ps: lambda e: e.tensor_tensor(out=qtmp[:], in0=ps[:], in1=ropeO[:], op=ALU.mult))(ps),
                                     reads=[pr, ropeO_r], writes=[qtmp_r])
                                ps, pr = bank()
                                P.op("pe", (lambda ps: lambda e: e.matmul(ps[:], lhsT=sel128[:], rhs=qtmp[:], start=True, stop=True))(ps),
                                     reads=[qtmp_r, r_const], writes=[pr])
                                po = (h % 2) * 64
                                P.op("act", (lambda ps, h, c0, po: lambda e: e.activation(out=QPE[po:po + 64, h // 2, c0:c0 + 512], in_=ps[po:po + 64, :], func=AF.Copy))(ps, h, c0, po),
                                     reads=[pr], writes=[QPE_r])
                _phase2()
                P.fence_all()
                def _phase3():
                    with contextlib.ExitStack() as sb_:
                        KPEs = sbuf(sb_, "KPEs", [128, 2, S], BF16); KPEs_r = Res()
                        KTh = [sbuf(sb_, "KTh%d" % i, [128, S], BF16) for i in range(2)]; KTh_r = [Res(), Res()]
                        VTh = [sbuf(sb_, "VTh%d" % i, [128, 32, 128], BF16) for i in range(2)]; VTh_r = [Res(), Res()]
                        NPT = 6
                        LA = 3
                        PTa = [sbuf(sb_, "PTa%d" % i, [128, 512], BF16) for i in range(NPT)]; PTa_r = [Res() for _ in range(NPT)]
                        rinvF = sbuf(sb_, "rinvF", [128, 512], F32); rinvF_r = Res()
                        OTn = sbuf(sb_, "OTn", [128, 512], BF16); OTn_r = Res()
                        P.op("pool", lambda e: e.memset(KPEs[:], 0.0), writes=[KPEs_r])
                        P.dma("sp", "KPEs", lambda e: e.dma_start(out=KPEs[0:64, 0, :], in_=KPE[:, :]), reads=scr_deps, writes=[KPEs_r])
                        P.dma("sp", "KPEs", lambda e: e.dma_start(out=KPEs[64:128, 1, :], in_=KPE[:, :]), reads=scr_deps, writes=[KPEs_r])
                        scale = 192.0 ** -0.5
                        its = []
                        gctr = 0
                        for h in range(8):
                            for G in range(4):
                                oO = 2 * (gctr % 2)
                                gctr += 1
                                nkb = 8 * G + 8
                                for kb in range(nkb):
                                    its.append(dict(h=h, G=G, kb=kb, nkb=nkb, oO=oO, oR=oO + 1, idx=len(its)))

                        def load_head(hh):
                            sl_ = hh % 2
                            P.dma("sp", "KTh%d" % sl_, (lambda hh, sl_: lambda e: e.dma_start(out=KTh[sl_][:], in_=KT[hh, :, :]))(hh, sl_),
                                  reads=scr_deps, writes=[KTh_r[sl_]])
                            P.dma("pool", "VTh%d" % sl_, (lambda hh, sl_: lambda e: e.dma_start(out=VTh[sl_][:],
                                                                                             in_=VT[:, :, hh * 128:(hh + 1) * 128].rearrange("b p f -> p b f")))(hh, sl_),
                                  reads=scr_deps, writes=[VTh_r[sl_]])

                        load_head(0)
                        load_head(1)

                        def emit_S(it):
                            h, G, kb = it["h"], it["G"], it["kb"]
                            sl = h % 2
                            po = (h % 2) * 64
                            if G == 1 and kb == 0 and 1 <= h < 7:
                                load_head(h + 1)
                            j0 = max(4 * G, kb // 2) - 4 * G
                            c0 = j0 * 128
                            q0 = G * 512
                            ks = slice(kb * 128, (kb + 1) * 128)
                            ps, pr = bank()
                            pt_i = it["idx"] % NPT
                            pt, pt_r = PTa[pt_i], PTa_r[pt_i]
                            it["pt"], it["pt_r"], it["c0"] = pt, pt_r, c0
                            P.op("pe", (lambda ps, c0, ks, h, q0, sl: lambda e: e.matmul(ps[:, c0:512], lhsT=KTh[sl][:, ks], rhs=QT[:, h, q0 + c0:q0 + 512],
                                                                                         start=True, stop=False))(ps, c0, ks, h, q0, sl),
                                 reads=[KTh_r[sl], QT_r], writes=[pr])
                            P.op("pe", (lambda ps, c0, ks, h, q0, po: lambda e: e.matmul(ps[:, c0:512], lhsT=KPEs[:, h % 2, ks],
                                                                                         rhs=QPE[:, h // 2, q0 + c0:q0 + 512], start=False, stop=True))(ps, c0, ks, h, q0, po),
                                 reads=[KPEs_r, QPE_r], writes=[pr])
                            P.op("act", (lambda ps, pt, c0: lambda e: e.activation(out=pt[:, c0:512], in_=ps[:, c0:512], func=AF.Exp, scale=scale))(ps, pt, c0),
                                 reads=[pr], writes=[pt_r])
                            if kb >= 8 * G:
                                P.op("dve", (lambda pt, c0, kb: lambda e: e.tensor_tensor(out=pt[:, c0:c0 + 128], in0=pt[:, c0:c0 + 128],
                                                                                          in1=amask_b[:, kb % 2, :], op=ALU.mult))(pt, c0, kb),
                                     reads=[pt_r, r_am], writes=[pt_r])

                        def emit_PV(it):
                            h, G, kb, nkb, oO, oR = it["h"], it["G"], it["kb"], it["nkb"], it["oO"], it["oR"]
                            sl = h % 2
                            pt, pt_r, c0 = it["pt"], it["pt_r"], it["c0"]
                            hc = slice(h * 128, (h + 1) * 128)
                            P.op("pe", (lambda oO, pt, c0, kb, sl, nkb: lambda e: e.matmul(psN[:, oO, c0:512], lhsT=VTh[sl][:, kb, 0:128], rhs=pt[:, c0:512],
                                                                                           start=(kb == 0), stop=(kb == nkb - 1), skip_group_check=True))(oO, pt, c0, kb, sl, nkb),
                                 reads=[pt_r, VTh_r[sl]], writes=[psN_r[oO]])
                            P.op("pe", (lambda oR, pt, c0, kb, nkb: lambda e: e.matmul(psN[:, oR, c0:512], lhsT=ones_bf[:], rhs=pt[:, c0:512],
                                                                                       start=(kb == 0), stop=(kb == nkb - 1), skip_group_check=True))(oR, pt, c0, kb, nkb),
                                 reads=[pt_r, r_const], writes=[psN_r[oR]])
                            if kb == nkb - 1:
                                P.op("dve", (lambda oR: lambda e: e.reciprocal(out=rinvF[:], in_=psN[:, oR, :]))(oR), reads=[psN_r[oR]], writes=[rinvF_r])
                                P.op("dve", (lambda oO: lambda e: e.tensor_tensor(out=OTn[:], in0=psN[:, oO, :], in1=rinvF[:], op=ALU.mult))(oO),
                                     reads=[psN_r[oO], rinvF_r], writes=[OTn_r])

                                def fin_pe(G=G, hc=hc):
                                    ps, pr = bank()
                                    for j in range(4):
                                        P.op("pe", (lambda ps, j: lambda e: e.matmul(ps[:, j * 128:(j + 1) * 128], lhsT=OTn[:, j * 128:(j + 1) * 128], rhs=ident_bf[:],
                                                                                     start=True, stop=True))(ps, j), reads=[OTn_r, r_const], writes=[pr])
                                    P.op("act", (lambda ps, G, hc: lambda e: e.activation(out=YB[:, 4 * G:4 * G + 4, hc], in_=ps[:].rearrange("p (j d) -> p j d", d=128),
                                                                                          func=AF.Copy))(ps, G, hc),
                                         reads=[pr], writes=[YB_r[4 * G + j] for j in range(4)])
                                pending.append([DEFER, fin_pe])

                        n_it = len(its)
                        DEFER = 3
                        pending = []
                        for i in range(n_it + LA):
                            if i < n_it:
                                emit_S(its[i])
                            if i - LA >= 0:
                                emit_PV(its[i - LA])
                            for pnd in list(pending):
                                pnd[0] -= 1
                                if pnd[0] <= 0:
                                    pnd[1]()
                                    pending.remove(pnd)
                        for pnd in pending:
                            pnd[1]()
                _phase3()
            P.fence_all()
            if debug:
                P.dma("sp", "dbg2", lambda e: e.dma_start(out=DBG2.rearrange("b p f -> p b f"), in_=YB[:]), reads=YB_r)
            def _phase4():
                with contextlib.ExitStack() as sc_:
                    WBg = sbuf(sc_, "WBg", [128, 6, 8, 512], BF16); r_WB = [Res() for _ in range(6)]
                    WO = sbuf(sc_, "WO", [128, 8, D], BF16); r_WO = Res()
                    for blk in (0, 2, 4, 1, 3, 5):
                        P.dma("pool", "WBg_%d" % blk, (lambda blk: lambda e: e.dma_start(out=WBg[:, blk, :, :], in_=wBg_b[blk].rearrange("p (k n) -> p k n", k=8)))(blk),
                              writes=[r_WB[blk]])
                    P.dma("pool", "WO", lambda e: e.dma_start(out=WO[:], in_=w_out.rearrange("(k p) n -> p k n", p=128)), writes=[r_WO])
                    hnB = sbuf(sc_, "hnB", [128, D], F32); hnB_r = Res()
                    P.dma("sp", "hnB", lambda e: e.dma_start(out=hnB[:], in_=bc_part(head_norm[0:1, :])), writes=[hnB_r])
                    xt = [sbuf(sc_, "xc%d" % i, [128, 2, D], F32) for i in range(3)]; xt_r = [Res(), Res(), Res()]
                    xn = sbuf(sc_, "xn5", [128, 2, D], BF16); xn_r = Res()
                    hT = sbuf(sc_, "hT5", [128, 8, 256], BF16); hT_r = Res()
                    junk = sbuf(sc_, "junk5", [128, D], BF16); junk_r = Res()
                    stat = sbuf(sc_, "stat5", [128, 12], F32); stat_r = Res(small=True)
                    ya = [sbuf(sc_, "ya%d" % i, [128, 2, D], F32) for i in range(3)]; ya_r = [Res(), Res(), Res()]
                    so2 = [sbuf(sc_, "so%d" % i, [128, 512], F32) for i in range(2)]; so2_r = [Res(), Res()]
                    sga2 = [sbuf(sc_, "sga%d" % i, [128, 512], F32) for i in range(2)]; sga2_r = [Res(), Res()]
                    sgb2 = [sbuf(sc_, "sgb%d" % i, [128, 512], F32) for i in range(2)]; sgb2_r = [Res(), Res()]
                    Ym = [sbuf(sc_, "Ym%d" % i, [128, D], BF16) for i in range(2)]; Ym_r = [Res(), Res()]
                    yT = [sbuf(sc_, "yT%d" % i, [128, 8, 128], BF16) for i in range(2)]; yT_r = [Res(), Res()]
                    yo = [sbuf(sc_, "yo%d" % i, [128, D], F32) for i in range(2)]; yo_r = [Res(), Res()]
                    st3 = sbuf(sc_, "st3", [128, 4], F32); st3_r = Res(small=True)
                    xcv = x_own.rearrange("(t b p) d -> t p b d", b=2, p=128)
                    def p2c_ld(ti):
                        sl = ti % 3
                        P.dma("sp", "ya%d" % sl, (lambda ti, sl: lambda e: e.dma_start(out=ya[sl][:], in_=YA[ti * 2:ti * 2 + 2, :, :].rearrange("b p f -> p b f")))(ti, sl),
                              reads=scr_deps, writes=[ya_r[sl]])
                        norm_transpose(xcv[ti], 2, xt[sl], xt_r[sl], xn, xn_r, hT, hT_r, AM, BM, junk, junk_r, stat, stat_r, "xc%d" % sl, part="dma")
                    def p2c_nt(ti):
                        sl = ti % 3
                        if ti + 1 < 8:
                            p2c_ld(ti + 1)
                        norm_transpose(xcv[ti], 2, xt[sl], xt_r[sl], xn, xn_r, hT, hT_r, AM, BM, junk, junk_r, stat, stat_r, "xc%d" % sl, part="pre_nodma")
                        norm_transpose(xcv[ti], 2, xt[sl], xt_r[sl], xn, xn_r, hT, hT_r, AM, BM, junk, junk_r, stat, stat_r, "xc%d" % sl, part="tr")
                    def p2c_A(lb):
                        ti = lb // 2
                        b = lb % 2
                        sl = ti % 3
                        s2 = lb % 2
                        P.op("pool", (lambda b, sl: lambda e: e.tensor_tensor(out=ya[sl][:, b, :], in0=ya[sl][:, b, :], in1=hnB[:], op=ALU.mult))(b, sl),
                             reads=[ya_r[sl], hnB_r], writes=[ya_r[sl]])
                        for half in range(2):
                            hs = slice(half * 512, (half + 1) * 512)
                            so, so_r, sga, sga_r, sgb, sgb_r = so2[half], so2_r[half], sga2[half], sga2_r[half], sgb2[half], sgb2_r[half]
                            for (goff, dst, dst_r) in ((0, so, so_r), (1024, sga, sga_r), (2048, sgb, sgb_r)):
                                ps, pr = bank()
                                for k in range(8):
                                    P.op("pe", (lambda ps, b, half, k, goff: lambda e: e.matmul(ps[:], lhsT=hT[:, k, b * 128:(b + 1) * 128],
                                                                                                rhs=WBg[:, (goff + half * 512) // 512, k, :],
                                                                                                start=(k == 0), stop=(k == 7)))(ps, b, half, k, goff),
                                         reads=[hT_r, r_WB[(goff + half * 512) // 512]], writes=[pr])
                                P.op("act", (lambda ps, dst: lambda e: e.activation(out=dst[:], in_=ps[:], func=AF.Sigmoid))(ps, dst), reads=[pr], writes=[dst_r])
                            P.op("dve", (lambda so, sga: lambda e: e.tensor_tensor(out=so[:], in0=so[:], in1=sga[:], op=ALU.mult))(so, sga),
                                 reads=[so_r, sga_r], writes=[so_r])
                            P.op("dve", (lambda b, sl, hs, so: lambda e: e.tensor_tensor(out=so[:], in0=so[:], in1=ya[sl][:, b, hs], op=ALU.mult))(b, sl, hs, so),
                                 reads=[so_r, ya_r[sl]], writes=[so_r])
                            P.op("pool", (lambda lb, hs, sgb: lambda e: e.tensor_tensor(out=sgb[:], in0=sgb[:], in1=YB[:, lb, hs], op=ALU.mult))(lb, hs, sgb),
                                 reads=[sgb_r, YB_r[lb]], writes=[sgb_r])
                            P.op("dve", (lambda s2, hs, so, sgb: lambda e: e.tensor_tensor(out=Ym[s2][:, hs], in0=so[:], in1=sgb[:], op=ALU.add))(s2, hs, so, sgb),
                                 reads=[so_r, sgb_r], writes=[Ym_r[s2]])
                    def p2c_B(lb):
                        ti = lb // 2
                        b = lb % 2
                        sl = ti % 3
                        s2 = lb % 2
                        for kk in range(2):
                            ps, pr = bank()
                            for k4 in range(4):
                                k = kk * 4 + k4
                                P.op("pe", (lambda ps, s2, k, k4: lambda e: e.matmul(ps[:, k4 * 128:(k4 + 1) * 128], lhsT=Ym[s2][:, k * 128:(k + 1) * 128],
                                                                                     rhs=ident_bf[:], start=True, stop=True))(ps, s2, k, k4),
                                     reads=[Ym_r[s2], r_const], writes=[pr])
                            P.op("act", (lambda ps, kk, s2: lambda e: e.activation(out=yT[s2][:, kk * 4:(kk + 1) * 4, :],
                                                                                   in_=ps[:].rearrange("p (k t) -> p k t", t=128), func=AF.Copy))(ps, kk, s2),
                                 reads=[pr], writes=[yT_r[s2]])
                        for half in range(2):
                            ps, pr = bank()
                            for k in range(8):
                                P.op("pe", (lambda ps, k, half, s2: lambda e: e.matmul(ps[:], lhsT=yT[s2][:, k, :], rhs=WO[:, k, half * 512:(half + 1) * 512],
                                                                                       start=(k == 0), stop=(k == 7)))(ps, k, half, s2),
                                     reads=[yT_r[s2], r_WO], writes=[pr])
                            P.op("act", (lambda ps, half, s2: lambda e: e.activation(out=yo[s2][:, half * 512:(half + 1) * 512], in_=ps[:], func=AF.Copy))(ps, half, s2),
                                 reads=[pr], writes=[yo_r[s2]])
                        P.op("dve", lambda e: e.memset(st3[:, 0:1], 0.0), writes=[st3_r])
                        P.op("act", (lambda s2: lambda e: e.activation(out=junk[:], in_=yo[s2][:], func=AF.Square, accum_out=st3[:, 0:1]))(s2),
                             reads=[yo_r[s2]], writes=[junk_r, st3_r])
                        P.op("act", lambda e: e.activation(out=st3[:, 1:2], in_=st3[:, 0:1], func=AF.Sqrt, bias=EPS, scale=1.0 / D), reads=[st3_r], writes=[st3_r])
                        P.op("dve", lambda e: e.reciprocal(out=st3[:, 2:3], in_=st3[:, 1:2]), reads=[st3_r], writes=[st3_r])
                        P.op("dve", (lambda s2: lambda e: e.scalar_tensor_tensor(out=yo[s2][:], in0=yo[s2][:], scalar=st3[:, 2:3], in1=GM[:],
                                                                                 op0=ALU.mult, op1=ALU.mult))(s2), reads=[yo_r[s2], st3_r, r_mod], writes=[yo_r[s2]])
                        P.op("pool", (lambda s2, sl, b: lambda e: e.tensor_tensor(out=yo[s2][:], in0=yo[s2][:], in1=xt[sl][:, b, :], op=ALU.add))(s2, sl, b),
                             reads=[yo_r[s2], xt_r[sl]], writes=[yo_r[s2]])
                        P.dma("pool", "x1w%d" % s2, (lambda lb, s2: lambda e: e.dma_start(out=X1[lb * 128:(lb + 1) * 128, :], in_=yo[s2][:]))(lb, s2),
                              reads=[yo_r[s2]])
                    p2c_ld(0)
                    p2c_nt(0)
                    p2c_A(0)
                    p2c_A(1)
                    for lb in range(16):
                        if lb % 2 == 0 and lb // 2 + 1 < 8:
                            p2c_nt(lb // 2 + 1)
                        p2c_B(lb)
                        if lb + 2 < 16:
                            p2c_A(lb + 2)
            _phase4()
        P.fence_all()
        x1_deps = []
        for key in ("x1w0", "x1w1"):
            rr = Res(key)
            rr.writer = ("dma:" + key, P.count["dma:" + key])
            x1_deps.append(rr)

        out_deps = []
        def _phase5():
            with contextlib.ExitStack() as st:
                W1 = sbuf(st, "W1", [128, 8, 8, 512], BF16); r_W1 = [Res() for _ in range(8)]
                W2 = sbuf(st, "W2", [128, 2, 32, 512], BF16); r_W2 = [Res(), Res()]
                for blk in range(8):
                    P.dma("pool", "W1_%d" % blk, (lambda blk: lambda e: e.dma_start(out=W1[:, blk, :, :], in_=w_ff1b[blk].rearrange("p (k n) -> p k n", k=8)))(blk),
                          writes=[r_W1[blk]])
                for half in range(2):
                    for q4 in range(4):
                        P.dma("pool", "W2_%d" % half, (lambda half, q4: lambda e: e.dma_start(out=W2[:, half, q4 * 8:(q4 + 1) * 8, :],
                                                                                    in_=w_ff2b[half][:, q4 * 4096:(q4 + 1) * 4096].rearrange("p (f n) -> p f n", f=8)))(half, q4),
                              writes=[r_W2[half]])
                xt = [sbuf(st, "x1t%d" % i, [128, 2, D], F32) for i in range(2)]; xt_r = [Res(), Res()]
                xn = sbuf(st, "xn3", [128, 2, D], BF16); xn_r = Res()
                hT = sbuf(st, "hT3", [128, 8, 256], BF16); hT_r = Res()
                junk = xn[:, 1, :]; junk_r = xn_r
                junkF = sbuf(st, "junkF", [128, 256], BF16); junkF_r = Res()
                st3q = sbuf(st, "st3q", [128, 4], F32)
                stat = sbuf(st, "stat4", [128, 12], F32); stat_r = Res(small=True)
                uT2 = [sbuf(st, "uT%d" % i, [128, 32, 256], BF16) for i in range(2)]; uT2_r = [Res(), Res()]
                _rl = sbuf(st, "rl0", [128, 512], F32); _rl_r = Res()
                rl = [_rl, _rl]; rl_r = [_rl_r, _rl_r]
                yo = [sbuf(st, "yo3_%d" % i, [128, D], F32) for i in range(2)]; yo_r = [Res(), Res()]
                st3 = sbuf(st, "st4", [128, 4], F32); st3_r = Res(small=True)
                x1v = X1.rearrange("(t b p) d -> t p b d", b=2, p=128)
                rctr = 0
                rctr_box = [0]
                def p3_nt(ti, part):
                    sl = ti % 2
                    norm_transpose(x1v[ti], 2, xt[sl], xt_r[sl], xn, xn_r, hT, hT_r, AFm, BFm, junk, junk_r, stat, stat_r, "x1t%d" % sl, dma_reads=x1_deps, part=part)
                def p3_F1(ti):
                    sl = ti % 2
                    uT, uT_r = uT2[ti % 2], uT2_r[ti % 2]
                    rctr = rctr_box[0]
                    for f2 in range(16):
                        ps, pr = bank()
                        for ff in range(2):
                            f = f2 * 2 + ff
                            for k in range(8):
                                P.op("pe", (lambda ps, ff, f, k: lambda e: e.matmul(ps[:, ff * 256:(ff + 1) * 256], lhsT=W1[:, f // 4, k, (f % 4) * 128:(f % 4 + 1) * 128], rhs=hT[:, k, 0:256],
                                                                                    start=(k == 0), stop=(k == 7)))(ps, ff, f, k),
                                     reads=[hT_r, r_W1[f // 4]], writes=[pr])
                        ri = rctr % 2
                        rctr += 1
                        P.op("act", (lambda ps, ri: lambda e: e.activation(out=rl[ri][:], in_=ps[:], func=AF.Relu))(ps, ri), reads=[pr], writes=[rl_r[ri]])
                        P.op("pool", (lambda f2, ri: lambda e: e.tensor_tensor(out=uT[:, f2 * 2:f2 * 2 + 2, :], in0=rl[ri][:].rearrange("p (a t) -> p a t", t=256),
                                                                               in1=rl[ri][:].rearrange("p (a t) -> p a t", t=256), op=ALU.mult))(f2, ri),
                             reads=[rl_r[ri]], writes=[uT_r])
                    rctr_box[0] = rctr
                def p3_F2(ti, b):
                    sl = ti % 2
                    uT, uT_r = uT2[ti % 2], uT2_r[ti % 2]
                    lb = ti * 2 + b
                    s2 = lb % 2
                    for half in range(2):
                        ps, pr = bank()
                        for f in range(32):
                            P.op("pe", (lambda ps, f, b, half: lambda e: e.matmul(ps[:], lhsT=uT[:, f, b * 128:(b + 1) * 128], rhs=W2[:, half, f, :],
                                                                                  start=(f == 0), stop=(f == 31)))(ps, f, b, half),
                                 reads=[uT_r, r_W2[half]], writes=[pr])
                        P.op("act", (lambda ps, half, s2: lambda e: e.activation(out=yo[s2][:, half * 512:(half + 1) * 512], in_=ps[:], func=AF.Copy))(ps, half, s2),
                             reads=[pr], writes=[yo_r[s2]])
                    P.op("dve", lambda e: e.memset(st3q[:], 0.0), writes=[st3_r])
                    for q_ in range(4):
                        P.op("act", (lambda s2, q_: lambda e: e.activation(out=junkF[:], in_=yo[s2][:, q_ * 256:(q_ + 1) * 256], func=AF.Square,
                                                                           accum_out=st3q[:, q_:q_ + 1]))(s2, q_),
                             reads=[yo_r[s2]], writes=[junkF_r, st3_r])
                    P.op("dve", lambda e: e.tensor_reduce(out=st3[:, 0:1], in_=st3q[:], axis=AX.X, op=ALU.add), reads=[st3_r], writes=[st3_r])
                    P.op("act", lambda e: e.activation(out=st3[:, 1:2], in_=st3[:, 0:1], func=AF.Sqrt, bias=EPS, scale=1.0 / D), reads=[st3_r], writes=[st3_r])
                    P.op("dve", lambda e: e.reciprocal(out=st3[:, 2:3], in_=st3[:, 1:2]), reads=[st3_r], writes=[st3_r])
                    P.op("dve", (lambda s2: lambda e: e.scalar_tensor_tensor(out=yo[s2][:], in0=yo[s2][:], scalar=st3[:, 2:3], in1=GF[:],
                                                                             op0=ALU.mult, op1=ALU.mult))(s2), reads=[yo_r[s2], st3_r, r_mod], writes=[yo_r[s2]])
                    P.op("dve", (lambda s2, sl, b: lambda e: e.tensor_tensor(out=yo[s2][:], in0=yo[s2][:], in1=xt[sl][:, b, :], op=ALU.add))(s2, sl, b),
                         reads=[yo_r[s2], xt_r[sl]], writes=[yo_r[s2]])
                    P.dma("pool", "out%d" % s2, (lambda lb, s2: lambda e: e.dma_start(out=out[lb * 128:(lb + 1) * 128, :], in_=yo[s2][:]))(lb, s2),
                          reads=[yo_r[s2]])
                p3_nt(0, "all")
                p3_F1(0)
                for ti in range(8):
                    if ti + 1 < 8:
                        p3_nt(ti + 1, "pre")
                    p3_F2(ti, 0)
                    if ti + 1 < 8:
                        p3_nt(ti + 1, "tr")
                        p3_F1(ti + 1)
                    p3_F2(ti, 1)
                for key in ("out0", "out1"):
                    rr = Res(key)
                    rr.writer = ("dma:" + key, P.count["dma:" + key])
                    out_deps.append(rr)
        _phase5()
        P.x1_deps = x1_deps
        fin = list(out_deps)
        if debug:
            for key in ("dbg", "dbg2"):
                rr = Res(key); rr.writer = ("dma:" + key, P.count["dma:" + key]); fin.append(rr)
        P.emit(final_waits=fin)
    return nc


def _prep_inputs(inp):
    f32 = np.float32
    x = np.asarray(inp["x"], f32)
    c = np.asarray(inp["c"], f32)
    L = 0
    w_in = np.asarray(inp["w_in"], f32)[L]
    splits = np.cumsum([512, 512, 1024, 1024, 4, 4, 384, 256, 64, 1024, 1024])[:-1]
    q_a, k_a, v_a, o_a, i_a, f_a, c_q, c_kv, k_pe, g_a, g_b = np.split(w_in, splits, axis=1)
    swap64 = np.concatenate([np.arange(32, 64), np.arange(0, 32)])
    wA = np.ascontiguousarray(np.concatenate([q_a, k_a, v_a, i_a, f_a, c_kv, k_pe, k_pe[:, swap64]], axis=1))
    wB = np.ascontiguousarray(np.concatenate([o_a, g_a, g_b, c_q], axis=1))
    assert wA.shape[1] == A_COLS and wB.shape[1] == B_COLS
    w_uq = np.asarray(inp["w_uq"], f32)[L].reshape(384, 8, 192)
    nope = w_uq[:, :, :128].reshape(384, 1024)
    pe = w_uq[:, :, 128:]
    pe2 = np.concatenate([pe, pe[:, :, swap64]], axis=2).reshape(384, 1024)
    w_uq2 = np.ascontiguousarray(np.concatenate([nope, pe2], axis=1))
    w_ukv = np.asarray(inp["w_ukv"], f32)[L].reshape(256, 8, 256)
    w_ukv2 = np.ascontiguousarray(np.concatenate([w_ukv[:, :, :128].reshape(256, 1024), w_ukv[:, :, 128:].reshape(256, 1024)], axis=1))
    fm = lambda v, k: np.ascontiguousarray(np.asarray(v, f32).reshape(k, 128).T)
    conv_w = np.asarray(inp["mlstm_conv_w"], f32)[L]
    conv_w_t = np.ascontiguousarray(conv_w.reshape(4, 8, 128).transpose(2, 1, 0))
    half = 32
    inv_freq = 10000.0 ** (-np.arange(half, dtype=np.float64) / half)
    pos = np.arange(S, dtype=np.float64)
    ang = pos[:, None] * inv_freq[None, :]
    cos, sin = np.cos(ang).astype(f32).T, np.sin(ang).astype(f32).T
    rope_tab = np.ascontiguousarray(np.concatenate([cos, cos, -sin, sin], axis=0))
    tri = np.triu(np.ones((128, 128), f32))
    common = dict(
        w_ada=np.ascontiguousarray(np.asarray(inp["w_ada"], f32)[L]),
        b_ada=np.ascontiguousarray(np.asarray(inp["b_ada"], f32)[L].reshape(1, -1)),
        wA=wA, wB=wB, w_uq2=w_uq2, w_ukv2=w_ukv2,
        w_out=np.ascontiguousarray(np.asarray(inp["w_out"], f32)[L]),
        w_ff1b=np.ascontiguousarray(np.asarray(inp["w_ff1"], f32)[L].reshape(8, 128, 8, 512).transpose(2, 1, 0, 3).reshape(8, 128, 4096)),
        w_ff2b=np.ascontiguousarray(np.asarray(inp["w_ff2"], f32)[L].reshape(32, 128, 2, 512).transpose(2, 1, 0, 3).reshape(2, 128, 16384)),
        wBg_b=np.ascontiguousarray(wB[:, :3072].reshape(8, 128, 6, 512).transpose(2, 1, 0, 3).reshape(6, 128, 4096)),
        npre_mix_t=fm(inp["norm_pre_mix"][L], 8), npre_mlp_t=fm(inp["norm_pre_mlp"][L], 8),
        npost_mix=np.ascontiguousarray(np.asarray(inp["norm_post_mix"], f32)[L].reshape(1, -1)),
        npost_mlp=np.ascontiguousarray(np.asarray(inp["norm_post_mlp"], f32)[L].reshape(1, -1)),
        conv_w_t=conv_w_t, conv_b_t=fm(inp["mlstm_conv_b"][L], 8),
        gate_b4=np.ascontiguousarray(np.tile(np.asarray(inp["mlstm_gate_b"], f32)[L], 4).reshape(1, 32)),
        head_norm=np.ascontiguousarray(np.asarray(inp["mlstm_head_norm"], f32)[L].reshape(1, -1)),
        qn_t=fm(inp["mla_q_norm"][L], 3), kvn_t=fm(inp["mla_kv_norm"][L], 2),
        rope_all=rope_tab,
    )
    maps = []
    for core in range(8):
        b, j = core // 2, core % 2
        xb = x[b]
        own = xb.reshape(16, 2, 128, D)[:, j].reshape(NOWN, D)
        own_pos = (np.arange(S).reshape(16, 2, 128)[:, j]).reshape(-1)
        am = np.zeros((128, 2, 128), f32)
        if j == 0:
            am[:, 0, :] = tri
        else:
            am[:, 0, :] = 1.0
            am[:, 1, :] = tri
        sm = np.zeros((128, 2), f32)
        sm[:, j] = 1.0
        m = dict(common)
        m.update(x_all=np.ascontiguousarray(xb), x_own=np.ascontiguousarray(own), c_t=fm(c[b], 8),
                 rope_own=np.ascontiguousarray(rope_tab[:, own_pos]), amask=am, selm=sm)
        maps.append(m)
    return maps


_NC_CACHE = {}


def run(inputs, debug=False, cores=8):
    maps = _prep_inputs(inputs)
    if debug not in _NC_CACHE:
        _NC_CACHE[debug] = build_program(debug)
    nc = _NC_CACHE[debug]
    res = run_bass_kernel_spmd(nc, maps[:cores], core_ids=list(range(cores)))
    return res


def kernel(**inputs):
    res = run(inputs)
    outp = np.zeros((4, S, D), np.float32)
    ov = outp.reshape(4, 16, 2, 128, D)
    for core in range(8):
        b, j = core // 2, core % 2
        ov[b, :, j] = np.asarray(res.results[core]["out"], np.float32).reshape(16, 128, D)
    return outp
```

```python
import contextlib
import os
import numpy as np
import concourse.bass as bass
import concourse.mybir as mybir
from concourse.bass_utils import run_bass_kernel_spmd

F32 = mybir.dt.float32
BF16 = mybir.dt.bfloat16
AF = mybir.ActivationFunctionType
ALU = mybir.AluOpType
AX = mybir.AxisListType
AP = bass.AP

SAME_ENGINE_SYNC = True
SYNC_ALL = False
EPS = 1e-6
S = 4096
D = 1024
NOWN = 2048


class Res:
    __slots__ = ("name", "writer", "readers", "small")

    def __init__(self, name="", small=False):
        self.name = name
        self.writer = None
        self.readers = {}
        self.small = small


class Prog:
    ENGS = ("pe", "act", "dve", "pool", "sp")

    def __init__(self, nc):
        self.nc = nc
        self.q = {e: [] for e in self.ENGS}
        self.count = {}
        self.waited = {e: {} for e in self.ENGS}
        self.fence = {}

    def fence_all(self):
        self.fence = dict(self.count)

    def _add(self, queue, track, fn, reads, writes):
        idx = self.count.get(track, 0) + 1
        self.count[track] = idx
        waits = {}

        def need(dep, res=None):
            t2, i2 = dep
            if t2 == track:
                if track == "pe" or track.startswith("dma") or not SAME_ENGINE_SYNC:
                    return
                if res is None or not (res.small or SYNC_ALL):
                    return
            if self.waited[queue].get(t2, 0) >= i2:
                return
            waits[t2] = max(waits.get(t2, 0), i2)

        for t2, i2 in self.fence.items():
            if t2 != track:
                need((t2, i2))
        for r in reads:
            if r.writer is not None:
                need(r.writer, r)
        for w in writes:
            if w.writer is not None:
                need(w.writer, w)
            for t2, i2 in w.readers.items():
                need((t2, i2), w)
        for t2, i2 in waits.items():
            self.waited[queue][t2] = i2
        for r in reads:
            r.readers[track] = idx
        for w in writes:
            w.writer = (track, idx)
            w.readers = {}
        self.q[queue].append((fn, waits, track, idx))

    def op(self, eng, fn, reads=(), writes=()):
        self._add(eng, eng, fn, reads, writes)

    def dma(self, queue, key, fn, reads=(), writes=()):
        self._add(queue, "dma:" + key, fn, reads, writes)

    def emit(self, final_waits=()):
        nc = self.nc
        needed = {}
        for e in self.ENGS:
            for fn, waits, track, idx in self.q[e]:
                for t2, i2 in waits.items():
                    needed.setdefault(t2, set()).add(i2)
        fin = {}
        for r in final_waits:
            if r.writer is not None:
                t2, i2 = r.writer
                fin[t2] = max(fin.get(t2, 0), i2)
                needed.setdefault(t2, set()).add(i2)
        rank = {}
        for t, s_ in needed.items():
            if t.startswith("dma"):
                rank[t] = {i: i for i in range(1, self.count[t] + 1)}
            else:
                rank[t] = {i: k + 1 for k, i in enumerate(sorted(s_))}
        tracks = sorted(needed.keys())
        with contextlib.ExitStack() as st:
            sems = {t: st.enter_context(nc.semaphore("s_" + t.replace(":", "_"))) for t in tracks}
            block = st.enter_context(nc.Block())

            def run(ename, e):
                for fn, waits, track, idx in self.q[ename]:
                    for t2, i2 in waits.items():
                        e.wait_ge(sems[t2], rank[t2][i2] * (16 if t2.startswith("dma") else 1))
                    ins = fn(e)
                    if track in rank and idx in rank[track]:
                        ins.then_inc(sems[track], 16 if track.startswith("dma") else 1)
                if ename == "sp":
                    for t2, i2 in fin.items():
                        e.wait_ge(sems[t2], rank[t2][i2] * (16 if t2.startswith("dma") else 1))

            block.sync(lambda e: run("sp", e))
            block.scalar(lambda e: run("act", e))
            block.vector(lambda e: run("dve", e))
            block.gpsimd(lambda e: run("pool", e))
            block.tensor(lambda e: run("pe", e))


def bc_mid(ap, n):
    a = [list(x) for x in ap.ap]
    return AP(ap.tensor, ap.offset, [a[0], [0, n]] + a[1:])


def bc_part(ap, n=128):
    a = [list(x) for x in ap.ap]
    return AP(ap.tensor, ap.offset, [[0, n]] + a[1:])


A_QK, A_V, A_G, A_CKV, A_KPE = 0, 1024, 2048, 2056, 2312
A_COLS = 2440
B_O, B_GA, B_GB, B_CQ = 0, 1024, 2048, 3072
B_COLS = 3456


def build_program(debug=False):
    nc = bass.Bass("TRN2", target_bir_lowering=False)
    dt_in = lambda name, shape: nc.dram_tensor(name, shape, F32, kind="ExternalInput").ap()
    x_all = dt_in("x_all", [S, D])
    x_own = dt_in("x_own", [NOWN, D])
    c_t = dt_in("c_t", [128, 8])
    w_ada = dt_in("w_ada", [D, 6 * D])
    b_ada = dt_in("b_ada", [1, 6 * D])
    wA = dt_in("wA", [D, A_COLS])
    wB = dt_in("wB", [D, B_COLS])
    w_uq2 = dt_in("w_uq2", [384, 2048])
    w_ukv2 = dt_in("w_ukv2", [256, 2048])
    w_out = dt_in("w_out", [D, D])
    w_ff1b = dt_in("w_ff1b", [8, 128, 8 * 512])
    w_ff2b = dt_in("w_ff2b", [2, 128, 32 * 512])
    wBg_b = dt_in("wBg_b", [6, 128, 8 * 512])
    npre_mix_t = dt_in("npre_mix_t", [128, 8])
    npre_mlp_t = dt_in("npre_mlp_t", [128, 8])
    npost_mix = dt_in("npost_mix", [1, D])
    npost_mlp = dt_in("npost_mlp", [1, D])
    conv_w_t = dt_in("conv_w_t", [128, 8, 4])
    conv_b_t = dt_in("conv_b_t", [128, 8])
    gate_b4 = dt_in("gate_b4", [1, 32])
    head_norm = dt_in("head_norm", [1, D])
    qn_t = dt_in("qn_t", [128, 3])
    kvn_t = dt_in("kvn_t", [128, 2])
    rope_all = dt_in("rope_all", [128, S])
    rope_own = dt_in("rope_own", [128, NOWN])
    amask = dt_in("amask", [128, 2, 128])
    selm = dt_in("selm", [128, 2])
    out = nc.dram_tensor("out", [NOWN, D], F32, kind="ExternalOutput").ap()
    skind = "ExternalOutput" if debug else "Internal"
    KT = nc.dram_tensor("KT", [8, 128, S], BF16, kind=skind).ap()
    VT = nc.dram_tensor("VT", [32, 128, 8 * 128], BF16, kind=skind).ap()
    KPE = nc.dram_tensor("KPE", [64, S], BF16, kind=skind).ap()
    YA = nc.dram_tensor("YA", [16, 128, D], F32, kind=skind).ap()
    X1 = nc.dram_tensor("X1", [NOWN, D], F32, kind=skind).ap()
    if debug:
        DBG = nc.dram_tensor("DBG", [128, 6 * D], F32, kind="ExternalOutput").ap()
        DBG2 = nc.dram_tensor("DBG2", [16, 128, D], BF16, kind="ExternalOutput").ap()
        D_hT = nc.dram_tensor("D_hT", [128, 8, 512], BF16, kind="ExternalOutput").ap()
        D_qT = nc.dram_tensor("D_qT", [128, 4, 512], BF16, kind="ExternalOutput").ap()
        D_kT = nc.dram_tensor("D_kT", [128, 4, 512], BF16, kind="ExternalOutput").ap()
        D_g = nc.dram_tensor("D_g", [128, 4, 8], F32, kind="ExternalOutput").ap()
        D_ev = nc.dram_tensor("D_ev", [128, 3, 16], F32, kind="ExternalOutput").ap()
        D_ckvn = nc.dram_tensor("D_ckvn", [128, 2, 512], BF16, kind="ExternalOutput").ap()
        D_ktmp = nc.dram_tensor("D_ktmp", [128, 512], BF16, kind="ExternalOutput").ap()
        D_Vt = nc.dram_tensor("D_Vt", [128, 4, 4, 257], BF16, kind="ExternalOutput").ap()
        D_ps = nc.dram_tensor("D_ps", [128, 4, 257], F32, kind="ExternalOutput").ap()
        D_sc4 = nc.dram_tensor("D_sc4", [128, 8, 4], F32, kind="ExternalOutput").ap()
        D_kpst = nc.dram_tensor("D_kpst", [64, 512], BF16, kind="ExternalOutput").ap()

    P = Prog(nc)
    with contextlib.ExitStack() as top:
        def sbuf(st, name, shape, dt):
            return st.enter_context(nc.sbuf_tensor(name, shape, dt))

        psg = [top.enter_context(nc.psum_tensor("psg%d" % i, [128, 512], F32)) for i in range(4)]
        psg_r = [Res("psg%d" % i) for i in range(4)]
        psN = top.enter_context(nc.psum_tensor("psN", [128, 4, 512], F32))
        psN_r = [Res("psN%d" % i) for i in range(4)]
        pctr = [0]

        def bank():
            i = pctr[0] % 4
            pctr[0] += 1
            return psg[i], psg_r[i]

        ident_bf = sbuf(top, "ident_bf", [128, 128], BF16)
        ident_f = sbuf(top, "ident_f", [128, 128], F32)
        ones_f = sbuf(top, "ones_f", [128, 128], F32)
        ones_bf = sbuf(top, "ones_bf", [128, 128], BF16)
        tri_f = sbuf(top, "tri_f", [128, 128], F32)
        maskS = sbuf(top, "maskS", [128, 128], F32)
        sel = sbuf(top, "sel", [128, 64], BF16)
        sel2 = sbuf(top, "sel2", [128, 64], BF16)
        sel128 = sbuf(top, "sel128", [128, 128], BF16)
        amask_s = sbuf(top, "amask_s", [128, 2, 128], F32)
        amask_b = sbuf(top, "amask_b", [128, 2, 128], BF16)
        selm_s = sbuf(top, "selm_s", [128, 2], F32)
        r_const = Res("const", small=True)
        _cr = {}

        def pl(fn, name, last=False):
            r = _cr.setdefault(name, Res("c_" + name, small=True))
            P.op("pool", fn, reads=[r], writes=[r] + ([r_const] if last else []))

        def sel_eq(t, ncols, base):
            return lambda e: e.affine_select(out=t[:], in_=t[:], pattern=[[-1, ncols]], compare_op=ALU.is_equal, fill=0.0,
                                             base=base, channel_multiplier=1)

        def sel_ge(t):
            return lambda e: e.affine_select(out=t[:], in_=t[:], pattern=[[1, 128]], compare_op=ALU.is_ge, fill=0.0,
                                             base=0, channel_multiplier=-1)

        pl(lambda e: e.memset(ident_bf[:], 1.0), "ident_bf")
        pl(sel_eq(ident_bf, 128, 0), "ident_bf")
        pl(lambda e: e.memset(ident_f[:], 1.0), "ident_f")
        pl(sel_eq(ident_f, 128, 0), "ident_f")
        pl(lambda e: e.memset(ones_f[:], 1.0), "ones_f")
        pl(lambda e: e.memset(ones_bf[:], 1.0), "ones_bf")
        pl(lambda e: e.memset(tri_f[:], 1.0), "tri_f")
        pl(sel_ge(tri_f), "tri_f")
        pl(lambda e: e.memset(maskS[:], 128.0 ** -0.5), "maskS")
        pl(sel_ge(maskS), "maskS")
        pl(lambda e: e.memset(sel[:], 1.0), "sel")
        pl(sel_eq(sel, 64, 0), "sel")
        pl(lambda e: e.memset(sel2[:], 1.0), "sel2")
        pl(sel_eq(sel2, 64, -64), "sel2")
        P.op("pool", lambda e: e.tensor_tensor(out=sel[:], in0=sel[:], in1=sel2[:], op=ALU.add), reads=[_cr["sel"], _cr["sel2"]], writes=[_cr["sel"]])
        P.op("pool", lambda e: e.tensor_copy(out=sel128[:, 0:64], in_=sel[:]), reads=[_cr["sel"]], writes=[r_const])
        P.op("pool", lambda e: e.tensor_copy(out=sel128[:, 64:128], in_=sel[:]), reads=[_cr["sel"]] + list(_cr.values()), writes=[r_const])
        r_am = Res("amask", small=True)
        r_cd = [Res("cd%d" % i) for i in range(5)]
        P.dma("sp", "const1", lambda e: e.dma_start(out=amask_s[:], in_=amask[:, :, :]), writes=[r_cd[0]])
        P.dma("sp", "const2", lambda e: e.dma_start(out=selm_s[:], in_=selm[:, :]), writes=[r_cd[1]])

        npre_mix_s = sbuf(top, "npre_mix_s", [128, 8], F32)
        npre_mlp_s = sbuf(top, "npre_mlp_s", [128, 8], F32)
        cw_s = sbuf(top, "cw_s", [128, 8, 4], F32)
        cb_s = sbuf(top, "cb_s", [128, 8], F32)
        gb_s = sbuf(top, "gb_s", [128, 32], F32)
        qn_s = sbuf(top, "qn_s", [128, 3], F32)
        kvn_s = sbuf(top, "kvn_s", [128, 2], F32)
        r_small = Res("small", small=True)
        for dst, src in ((npre_mix_s, npre_mix_t), (npre_mlp_s, npre_mlp_t), (cb_s, conv_b_t),
                         (qn_s, qn_t), (kvn_s, kvn_t)):
            P.dma("sp", "const3", (lambda dst, src: lambda e: e.dma_start(out=dst[:], in_=src[:, :]))(dst, src),
                  writes=[r_cd[2]])
        P.dma("sp", "const4", lambda e: e.dma_start(out=cw_s[:], in_=conv_w_t[:, :, :]), writes=[r_cd[3]])
        P.dma("sp", "const5", lambda e: e.dma_start(out=gb_s[:], in_=bc_part(gate_b4[0:1, :])), writes=[r_cd[4]])
        P.op("dve", lambda e: e.tensor_copy(out=amask_b[:], in_=amask_s[:]), reads=r_cd, writes=[r_am, r_small])

        GM = sbuf(top, "GM", [128, D], F32)
        GF = sbuf(top, "GF", [128, D], F32)
        AM = sbuf(top, "AM", [128, 8], F32)
        BM = sbuf(top, "BM", [128, 8], F32)
        AFm = sbuf(top, "AFm", [128, 8], F32)
        BFm = sbuf(top, "BFm", [128, 8], F32)
        r_mod = Res("mod", small=True)

        def _phase0():
            with contextlib.ExitStack() as st:
                MODB = sbuf(st, "MODB", [128, 6 * D], F32)
                badaB = sbuf(st, "badaB", [128, 6 * D], F32)
                wa = [sbuf(st, "wa%d" % i, [128, 8, 512], BF16) for i in range(2)]
                wa_r = [Res("wa0"), Res("wa1")]
                cs = sbuf(st, "cs", [128, 8], F32)
                scs = sbuf(st, "scs", [128, 8], F32)
                screp = sbuf(st, "screp", [128, 8, 128], BF16)
                npmB = sbuf(st, "npmB", [128, D], F32)
                npfB = sbuf(st, "npfB", [128, D], F32)
                dtmp = sbuf(st, "dtmp", [128, 16, 128], F32)
                dg = sbuf(st, "dg", [128, 32], F32)
                r_cs, r_screp, r_bada, r_modb, r_np = Res(small=True), Res(), Res(), Res(small=True), Res()
                P.dma("sp", "p0a", lambda e: e.dma_start(out=cs[:], in_=c_t[:, :]), writes=[r_cs])
                P.dma("sp", "p0b", lambda e: e.dma_start(out=badaB[:], in_=bc_part(b_ada[0:1, :])), writes=[r_bada])
                P.dma("sp", "p0c", lambda e: e.dma_start(out=npmB[:], in_=bc_part(npost_mix[0:1, :])), writes=[r_np])
                P.dma("sp", "p0c", lambda e: e.dma_start(out=npfB[:], in_=bc_part(npost_mlp[0:1, :])), writes=[r_np])
                P.op("act", lambda e: e.activation(out=scs[:], in_=cs[:], func=AF.Silu), reads=[r_cs], writes=[r_cs])
                for k in range(8):
                    P.op("dve", (lambda k: lambda e: e.tensor_scalar(out=screp[:, k, :], in0=ones_f[:], scalar1=scs[:, k:k + 1],
                                                                      scalar2=None, op0=ALU.mult))(k),
                         reads=[r_cs, r_const], writes=[r_screp])
                wadv = w_ada.rearrange("(k p) n -> p k n", p=128)
                for ng in range(12):
                    sl = ng % 2
                    P.dma("pool", "wa%d" % sl, (lambda ng, sl: lambda e: e.dma_start(out=wa[sl][:], in_=wadv[:, :, ng * 512:(ng + 1) * 512]))(ng, sl),
                          writes=[wa_r[sl]])
                    ps, pr = bank()
                    for k in range(8):
                        P.op("pe", (lambda ps, k, sl: lambda e: e.matmul(ps[:], lhsT=screp[:, k, :], rhs=wa[sl][:, k, :],
                                                                         start=(k == 0), stop=(k == 7)))(ps, k, sl),
                             reads=[r_screp, wa_r[sl]], writes=[pr])
                    P.op("dve", (lambda ps, ng: lambda e: e.tensor_tensor(out=MODB[:, ng * 512:(ng + 1) * 512], in0=ps[:],
                                                                          in1=badaB[:, ng * 512:(ng + 1) * 512], op=ALU.add))(ps, ng),
                         reads=[pr, r_bada], writes=[r_modb])
                if debug:
                    P.dma("sp", "dbg", lambda e: e.dma_start(out=DBG[:, :], in_=MODB[:]), reads=[r_modb])
                P.op("dve", lambda e: e.tensor_tensor(out=GM[:], in0=MODB[:, 2 * D:3 * D], in1=npmB[:], op=ALU.mult),
                     reads=[r_modb, r_np], writes=[r_mod])
                P.op("dve", lambda e: e.tensor_tensor(out=GF[:], in0=MODB[:, 5 * D:6 * D], in1=npfB[:], op=ALU.mult),
                     reads=[r_modb, r_np], writes=[r_mod])
                for half, off in ((0, 0), (1, 3 * D)):
                    P.op("dve", (lambda off: lambda e: e.tensor_tensor(
                        out=dtmp[:], in0=MODB[:, off:off + 2 * D].rearrange("p (a b) -> p a b", b=128),
                        in1=bc_mid(ident_f[:], 16), op=ALU.mult))(off), reads=[r_modb, r_const], writes=[r_modb])
                    P.op("dve", (lambda half: lambda e: e.tensor_reduce(out=dg[:, half * 16:(half + 1) * 16], in_=dtmp[:],
                                                                         axis=AX.X, op=ALU.add))(half),
                         reads=[r_modb], writes=[r_modb])
                P.op("dve", lambda e: e.scalar_tensor_tensor(out=AM[:], in0=dg[:, 8:16], scalar=1.0, in1=npre_mix_s[:],
                                                             op0=ALU.add, op1=ALU.mult), reads=[r_modb, r_small], writes=[r_mod])
                P.op("dve", lambda e: e.tensor_copy(out=BM[:], in_=dg[:, 0:8]), reads=[r_modb], writes=[r_mod])
                P.op("dve", lambda e: e.scalar_tensor_tensor(out=AFm[:], in0=dg[:, 24:32], scalar=1.0, in1=npre_mlp_s[:],
                                                             op0=ALU.add, op1=ALU.mult), reads=[r_modb, r_small], writes=[r_mod])
                P.op("dve", lambda e: e.tensor_copy(out=BFm[:], in_=dg[:, 16:24]), reads=[r_modb], writes=[r_mod])

        _phase0()
        P.fence_all()
        def norm_transpose(src_ap_blocks, nb, xt, xt_r, xn, xn_r, hT, hT_r, Asc, Bsc, junk, junk_r, stat, stat_r, dkey, dma_reads=(), part="all"):
            if part == "dma":
                P.dma("sp", dkey, lambda e: e.dma_start(out=xt[:, 0:nb, :], in_=src_ap_blocks), reads=list(dma_reads), writes=[xt_r])
                return
            if part in ("all", "pre", "pre_nodma"):
                _nt_pre(src_ap_blocks, nb, xt, xt_r, xn, xn_r, junk, junk_r, stat, stat_r, dkey, dma_reads, do_dma=(part != "pre_nodma"))
            if part in ("all", "tr"):
                _nt_tr(nb, xn, xn_r, hT, hT_r, Asc, Bsc)

        def _nt_pre(src_ap_blocks, nb, xt, xt_r, xn, xn_r, junk, junk_r, stat, stat_r, dkey, dma_reads, do_dma=True):
            if do_dma:
                P.dma("sp", dkey, lambda e: e.dma_start(out=xt[:, 0:nb, :], in_=src_ap_blocks), reads=list(dma_reads), writes=[xt_r])
            P.op("dve", lambda e: e.memset(stat[:, 0:4], 0.0), writes=[stat_r])
            for b in range(nb):
                P.op("act", (lambda b: lambda e: e.activation(out=junk[:], in_=xt[:, b, :], func=AF.Square,
                                                              accum_out=stat[:, b:b + 1]))(b),
                     reads=[xt_r], writes=[junk_r, stat_r])
            P.op("act", lambda e: e.activation(out=stat[:, 4:4 + nb], in_=stat[:, 0:nb], func=AF.Sqrt, bias=EPS, scale=1.0 / D),
                 reads=[stat_r], writes=[stat_r])
            P.op("dve", lambda e: e.reciprocal(out=stat[:, 8:8 + nb], in_=stat[:, 4:4 + nb]),
                 reads=[stat_r], writes=[stat_r])
            for b in range(nb):
                P.op("dve", (lambda b: lambda e: e.tensor_scalar(out=xn[:, b, :], in0=xt[:, b, :], scalar1=stat[:, 8 + b:9 + b],
                                                                 scalar2=None, op0=ALU.mult))(b),
                     reads=[xt_r, stat_r], writes=[xn_r])

        def _nt_tr(nb, xn, xn_r, hT, hT_r, Asc, Bsc):
            for k in range(8):
                ps, pr = bank()
                for b in range(nb):
                    P.op("pe", (lambda ps, b, k: lambda e: e.matmul(ps[:, b * 128:(b + 1) * 128], lhsT=xn[:, b, k * 128:(k + 1) * 128],
                                                                    rhs=ident_bf[:], start=True, stop=True))(ps, b, k),
                         reads=[xn_r, r_const], writes=[pr])
                P.op("act", (lambda ps, k: lambda e: e.activation(out=hT[:, k, 0:nb * 128], in_=ps[:, 0:nb * 128], func=AF.Identity,
                                                                  bias=Bsc[:, k:k + 1], scale=Asc[:, k:k + 1]))(ps, k),
                     reads=[pr, r_mod], writes=[hT_r])

        def rms_feature_major(ps_list, pr_list, nj, wsc, outT, out_r, sq, sq_r, rstd, rstd_r, n_feat):
            for j in range(nj):
                P.op("act", (lambda j: lambda e: e.activation(out=sq[:, j, :], in_=ps_list[j][:], func=AF.Square))(j),
                     reads=[pr_list[j]], writes=[sq_r])
            ps, pr = bank()
            for j in range(nj):
                P.op("pe", (lambda ps, j: lambda e: e.matmul(ps[:], lhsT=ones_bf[:], rhs=sq[:, j, :], start=(j == 0), stop=(j == nj - 1)))(ps, j),
                     reads=[sq_r, r_const], writes=[pr])
            P.op("act", (lambda ps: lambda e: e.activation(out=rstd[:], in_=ps[:], func=AF.Sqrt, bias=EPS, scale=1.0 / n_feat))(ps),
                 reads=[pr], writes=[rstd_r])
            P.op("dve", lambda e: e.reciprocal(out=rstd[:], in_=rstd[:]), reads=[rstd_r], writes=[rstd_r])
            for j in range(nj):
                P.op("dve", (lambda j: lambda e: e.scalar_tensor_tensor(out=outT[:, j, :], in0=ps_list[j][:], scalar=wsc[:, j:j + 1],
                                                                        in1=rstd[:], op0=ALU.mult, op1=ALU.mult))(j),
                     reads=[pr_list[j], rstd_r, r_small], writes=[out_r])

        def _phase1():
            with contextlib.ExitStack() as st:
                WA = sbuf(st, "WA", [128, 8, A_COLS], BF16)
                WUKV = sbuf(st, "WUKV", [128, 2, 2048], BF16)
                r_WA, r_WUKV = Res("WA"), Res("WUKV")
                wAv = wA.rearrange("(k p) n -> p k n", p=128)
                for k in range(8):
                    P.dma("pool", "WA", (lambda k: lambda e: e.dma_start(out=WA[:, k, :], in_=wAv[:, k, :]))(k), writes=[r_WA])
                P.dma("pool", "WUKV", lambda e: e.dma_start(out=WUKV[:], in_=w_ukv2.rearrange("(k p) n -> p k n", p=128)), writes=[r_WUKV])
                xt = [sbuf(st, "xt%d" % i, [128, 4, D], F32) for i in range(2)]
                xt_r = [Res(), Res()]
                xn = sbuf(st, "xn", [128, 4, D], BF16); xn_r = Res()
                hT = sbuf(st, "hT", [128, 8, 512], BF16); hT_r = Res()
                stat = sbuf(st, "stat", [128, 12], F32); stat_r = Res(small=True)
                xqk = sbuf(st, "xqk", [128, 2, 515], F32); xqk_r = [Res(), Res()]
                hal = sbuf(st, "hal", [128, 8, 3], F32); hal_r = Res(small=True)
                junkM = sbuf(st, "junkM", [128, 256], BF16); junkM_r = Res()
                acc = sbuf(st, "acc", [128, 512], F32); acc_r = Res()
                qT2 = [sbuf(st, "qT%d" % i, [128, 4, 512], BF16) for i in range(2)]; qT2_r = [Res(), Res()]
                kT2 = [sbuf(st, "kT%d" % i, [128, 4, 512], BF16) for i in range(2)]; kT2_r = [Res(), Res()]
                Ktok2 = [sbuf(st, "Ktok%d" % i, [128, 4, 512], BF16) for i in range(2)]; Ktok2_r = [Res(), Res()]
                g = sbuf(st, "g", [128, 4, 8], F32); g_r = Res(small=True)
                spv = sbuf(st, "spv", [128, 4, 4], F32); sp_r = Res(small=True)
                wv = sbuf(st, "wv", [128, 4, 4], F32); wv_r = Res(small=True)
                ez = sbuf(st, "ez", [128, 4, 4], F32); ez_r = Res(small=True)
                ev2 = [sbuf(st, "ev%d" % i, [128, 4, 4], F32) for i in range(2)]
                eb2 = [sbuf(st, "eb%d" % i, [128, 4, 4], F32) for i in range(2)]
                eL2 = [sbuf(st, "eL%d" % i, [128, 4, 4], F32) for i in range(2)]
                ebi2 = [sbuf(st, "ebi%d" % i, [128, 4, 4], F32) for i in range(2)]
                gate2_r = [Res(small=True), Res(small=True)]
                Vt2 = [sbuf(st, "Vt%d" % i, [128, 4, 4, 257], BF16) for i in range(2)]; Vt2_r = [Res(), Res()]
                sq = sbuf(st, "sq", [128, 2, 512], BF16); sq_r = Res()
                rstd = sbuf(st, "rstd", [128, 512], F32); rstd_r = Res()
                ckvn = sbuf(st, "ckvn", [128, 2, 512], BF16); ckvn_r = Res()
                kst = sbuf(st, "kst", [128, 8, 512], BF16); kst_r = Res()
                vst = sbuf(st, "vst", [128, 4, 8, 128], BF16); vst_r = Res()
                ropeA2 = [sbuf(st, "ropeA%d" % i, [128, 512], F32) for i in range(2)]; ropeA2_r = [Res(), Res()]
                ktmp = sbuf(st, "ktmp", [128, 512], BF16); ktmp_r = Res()
                kpst = sbuf(st, "kpst", [64, 512], BF16); kpst_r = Res()
                PT = sbuf(st, "PT", [128, 4, 128], BF16); PT_r = Res()
                C32 = sbuf(st, "C32", [128, 4, 257], F32); Cbf = sbuf(st, "Cbf", [128, 4, 257], BF16)
                C32_r = Res(small=True); Cbf_r = Res(small=True)
                t1 = sbuf(st, "t1", [128, 257], F32); t1_r = Res(small=True)
                Nsb = [sbuf(st, "Nsb%d" % i, [128, 4, 257], F32) for i in range(2)]; Nsb_r = [Res(small=True), Res(small=True)]
                scs_ = [sbuf(st, "scs_%d" % i, [128, 8, 4], F32) for i in range(2)]; scs_r = [Res(small=True), Res(small=True)]
                yraw = [sbuf(st, "yraw%d" % i, [128, D], F32) for i in range(2)]; yraw_r = [Res(), Res()]
                hnB = None
                P.op("pool", lambda e: e.memset(hal[:], 0.0), writes=[hal_r])
                P.op("pool", lambda e: e.memset(vst[:], 1.0), writes=[vst_r])
                P.op("pool", lambda e: e.memset(C32[:], 0.0), writes=[C32_r])
                P.op("pool", lambda e: e.memset(Cbf[:], 0.0), writes=[Cbf_r])
                xav = x_all.rearrange("(t b p) d -> t p b d", b=4, p=128)
                def make_tile(ti):
                    sl = ti % 2
                    c0 = ti * 512
                    s_ = ti % 2
                    qT, qT_r = qT2[s_], qT2_r[s_]
                    kT, kT_r = kT2[s_], kT2_r[s_]
                    Ktok, Ktok_r = Ktok2[s_], Ktok2_r[s_]
                    Vt, Vt_r = Vt2[s_], Vt2_r[s_]
                    ev, eb, eL, ebi = ev2[s_], eb2[s_], eL2[s_], ebi2[s_]
                    gate_r = gate2_r[s_]
                    ropeA, ropeA_r = ropeA2[s_], ropeA2_r[s_]
                    def sec_loads():
                        P.dma("sp", "ropeA%d" % s_, (lambda c0, ropeA: lambda e: e.dma_start(out=ropeA[:], in_=rope_all[:, c0:c0 + 512]))(c0, ropeA), writes=[ropeA_r])
                        norm_transpose(xav[ti], 4, xt[sl], xt_r[sl], xn, xn_r, hT, hT_r, AM, BM, xn[:, 3, :], xn_r, stat, stat_r, "xt%d" % sl, part="dma")
                    def sec_nt_pre():
                        norm_transpose(xav[ti], 4, xt[sl], xt_r[sl], xn, xn_r, hT, hT_r, AM, BM, xn[:, 3, :], xn_r, stat, stat_r, "xt%d" % sl, part="pre_nodma")
                    def sec_nt_tr():
                        norm_transpose(xav[ti], 4, xt[sl], xt_r[sl], xn, xn_r, hT, hT_r, AM, BM, xn[:, 3, :], xn_r, stat, stat_r, "xt%d" % sl, part="tr")
                    def sec_gates():
                        ps, pr = bank()
                        for b in range(4):
                            for k in range(8):
                                P.op("pe", (lambda ps, b, k: lambda e: e.matmul(ps[:, b * 8:(b + 1) * 8], lhsT=hT[:, k, b * 128:(b + 1) * 128],
                                                                                rhs=WA[:, k, A_G:A_G + 8], start=(k == 0), stop=(k == 7)))(ps, b, k),
                                     reads=[hT_r, r_WA], writes=[pr])
                        P.op("dve", (lambda ps: lambda e: e.tensor_tensor(out=g[:], in0=ps[:, 0:32].rearrange("p (b c) -> p b c", c=8),
                                                                          in1=gb_s[:].rearrange("p (b c) -> p b c", c=8), op=ALU.add))(ps),
                             reads=[pr, r_small], writes=[g_r])
                        P.op("act", lambda e: e.activation(out=spv[:], in_=g[:, :, 4:8], func=AF.Exp, scale=-1.0), reads=[g_r], writes=[sp_r])
                        P.op("dve", lambda e: e.tensor_scalar(out=wv[:], in0=spv[:], scalar1=1.0, scalar2=None, op0=ALU.add), reads=[sp_r], writes=[wv_r])
                        P.op("act", lambda e: e.activation(out=spv[:], in_=wv[:], func=AF.Ln), reads=[wv_r], writes=[sp_r])
                        for _it in range(3):
                            P.op("act", lambda e: e.activation(out=ez[:], in_=spv[:], func=AF.Exp, scale=-1.0), reads=[sp_r], writes=[ez_r])
                            P.op("dve", lambda e: e.tensor_tensor(out=ez[:], in0=ez[:], in1=wv[:], op=ALU.mult), reads=[ez_r, wv_r], writes=[ez_r])
                            P.op("dve", lambda e: e.scalar_tensor_tensor(out=spv[:], in0=spv[:], scalar=-1.0, in1=ez[:], op0=ALU.add, op1=ALU.add),
                                 reads=[sp_r, ez_r], writes=[sp_r])
                        psb, prb = bank()
                        P.op("pe", (lambda psb: lambda e: e.matmul(psb[:, 0:16], lhsT=tri_f[:], rhs=spv[:].rearrange("p b c -> p (b c)"),
                                                                   start=True, stop=True))(psb), reads=[sp_r, r_const], writes=[prb])
                        P.op("pe", (lambda psb: lambda e: e.matmul(psb[:, 16:32], lhsT=ones_f[:], rhs=spv[:].rearrange("p b c -> p (b c)"),
                                                                   start=True, stop=True))(psb), reads=[sp_r, r_const], writes=[prb])
                        P.op("dve", (lambda psb: lambda e: e.tensor_tensor(out=ev[:], in0=g[:, :, 0:4],
                                                                           in1=psb[:, 0:16].rearrange("p (b c) -> p b c", c=4), op=ALU.add))(psb),
                             reads=[prb, g_r], writes=[gate_r])
                        P.op("act", lambda e: e.activation(out=ev[:], in_=ev[:], func=AF.Exp), reads=[gate_r], writes=[gate_r])
                        P.op("act", (lambda psb: lambda e: e.activation(out=eb[:], in_=psb[:, 0:16].rearrange("p (b c) -> p b c", c=4),
                                                                        func=AF.Exp, scale=-1.0))(psb), reads=[prb], writes=[gate_r])
                        P.op("act", (lambda psb: lambda e: e.activation(out=ebi[:], in_=psb[:, 0:16].rearrange("p (b c) -> p b c", c=4),
                                                                        func=AF.Exp))(psb), reads=[prb], writes=[gate_r])
                        P.op("act", (lambda psb: lambda e: e.activation(out=eL[:], in_=psb[:, 16:32].rearrange("p (b c) -> p b c", c=4),
                                                                        func=AF.Exp, scale=-1.0))(psb), reads=[prb], writes=[gate_r])
                    def sec_qk(nsel):
                        for n in nsel:
                            ps, pr = bank()
                            for k in range(8):
                                P.op("pe", (lambda ps, n, k: lambda e: e.matmul(ps[:], lhsT=WA[:, k, A_QK + n * 128:A_QK + (n + 1) * 128],
                                                                                rhs=hT[:, k, :], start=(k == 0), stop=(k == 7)))(ps, n, k),
                                     reads=[hT_r, r_WA], writes=[pr])
                            P.op("dve", (lambda n: lambda e: e.tensor_copy(out=xqk[:, n % 2, 0:3], in_=hal[:, n, :]))(n),
                                 reads=[hal_r], writes=[xqk_r[n % 2]])
                            P.op("act", (lambda ps, n: lambda e: e.activation(out=xqk[:, n % 2, 3:515], in_=ps[:], func=AF.Copy))(ps, n),
                                 reads=[pr], writes=[xqk_r[n % 2]])
                            P.op("dve", (lambda n: lambda e: e.tensor_scalar(out=acc[:], in0=xqk[:, n % 2, 3:515], scalar1=cw_s[:, n, 3:4],
                                                                             scalar2=None, op0=ALU.mult))(n),
                                 reads=[xqk_r[n % 2], r_small], writes=[acc_r])
                            for jj in range(3):
                                P.op("dve", (lambda n, jj: lambda e: e.scalar_tensor_tensor(out=acc[:], in0=xqk[:, n % 2, jj:jj + 512],
                                                                                             scalar=cw_s[:, n, jj:jj + 1], in1=acc[:],
                                                                                             op0=ALU.mult, op1=ALU.add))(n, jj),
                                     reads=[xqk_r[n % 2], acc_r], writes=[acc_r])
                            P.op("dve", (lambda n: lambda e: e.tensor_copy(out=hal[:, n, :], in_=xqk[:, n % 2, 512:515]))(n),
                                 reads=[xqk_r[n % 2]], writes=[hal_r])
                            if n < 4:
                                P.op("act", (lambda n: lambda e: e.activation(out=qT[:, n, :], in_=acc[:], func=AF.Silu, bias=cb_s[:, n:n + 1]))(n),
                                     reads=[acc_r, r_small], writes=[qT_r])
                            else:
                                P.op("act", (lambda n: lambda e: e.activation(out=kT[:, n - 4, :], in_=acc[:], func=AF.Silu, bias=cb_s[:, n:n + 1]))(n),
                                     reads=[acc_r, r_small], writes=[kT_r])
                    def sec_ktr(bsel):
                        for b in bsel:
                            ps, pr = bank()
                            for h in range(4):
                                P.op("pe", (lambda ps, b, h: lambda e: e.matmul(ps[:, h * 128:(h + 1) * 128], lhsT=kT[:, h, b * 128:(b + 1) * 128],
                                                                                rhs=ident_bf[:], start=True, stop=True))(ps, b, h),
                                     reads=[kT_r, r_const], writes=[pr])
                            P.op("act", (lambda ps, b: lambda e: e.activation(out=Ktok[:, b, :], in_=ps[:], func=AF.Copy))(ps, b),
                                 reads=[pr], writes=[Ktok_r])
                    def sec_v(bsel):
                        for b in bsel:
                            for half in range(2):
                                ps, pr = bank()
                                for k in range(8):
                                    P.op("pe", (lambda ps, b, half, k: lambda e: e.matmul(ps[:], lhsT=hT[:, k, b * 128:(b + 1) * 128],
                                                                                          rhs=WA[:, k, A_V + half * 512:A_V + (half + 1) * 512],
                                                                                          start=(k == 0), stop=(k == 7)))(ps, b, half, k),
                                         reads=[hT_r, r_WA], writes=[pr])
                                for hh in range(2):
                                    h = half * 2 + hh
                                    eng = "act" if hh == 0 else "dve"
                                    if eng == "act":
                                        P.op("act", (lambda ps, b, h, hh: lambda e: e.activation(out=Vt[:, b, h, 0:256], in_=ps[:, hh * 256:(hh + 1) * 256],
                                                                                                 func=AF.Copy, scale=ev[:, b, h:h + 1]))(ps, b, h, hh),
                                             reads=[pr, gate_r], writes=[Vt_r])
                                    else:
                                        P.op("dve", (lambda ps, b, h, hh: lambda e: e.tensor_scalar(out=Vt[:, b, h, 0:256], in0=ps[:, hh * 256:(hh + 1) * 256],
                                                                                                    scalar1=ev[:, b, h:h + 1], scalar2=None, op0=ALU.mult))(ps, b, h, hh),
                                             reads=[pr, gate_r], writes=[Vt_r])
                        if 3 in bsel:
                            P.op("dve", lambda e: e.tensor_copy(out=Vt[:, :, :, 256], in_=ev[:]), reads=[gate_r], writes=[Vt_r])
                    def sec_ckv():
                        pss, prs = [], []
                        for j in range(2):
                            ps, pr = bank()
                            for k in range(8):
                                P.op("pe", (lambda ps, j, k: lambda e: e.matmul(ps[:], lhsT=WA[:, k, A_CKV + j * 128:A_CKV + (j + 1) * 128],
                                                                                rhs=hT[:, k, :], start=(k == 0), stop=(k == 7)))(ps, j, k),
                                     reads=[hT_r, r_WA], writes=[pr])
                            pss.append(ps); prs.append(pr)
                        rms_feature_major(pss, prs, 2, kvn_s, ckvn, ckvn_r, sq, sq_r, rstd, rstd_r, 256)
                    def sec_kup(hsel):
                        for h in hsel:
                            ps, pr = bank()
                            for j in range(2):
                                P.op("pe", (lambda ps, h, j: lambda e: e.matmul(ps[:], lhsT=WUKV[:, j, h * 128:(h + 1) * 128], rhs=ckvn[:, j, :],
                                                                                start=(j == 0), stop=(j == 1)))(ps, h, j),
                                     reads=[ckvn_r, r_WUKV], writes=[pr])
                            if h % 2 == 0:
                                P.op("act", (lambda ps, h: lambda e: e.activation(out=kst[:, h, :], in_=ps[:], func=AF.Copy))(ps, h), reads=[pr], writes=[kst_r])
                            else:
                                P.op("dve", (lambda ps, h: lambda e: e.tensor_copy(out=kst[:, h, :], in_=ps[:]))(ps, h), reads=[pr], writes=[kst_r])
                        if 7 in hsel:
                            P.dma("sp", "kst", (lambda c0: lambda e: e.dma_start(out=KT[:, :, c0:c0 + 512].rearrange("h d t -> d h t"), in_=kst[:]))(c0),
                                  reads=[kst_r])
                    def sec_vup(bsel):
                        for b in bsel:
                            for half in range(2):
                                ps, pr = bank()
                                for j in range(2):
                                    P.op("pe", (lambda ps, b, half, j: lambda e: e.matmul(ps[:], lhsT=ckvn[:, j, b * 128:(b + 1) * 128],
                                                                                          rhs=WUKV[:, j, 1024 + half * 512:1024 + (half + 1) * 512],
                                                                                          start=(j == 0), stop=(j == 1)))(ps, b, half, j),
                                         reads=[ckvn_r, r_WUKV], writes=[pr])
                                if half == 0:
                                    P.op("act", (lambda ps, b, half: lambda e: e.activation(out=vst[:, b, 4 * half:4 * half + 4, 0:128],
                                                                                            in_=ps[:].rearrange("p (h v) -> p h v", v=128), func=AF.Copy))(ps, b, half),
                                         reads=[pr], writes=[vst_r])
                                else:
                                    P.op("dve", (lambda ps, b, half: lambda e: e.tensor_copy(out=vst[:, b, 4 * half:4 * half + 4, 0:128],
                                                                                             in_=ps[:].rearrange("p (h v) -> p h v", v=128)))(ps, b, half),
                                         reads=[pr], writes=[vst_r])
                        if 3 in bsel:
                            P.dma("sp", "vst", (lambda ti: lambda e: e.dma_start(out=VT[ti * 4:ti * 4 + 4, :, :].rearrange("b p f -> p b f"),
                                                                                   in_=vst[:].rearrange("p b h f -> p b (h f)")))(ti), reads=[vst_r])
                    def sec_kpe():
                        ps, pr = bank()
                        for k in range(8):
                            P.op("pe", (lambda ps, k: lambda e: e.matmul(ps[:], lhsT=WA[:, k, A_KPE:A_KPE + 128], rhs=hT[:, k, :],
                                                                         start=(k == 0), stop=(k == 7)))(ps, k), reads=[hT_r, r_WA], writes=[pr])
                        P.op("dve", (lambda ps: lambda e: e.tensor_tensor(out=ktmp[:], in0=ps[:], in1=ropeA[:], op=ALU.mult))(ps),
                             reads=[pr, ropeA_r], writes=[ktmp_r])
                        ps, pr = bank()
                        P.op("pe", (lambda ps: lambda e: e.matmul(ps[0:64, :], lhsT=sel[:], rhs=ktmp[:], start=True, stop=True))(ps),
                             reads=[ktmp_r, r_const], writes=[pr])
                        P.op("act", (lambda ps: lambda e: e.activation(out=kpst[:], in_=ps[0:64, :], func=AF.Copy))(ps), reads=[pr], writes=[kpst_r])
                        P.dma("sp", "kpst", (lambda c0: lambda e: e.dma_start(out=KPE[:, c0:c0 + 512], in_=kpst[:]))(c0), reads=[kpst_r])
                    def sec_dbg():
                        if debug and ti == 0:
                            P.dma("sp", "d1", lambda e: e.dma_start(out=D_hT[:, :, :], in_=hT[:]), reads=[hT_r])
                            P.dma("sp", "d2", lambda e: e.dma_start(out=D_qT[:, :, :], in_=qT[:]), reads=[qT_r])
                            P.dma("sp", "d3", lambda e: e.dma_start(out=D_kT[:, :, :], in_=kT[:]), reads=[kT_r])
                            P.dma("sp", "d4", lambda e: e.dma_start(out=D_g[:, :, :], in_=g[:]), reads=[g_r])
                            P.dma("sp", "d5", lambda e: e.dma_start(out=D_ev[:, 0, :], in_=ev[:].rearrange("p b c -> p (b c)")), reads=[gate_r])
                            P.dma("sp", "d5", lambda e: e.dma_start(out=D_ev[:, 1, :], in_=eb[:].rearrange("p b c -> p (b c)")), reads=[gate_r])
                            P.dma("sp", "d5", lambda e: e.dma_start(out=D_ev[:, 2, :], in_=eL[:].rearrange("p b c -> p (b c)")), reads=[gate_r])
                            P.dma("sp", "d6", lambda e: e.dma_start(out=D_ckvn[:, :, :], in_=ckvn[:]), reads=[ckvn_r])
                            P.dma("sp", "d7", lambda e: e.dma_start(out=D_ktmp[:, :], in_=ktmp[:]), reads=[ktmp_r])
                            P.dma("sp", "d8", lambda e: e.dma_start(out=D_Vt[:, :, :, :], in_=Vt[:]), reads=[Vt_r])
                    pscs = {}
                    def mc1(b):
                        gb = ti * 4 + b
                        par = gb % 2
                        lb = gb // 2
                        yr, yr_r = yraw[lb % 2], yraw_r[lb % 2]
                        bs = slice(b * 128, (b + 1) * 128)
                        nsl = gb % 2
                        Nb, Nb_r, sc, sc_r = Nsb[nsl], Nsb_r[nsl], scs_[nsl], scs_r[nsl]
                        ps, pr = bank(); pscs[("S", b)] = (ps, pr)
                        for h in range(4):
                            P.op("pe", (lambda ps, h, bs: lambda e: e.matmul(ps[:, h * 128:(h + 1) * 128], lhsT=kT[:, h, bs], rhs=qT[:, h, bs],
                                                                             start=True, stop=True))(ps, h, bs), reads=[kT_r, qT_r], writes=[pr])
                    def mc2(b):
                        gb = ti * 4 + b
                        par = gb % 2
                        lb = gb // 2
                        yr, yr_r = yraw[lb % 2], yraw_r[lb % 2]
                        bs = slice(b * 128, (b + 1) * 128)
                        nsl = gb % 2
                        Nb, Nb_r, sc, sc_r = Nsb[nsl], Nsb_r[nsl], scs_[nsl], scs_r[nsl]
                        ps, pr = pscs[("S", b)]
                        P.op("dve", (lambda ps: lambda e: e.tensor_tensor(out=PT[:], in0=ps[:].rearrange("p (h t) -> p h t", t=128),
                                                                          in1=bc_mid(maskS[:], 4), op=ALU.mult))(ps),
                             reads=[pr, r_const], writes=[PT_r])
                    def mc3(b):
                        gb = ti * 4 + b
                        par = gb % 2
                        lb = gb // 2
                        yr, yr_r = yraw[lb % 2], yraw_r[lb % 2]
                        bs = slice(b * 128, (b + 1) * 128)
                        nsl = gb % 2
                        Nb, Nb_r, sc, sc_r = Nsb[nsl], Nsb_r[nsl], scs_[nsl], scs_r[nsl]
                        psc = []; pscs[("C", b)] = psc
                        for h in range(4):
                            P.op("pe", (lambda h, b: lambda e: e.matmul(psN[:, h, 0:257], lhsT=PT[:, h, :], rhs=Vt[:, b, h, :], start=True, stop=False))(h, b),
                                 reads=[PT_r, Vt_r], writes=[psN_r[h]])
                            P.op("pe", (lambda h, bs: lambda e: e.matmul(psN[:, h, 0:257], lhsT=qT[:, h, bs], rhs=Cbf[:, h, :], start=False, stop=True))(h, bs),
                                 reads=[qT_r, Cbf_r], writes=[psN_r[h]])
                        for h in range(4):
                            ps, pr = bank(); psc.append((ps, pr))
                            P.op("pe", (lambda ps, h, b: lambda e: e.matmul(ps[:, 0:257], lhsT=Ktok[:, b, h * 128:(h + 1) * 128], rhs=Vt[:, b, h, :],
                                                                            start=True, stop=True))(ps, h, b), reads=[Ktok_r, Vt_r], writes=[pr])
                    def mc4(b):
                        gb = ti * 4 + b
                        par = gb % 2
                        lb = gb // 2
                        yr, yr_r = yraw[lb % 2], yraw_r[lb % 2]
                        bs = slice(b * 128, (b + 1) * 128)
                        nsl = gb % 2
                        Nb, Nb_r, sc, sc_r = Nsb[nsl], Nsb_r[nsl], scs_[nsl], scs_r[nsl]
                        psc = pscs[("C", b)]
                        for h in range(4):
                            ps, pr = psc[h]
                            P.op("dve", (lambda ps, h, b: lambda e: e.tensor_scalar(out=t1[:], in0=ps[:, 0:257], scalar1=eL[:, b, h:h + 1], scalar2=None,
                                                                                    op0=ALU.mult))(ps, h, b), reads=[pr, gate_r], writes=[t1_r])
                            P.op("dve", (lambda h, b: lambda e: e.scalar_tensor_tensor(out=C32[:, h, :], in0=C32[:, h, :], scalar=eL[:, b, h:h + 1],
                                                                                       in1=t1[:], op0=ALU.mult, op1=ALU.add))(h, b),
                                 reads=[t1_r, gate_r, C32_r], writes=[C32_r])
                            P.op("act", (lambda h: lambda e: e.activation(out=Cbf[:, h, :], in_=C32[:, h, :], func=AF.Copy, scale=128.0 ** -0.5))(h),
                                 reads=[C32_r], writes=[Cbf_r])
                    def mc5(b):
                        gb = ti * 4 + b
                        par = gb % 2
                        lb = gb // 2
                        yr, yr_r = yraw[lb % 2], yraw_r[lb % 2]
                        bs = slice(b * 128, (b + 1) * 128)
                        nsl = gb % 2
                        Nb, Nb_r, sc, sc_r = Nsb[nsl], Nsb_r[nsl], scs_[nsl], scs_r[nsl]
                        for h in range(4):
                            P.op("act", (lambda h, Nb: lambda e: e.activation(out=Nb[:, h, :], in_=psN[:, h, 0:257], func=AF.Copy))(h, Nb),
                                 reads=[psN_r[h]], writes=[Nb_r])
                    def mo1(b):
                        gb = ti * 4 + b
                        par = gb % 2
                        lb = gb // 2
                        yr, yr_r = yraw[lb % 2], yraw_r[lb % 2]
                        bs = slice(b * 128, (b + 1) * 128)
                        nsl = gb % 2
                        Nb, Nb_r, sc, sc_r = Nsb[nsl], Nsb_r[nsl], scs_[nsl], scs_r[nsl]
                        P.op("dve", (lambda sc: lambda e: e.memset(sc[:, 0, :], 0.0))(sc), writes=[sc_r])
                        for h in range(4):
                            P.op("act", (lambda h, Nb, sc: lambda e: e.activation(out=junkM[:, 0:256], in_=Nb[:, h, 0:256], func=AF.Square,
                                                                                  accum_out=sc[:, 0, h:h + 1]))(h, Nb, sc), reads=[Nb_r], writes=[junkM_r, sc_r])
                    def mo2(b):
                        gb = ti * 4 + b
                        par = gb % 2
                        lb = gb // 2
                        yr, yr_r = yraw[lb % 2], yraw_r[lb % 2]
                        bs = slice(b * 128, (b + 1) * 128)
                        nsl = gb % 2
                        Nb, Nb_r, sc, sc_r = Nsb[nsl], Nsb_r[nsl], scs_[nsl], scs_r[nsl]
                        P.op("dve", (lambda Nb, sc: lambda e: e.scalar_tensor_tensor(out=sc[:, 1, :], in0=Nb[:, :, 256], scalar=-1.0, in1=Nb[:, :, 256],
                                                                                     op0=ALU.mult, op1=ALU.max))(Nb, sc), reads=[Nb_r], writes=[sc_r])
                        P.op("dve", (lambda b, sc: lambda e: e.tensor_tensor(out=sc[:, 1, :], in0=sc[:, 1, :], in1=ebi[:, b, :], op=ALU.max))(b, sc),
                             reads=[sc_r, gate_r], writes=[sc_r])
                        P.op("dve", (lambda sc: lambda e: e.tensor_tensor(out=sc[:, 2, :], in0=sc[:, 1, :], in1=sc[:, 1, :], op=ALU.mult))(sc),
                             reads=[sc_r], writes=[sc_r])
                        P.op("dve", (lambda sc: lambda e: e.scalar_tensor_tensor(out=sc[:, 3, :], in0=sc[:, 2, :], scalar=256.0 * EPS, in1=sc[:, 0, :],
                                                                                 op0=ALU.mult, op1=ALU.add))(sc), reads=[sc_r], writes=[sc_r])
                    def mo3(b):
                        gb = ti * 4 + b
                        par = gb % 2
                        lb = gb // 2
                        yr, yr_r = yraw[lb % 2], yraw_r[lb % 2]
                        bs = slice(b * 128, (b + 1) * 128)
                        nsl = gb % 2
                        Nb, Nb_r, sc, sc_r = Nsb[nsl], Nsb_r[nsl], scs_[nsl], scs_r[nsl]
                        P.op("act", (lambda sc: lambda e: e.activation(out=sc[:, 4, :], in_=sc[:, 3, :], func=AF.Sqrt, scale=1.0 / 256))(sc),
                             reads=[sc_r], writes=[sc_r])
                    def mo4(b):
                        gb = ti * 4 + b
                        par = gb % 2
                        lb = gb // 2
                        yr, yr_r = yraw[lb % 2], yraw_r[lb % 2]
                        bs = slice(b * 128, (b + 1) * 128)
                        nsl = gb % 2
                        Nb, Nb_r, sc, sc_r = Nsb[nsl], Nsb_r[nsl], scs_[nsl], scs_r[nsl]
                        P.op("dve", (lambda sc: lambda e: e.reciprocal(out=sc[:, 5, :], in_=sc[:, 4, :]))(sc), reads=[sc_r], writes=[sc_r])
                        P.op("dve", (lambda par, sc: lambda e: e.tensor_scalar(out=sc[:, 5, :], in0=sc[:, 5, :], scalar1=selm_s[:, par:par + 1],
                                                                               scalar2=None, op0=ALU.mult))(par, sc), reads=[sc_r, r_am], writes=[sc_r])
                    def mo5(b):
                        gb = ti * 4 + b
                        par = gb % 2
                        lb = gb // 2
                        yr, yr_r = yraw[lb % 2], yraw_r[lb % 2]
                        bs = slice(b * 128, (b + 1) * 128)
                        nsl = gb % 2
                        Nb, Nb_r, sc, sc_r = Nsb[nsl], Nsb_r[nsl], scs_[nsl], scs_r[nsl]
                        for h in range(4):
                            if par == 0:
                                P.op("act", (lambda h, yr, Nb, sc: lambda e: e.activation(out=yr[:, h * 256:(h + 1) * 256], in_=Nb[:, h, 0:256], func=AF.Copy,
                                                                                          scale=sc[:, 5, h:h + 1]))(h, yr, Nb, sc), reads=[Nb_r, sc_r], writes=[yr_r])
                            else:
                                P.op("dve", (lambda h, yr, Nb, sc: lambda e: e.scalar_tensor_tensor(out=yr[:, h * 256:(h + 1) * 256], in0=Nb[:, h, 0:256],
                                                                                                    scalar=sc[:, 5, h:h + 1], in1=yr[:, h * 256:(h + 1) * 256],
                                                                                                    op0=ALU.mult, op1=ALU.add))(h, yr, Nb, sc),
                                     reads=[Nb_r, sc_r, yr_r], writes=[yr_r])
                        if par == 1:
                            P.dma("sp", "yraw%d" % (lb % 2), (lambda lb, yr: lambda e: e.dma_start(out=YA[lb, :, :], in_=yr[:]))(lb, yr), reads=[yr_r])

                    xu = [sec_nt_pre, sec_nt_tr, sec_gates]
                    xu += [(lambda n: lambda: sec_qk([n]))(n) for n in range(8)]
                    xu += [(lambda b: lambda: sec_ktr([b]))(b) for b in range(4)]
                    xu += [(lambda b: lambda: sec_v([b]))(b) for b in range(4)]
                    xu += [sec_ckv]
                    xu += [(lambda h: lambda: sec_kup([h, h + 1]))(h) for h in range(0, 8, 2)]
                    xu += [(lambda b: lambda: sec_vup([b]))(b) for b in range(4)]
                    xu += [sec_kpe, sec_dbg]
                    mu = []
                    def mc12(b):
                        mc1(b); mc2(b)
                    def mc34(b):
                        mc3(b); mc4(b)
                    for b in range(4):
                        seq = [(mc12, b), (mo1, b - 1), (mc34, b), (mo2, b - 1), (mo3, b - 1), (mc5, b), (mo4, b - 1), (mo5, b - 1)]
                        for f, bb in seq:
                            if bb >= 0:
                                mu.append((lambda f, bb: lambda: f(bb))(f, bb))
                    for j in range(5):
                        mu.append((lambda f: lambda: f(3))([mo1, mo2, mo3, mo4, mo5][j]))
                    return xu, mu, sec_loads

                tiles = [make_tile(ti) for ti in range(8)]
                for ti in range(8):
                    if ti + 1 < 8:
                        tiles[ti][0].insert(2, tiles[ti + 1][2])
                tiles[0][2]()

                def run_merged(a, b):
                    ia = ib = 0
                    while ia < len(a) or ib < len(b):
                        fa = 0.35 + 0.65 * (ia + 1) / len(a) if ia < len(a) else 2.0
                        fb = (ib + 1) / len(b) if ib < len(b) else 2.0
                        if fa <= fb:
                            a[ia](); ia += 1
                        else:
                            b[ib](); ib += 1

                for u in tiles[0][0]:
                    u()
                for ti in range(8):
                    if ti + 1 < 8:
                        run_merged(tiles[ti][1], tiles[ti + 1][0])
                    else:
                        for u in tiles[ti][1]:
                            u()

        _phase1()
        P.fence_all()
        r_scr = Res("scratch")
        scr_deps = []
        for key in ("kst", "vst", "kpst", "yraw0", "yraw1"):
            rr = Res(key)
            rr.writer = ("dma:" + key, P.count["dma:" + key])
            scr_deps.append(rr)

        with contextlib.ExitStack() as st2:
            YB = sbuf(st2, "YB", [128, 16, D], BF16); YB_r = [Res() for _ in range(16)]
            with contextlib.ExitStack() as st:
                QT = sbuf(st, "QT", [128, 8, NOWN], BF16); QT_r = Res()
                QPE = sbuf(st, "QPE", [128, 4, NOWN], BF16); QPE_r = Res()
                def _phase2():
                    with contextlib.ExitStack() as sa:
                        WCQ = sbuf(sa, "WCQ", [128, 8, 384], BF16); r_WB = Res()
                        WUQ = sbuf(sa, "WUQ", [128, 3, 2048], BF16); r_WUQ = Res()
                        P.dma("pool", "WCQ", lambda e: e.dma_start(out=WCQ[:], in_=wB.rearrange("(k p) n -> p k n", p=128)[:, :, B_CQ:B_CQ + 384]), writes=[r_WB])
                        P.dma("pool", "WUQ", lambda e: e.dma_start(out=WUQ[:], in_=w_uq2.rearrange("(k p) n -> p k n", p=128)), writes=[r_WUQ])
                        xt = [sbuf(sa, "xo%d" % i, [128, 4, D], F32) for i in range(2)]; xt_r = [Res(), Res()]
                        xn = sbuf(sa, "xn2", [128, 4, D], BF16); xn_r = Res()
                        hT = sbuf(sa, "hT2", [128, 8, 512], BF16); hT_r = Res()
                        junk = sbuf(sa, "junk2", [128, D], BF16); junk_r = Res()
                        stat = sbuf(sa, "stat2", [128, 12], F32); stat_r = Res(small=True)
                        sq = sbuf(sa, "sq2", [128, 3, 512], BF16); sq_r = Res()
                        rstd = sbuf(sa, "rstd2", [128, 512], F32); rstd_r = Res()
                        cqn = sbuf(sa, "cqn", [128, 3, 512], BF16); cqn_r = Res()
                        ropeO = sbuf(sa, "ropeO", [128, 512], F32); ropeO_r = Res()
                        qtmp = sbuf(sa, "qtmp", [128, 512], BF16); qtmp_r = Res()
                        xov = x_own.rearrange("(t b p) d -> t p b d", b=4, p=128)
                        def p2a_nt(ti_, part):
                            norm_transpose(xov[ti_], 4, xt[ti_ % 2], xt_r[ti_ % 2], xn, xn_r, hT, hT_r, AM, BM, junk, junk_r, stat, stat_r, "xo%d" % (ti_ % 2), part=part)
                        p2a_nt(0, "all")
                        for ti in range(4):
                            sl = ti % 2
                            c0 = ti * 512
                            P.dma("sp", "ropeO", (lambda c0: lambda e: e.dma_start(out=ropeO[:], in_=rope_own[:, c0:c0 + 512]))(c0), writes=[ropeO_r])
                            pss, prs = [], []
                            for j in range(3):
                                ps, pr = bank()
                                for k in range(8):
                                    P.op("pe", (lambda ps, j, k: lambda e: e.matmul(ps[:], lhsT=WCQ[:, k, j * 128:(j + 1) * 128],
                                                                                    rhs=hT[:, k, :], start=(k == 0), stop=(k == 7)))(ps, j, k),
                                         reads=[hT_r, r_WB], writes=[pr])
                                pss.append(ps); prs.append(pr)
                            rms_feature_major(pss, prs, 3, qn_s, cqn, cqn_r, sq, sq_r, rstd, rstd_r, 384)
                            if ti + 1 < 4:
                                p2a_nt(ti + 1, "pre")
                            for h in range(8):
                                if h == 4 and ti + 1 < 4:
                                    p2a_nt(ti + 1, "tr")
                                ps, pr = bank()
                                for j in range(3):
                                    P.op("pe", (lambda ps, h, j: lambda e: e.matmul(ps[:], lhsT=WUQ[:, j, h * 128:(h + 1) * 128], rhs=cqn[:, j, :],
                                                                                    start=(j == 0), stop=(j == 2)))(ps, h, j),
                                         reads=[cqn_r, r_WUQ], writes=[pr])
                                P.op("act", (lambda ps, h, c0: lambda e: e.activation(out=QT[:, h, c0:c0 + 512], in_=ps[:], func=AF.Copy))(ps, h, c0),
                                     reads=[pr], writes=[QT_r])
                                ps, pr = bank()
                                for j in range(3):
                                    P.op("pe", (lambda ps, h, j: lambda e: e.matmul(ps[:], lhsT=WUQ[:, j, 1024 + h * 128:1024 + (h + 1) * 128], rhs=cqn[:, j, :],
                                                                                    start=(j == 0), stop=(j == 2)))(ps, h, j),
                                         reads=[cqn_r, r_WUQ], writes=[pr])
                                P.op("dve", (lambda ps: lambda e: e.tensor_tensor(out=qtmp[:], in0=ps[:], in1=ropeO[:], op=ALU.mult))(ps),
                                     reads=[pr, ropeO_r], writes=[qtmp_r])
                                ps, pr = bank()
                                P.op("pe", (lambda ps: lambda e: e.matmul(ps[:], lhsT=sel128[:], rhs=qtmp[:], start=True, stop=True))(ps),
                                     reads=[qtmp_r, r_const], writes=[pr])
                                po = (h % 2) * 64
                                P.op("act", (lambda ps, h, c0, po: lambda e: e.activation(out=QPE[po:po + 64, h // 2, c0:c0 + 512], in_=ps[po:po + 64, :], func=AF.Copy))(ps, h, c0, po),
                                     reads=[pr], writes=[QPE_r])
                _phase2()
                P.fence_all()
                def _phase3():
                    with contextlib.ExitStack() as sb_:
                        KPEs = sbuf(sb_, "KPEs", [128, 2, S], BF16); KPEs_r = Res()
                        KTh = [sbuf(sb_, "KTh%d" % i, [128, S], BF16) for i in range(2)]; KTh_r = [Res(), Res()]
                        VTh = [sbuf(sb_, "VTh%d" % i, [128, 32, 128], BF16) for i in range(2)]; VTh_r = [Res(), Res()]
                        NPT = 6
                        LA = 3
                        PTa = [sbuf(sb_, "PTa%d" % i, [128, 512], BF16) for i in range(NPT)]; PTa_r = [Res() for _ in range(NPT)]
                        rinvF = sbuf(sb_, "rinvF", [128, 512], F32); rinvF_r = Res()
                        OTn = sbuf(sb_, "OTn", [128, 512], BF16); OTn_r = Res()
                        P.op("pool", lambda e: e.memset(KPEs[:], 0.0), writes=[KPEs_r])
                        P.dma("sp", "KPEs", lambda e: e.dma_start(out=KPEs[0:64, 0, :], in_=KPE[:, :]), reads=scr_deps, writes=[KPEs_r])
                        P.dma("sp", "KPEs", lambda e: e.dma_start(out=KPEs[64:128, 1, :], in_=KPE[:, :]), reads=scr_deps, writes=[KPEs_r])
                        scale = 192.0 ** -0.5
                        its = []
                        gctr = 0
                        for h in range(8):
                            for G in range(4):
                                oO = 2 * (gctr % 2)
                                gctr += 1
                                nkb = 8 * G + 8
                                for kb in range(nkb):
                                    its.append(dict(h=h, G=G, kb=kb, nkb=nkb, oO=oO, oR=oO + 1, idx=len(its)))

                        def load_head(hh):
                            sl_ = hh % 2
                            P.dma("sp", "KTh%d" % sl_, (lambda hh, sl_: lambda e: e.dma_start(out=KTh[sl_][:], in_=KT[hh, :, :]))(hh, sl_),
                                  reads=scr_deps, writes=[KTh_r[sl_]])
                            P.dma("pool", "VTh%d" % sl_, (lambda hh, sl_: lambda e: e.dma_start(out=VTh[sl_][:],
                                                                                             in_=VT[:, :, hh * 128:(hh + 1) * 128].rearrange("b p f -> p b f")))(hh, sl_),
                                  reads=scr_deps, writes=[VTh_r[sl_]])

                        load_head(0)
                        load_head(1)

                        def emit_S(it):
                            h, G, kb = it["h"], it["G"], it["kb"]
                            sl = h % 2
                            po = (h % 2) * 64
                            if G == 1 and kb == 0 and 1 <= h < 7:
                                load_head(h + 1)
                            j0 = max(4 * G, kb // 2) - 4 * G
                            c0 = j0 * 128
                            q0 = G * 512
                            ks = slice(kb * 128, (kb + 1) * 128)
                            ps, pr = bank()
                            pt_i = it["idx"] % NPT
                            pt, pt_r = PTa[pt_i], PTa_r[pt_i]
                            it["pt"], it["pt_r"], it["c0"] = pt, pt_r, c0
                            P.op("pe", (lambda ps, c0, ks, h, q0, sl: lambda e: e.matmul(ps[:, c0:512], lhsT=KTh[sl][:, ks], rhs=QT[:, h, q0 + c0:q0 + 512],
                                                                                         start=True, stop=False))(ps, c0, ks, h, q0, sl),
                                 reads=[KTh_r[sl], QT_r], writes=[pr])
                            P.op("pe", (lambda ps, c0, ks, h, q0, po: lambda e: e.matmul(ps[:, c0:512], lhsT=KPEs[:, h % 2, ks],
                                                                                         rhs=QPE[:, h // 2, q0 + c0:q0 + 512], start=False, stop=True))(ps, c0, ks, h, q0, po),
                                 reads=[KPEs_r, QPE_r], writes=[pr])
                            P.op("act", (lambda ps, pt, c0: lambda e: e.activation(out=pt[:, c0:512], in_=ps[:, c0:512], func=AF.Exp, scale=scale))(ps, pt, c0),
                                 reads=[pr], writes=[pt_r])
                            if kb >= 8 * G:
                                P.op("dve", (lambda pt, c0, kb: lambda e: e.tensor_tensor(out=pt[:, c0:c0 + 128], in0=pt[:, c0:c0 + 128],
                                                                                          in1=amask_b[:, kb % 2, :], op=ALU.mult))(pt, c0, kb),
                                     reads=[pt_r, r_am], writes=[pt_r])

                        def emit_PV(it):
                            h, G, kb, nkb, oO, oR = it["h"], it["G"], it["kb"], it["nkb"], it["oO"], it["oR"]
                            sl = h % 2
                            pt, pt_r, c0 = it["pt"], it["pt_r"], it["c0"]
                            hc = slice(h * 128, (h + 1) * 128)
                            P.op("pe", (lambda oO, pt, c0, kb, sl, nkb: lambda e: e.matmul(psN[:, oO, c0:512], lhsT=VTh[sl][:, kb, 0:128], rhs=pt[:, c0:512],
                                                                                           start=(kb == 0), stop=(kb == nkb - 1), skip_group_check=True))(oO, pt, c0, kb, sl, nkb),
                                 reads=[pt_r, VTh_r[sl]], writes=[psN_r[oO]])
                            P.op("pe", (lambda oR, pt, c0, kb, nkb: lambda e: e.matmul(psN[:, oR, c0:512], lhsT=ones_bf[:], rhs=pt[:, c0:512],
                                                                                       start=(kb == 0), stop=(kb == nkb - 1), skip_group_check=True))(oR, pt, c0, kb, nkb),
                                 reads=[pt_r, r_const], writes=[psN_r[oR]])
                            if kb == nkb - 1:
                                P.op("dve", (lambda oR: lambda e: e.reciprocal(out=rinvF[:], in_=psN[:, oR, :]))(oR), reads=[psN_r[oR]], writes=[rinvF_r])
                                P.op("dve", (lambda oO: lambda e: e.tensor_tensor(out=OTn[:], in0=psN[:, oO, :], in1=rinvF[:], op=ALU.mult))(oO),
                                     reads=[psN_r[oO], rinvF_r], writes=[OTn_r])

                                def fin_pe(G=G, hc=hc):
                                    ps, pr = bank()
                                    for j in range(4):
                                        P.op("pe", (lambda ps, j: lambda e: e.matmul(ps[:, j * 128:(j + 1) * 128], lhsT=OTn[:, j * 128:(j + 1) * 128], rhs=ident_bf[:],
                                                                                     start=True, stop=True))(ps, j), reads=[OTn_r, r_const], writes=[pr])
                                    P.op("act", (lambda ps, G, hc: lambda e: e.activation(out=YB[:, 4 * G:4 * G + 4, hc], in_=ps[:].rearrange("p (j d) -> p j d", d=128),
                                                                                          func=AF.Copy))(ps, G, hc),
                                         reads=[pr], writes=[YB_r[4 * G + j] for j in range(4)])
                                pending.append([DEFER, fin_pe])

                        n_it = len(its)
                        DEFER = 3
                        pending = []
                        for i in range(n_it + LA):
                            if i < n_it:
                                emit_S(its[i])
                            if i - LA >= 0:
                                emit_PV(its[i - LA])
                            for pnd in list(pending):
                                pnd[0] -= 1
                                if pnd[0] <= 0:
                                    pnd[1]()
                                    pending.remove(pnd)
                        for pnd in pending:
                            pnd[1]()
                _phase3()
            P.fence_all()
            if debug:
                P.dma("sp", "dbg2", lambda e: e.dma_start(out=DBG2.rearrange("b p f -> p b f"), in_=YB[:]), reads=YB_r)
            def _phase4():
                with contextlib.ExitStack() as sc_:
                    WBg = sbuf(sc_, "WBg", [128, 6, 8, 512], BF16); r_WB = [Res() for _ in range(6)]
                    WO = sbuf(sc_, "WO", [128, 8, D], BF16); r_WO = Res()
                    for blk in (0, 2, 4, 1, 3, 5):
                        P.dma("pool", "WBg_%d" % blk, (lambda blk: lambda e: e.dma_start(out=WBg[:, blk, :, :], in_=wBg_b[blk].rearrange("p (k n) -> p k n", k=8)))(blk),
                              writes=[r_WB[blk]])
                    P.dma("pool", "WO", lambda e: e.dma_start(out=WO[:], in_=w_out.rearrange("(k p) n -> p k n", p=128)), writes=[r_WO])
                    hnB = sbuf(sc_, "hnB", [128, D], F32); hnB_r = Res()
                    P.dma("sp", "hnB", lambda e: e.dma_start(out=hnB[:], in_=bc_part(head_norm[0:1, :])), writes=[hnB_r])
                    xt = [sbuf(sc_, "xc%d" % i, [128, 2, D], F32) for i in range(3)]; xt_r = [Res(), Res(), Res()]
                    xn = sbuf(sc_, "xn5", [128, 2, D], BF16); xn_r = Res()
                    hT = sbuf(sc_, "hT5", [128, 8, 256], BF16); hT_r = Res()
                    junk = sbuf(sc_, "junk5", [128, D], BF16); junk_r = Res()
                    stat = sbuf(sc_, "stat5", [128, 12], F32); stat_r = Res(small=True)
                    ya = [sbuf(sc_, "ya%d" % i, [128, 2, D], F32) for i in range(3)]; ya_r = [Res(), Res(), Res()]
                    so2 = [sbuf(sc_, "so%d" % i, [128, 512], F32) for i in range(2)]; so2_r = [Res(), Res()]
                    sga2 = [sbuf(sc_, "sga%d" % i, [128, 512], F32) for i in range(2)]; sga2_r = [Res(), Res()]
                    sgb2 = [sbuf(sc_, "sgb%d" % i, [128, 512], F32) for i in range(2)]; sgb2_r = [Res(), Res()]
                    Ym = [sbuf(sc_, "Ym%d" % i, [128, D], BF16) for i in range(2)]; Ym_r = [Res(), Res()]
                    yT = [sbuf(sc_, "yT%d" % i, [128, 8, 128], BF16) for i in range(2)]; yT_r = [Res(), Res()]
                    yo = [sbuf(sc_, "yo%d" % i, [128, D], F32) for i in range(2)]; yo_r = [Res(), Res()]
                    st3 = sbuf(sc_, "st3", [128, 4], F32); st3_r = Res(small=True)
                    xcv = x_own.rearrange("(t b p) d -> t p b d", b=2, p=128)
                    def p2c_ld(ti):
                        sl = ti % 3
                        P.dma("sp", "ya%d" % sl, (lambda ti, sl: lambda e: e.dma_start(out=ya[sl][:], in_=YA[ti * 2:ti * 2 + 2, :, :].rearrange("b p f -> p b f")))(ti, sl),
                              reads=scr_deps, writes=[ya_r[sl]])
                        norm_transpose(xcv[ti], 2, xt[sl], xt_r[sl], xn, xn_r, hT, hT_r, AM, BM, junk, junk_r, stat, stat_r, "xc%d" % sl, part="dma")
                    def p2c_nt(ti):
                        sl = ti % 3
                        if ti + 1 < 8:
                            p2c_ld(ti + 1)
                        norm_transpose(xcv[ti], 2, xt[sl], xt_r[sl], xn, xn_r, hT, hT_r, AM, BM, junk, junk_r, stat, stat_r, "xc%d" % sl, part="pre_nodma")
                        norm_transpose(xcv[ti], 2, xt[sl], xt_r[sl], xn, xn_r, hT, hT_r, AM, BM, junk, junk_r, stat, stat_r, "xc%d" % sl, part="tr")
                    def p2c_A(lb):
                        ti = lb // 2
                        b = lb % 2
                        sl = ti % 3
                        s2 = lb % 2
                        P.op("pool", (lambda b, sl: lambda e: e.tensor_tensor(out=ya[sl][:, b, :], in0=ya[sl][:, b, :], in1=hnB[:], op=ALU.mult))(b, sl),
                             reads=[ya_r[sl], hnB_r], writes=[ya_r[sl]])
                        for half in range(2):
                            hs = slice(half * 512, (half + 1) * 512)
                            so, so_r, sga, sga_r, sgb, sgb_r = so2[half], so2_r[half], sga2[half], sga2_r[half], sgb2[half], sgb2_r[half]
                            for (goff, dst, dst_r) in ((0, so, so_r), (1024, sga, sga_r), (2048, sgb, sgb_r)):
                                ps, pr = bank()
                                for k in range(8):
                                    P.op("pe", (lambda ps, b, half, k, goff: lambda e: e.matmul(ps[:], lhsT=hT[:, k, b * 128:(b + 1) * 128],
                                                                                                rhs=WBg[:, (goff + half * 512) // 512, k, :],
                                                                                                start=(k == 0), stop=(k == 7)))(ps, b, half, k, goff),
                                         reads=[hT_r, r_WB[(goff + half * 512) // 512]], writes=[pr])
                                P.op("act", (lambda ps, dst: lambda e: e.activation(out=dst[:], in_=ps[:], func=AF.Sigmoid))(ps, dst), reads=[pr], writes=[dst_r])
                            P.op("dve", (lambda so, sga: lambda e: e.tensor_tensor(out=so[:], in0=so[:], in1=sga[:], op=ALU.mult))(so, sga),
                                 reads=[so_r, sga_r], writes=[so_r])
                            P.op("dve", (lambda b, sl, hs, so: lambda e: e.tensor_tensor(out=so[:], in0=so[:], in1=ya[sl][:, b, hs], op=ALU.mult))(b, sl, hs, so),
                                 reads=[so_r, ya_r[sl]], writes=[so_r])
                            P.op("pool", (lambda lb, hs, sgb: lambda e: e.tensor_tensor(out=sgb[:], in0=sgb[:], in1=YB[:, lb, hs], op=ALU.mult))(lb, hs, sgb),
                                 reads=[sgb_r, YB_r[lb]], writes=[sgb_r])
                            P.op("dve", (lambda s2, hs, so, sgb: lambda e: e.tensor_tensor(out=Ym[s2][:, hs], in0=so[:], in1=sgb[:], op=ALU.add))(s2, hs, so, sgb),
                                 reads=[so_r, sgb_r], writes=[Ym_r[s2]])
                    def p2c_B(lb):
                        ti = lb // 2
                        b = lb % 2
                        sl = ti % 3
                        s2 = lb % 2
                        for kk in range(2):
                            ps, pr = bank()
                            for k4 in range(4):
                                k = kk * 4 + k4
                                P.op("pe", (lambda ps, s2, k, k4: lambda e: e.matmul(ps[:, k4 * 128:(k4 + 1) * 128], lhsT=Ym[s2][:, k * 128:(k + 1) * 128],
                                                                                     rhs=ident_bf[:], start=True, stop=True))(ps, s2, k, k4),
                                     reads=[Ym_r[s2], r_const], writes=[pr])
                            P.op("act", (lambda ps, kk, s2: lambda e: e.activation(out=yT[s2][:, kk * 4:(kk + 1) * 4, :],
                                                                                   in_=ps[:].rearrange("p (k t) -> p k t", t=128), func=AF.Copy))(ps, kk, s2),
                                 reads=[pr], writes=[yT_r[s2]])
                        for half in range(2):
                            ps, pr = bank()
                            for k in range(8):
                                P.op("pe", (lambda ps, k, half, s2: lambda e: e.matmul(ps[:], lhsT=yT[s2][:, k, :], rhs=WO[:, k, half * 512:(half + 1) * 512],
                                                                                       start=(k == 0), stop=(k == 7)))(ps, k, half, s2),
                                     reads=[yT_r[s2], r_WO], writes=[pr])
                            P.op("act", (lambda ps, half, s2: lambda e: e.activation(out=yo[s2][:, half * 512:(half + 1) * 512], in_=ps[:], func=AF.Copy))(ps, half, s2),
                                 reads=[pr], writes=[yo_r[s2]])
                        P.op("dve", lambda e: e.memset(st3[:, 0:1], 0.0), writes=[st3_r])
                        P.op("act", (lambda s2: lambda e: e.activation(out=junk[:], in_=yo[s2][:], func=AF.Square, accum_out=st3[:, 0:1]))(s2),
                             reads=[yo_r[s2]], writes=[junk_r, st3_r])
                        P.op("act", lambda e: e.activation(out=st3[:, 1:2], in_=st3[:, 0:1], func=AF.Sqrt, bias=EPS, scale=1.0 / D), reads=[st3_r], writes=[st3_r])
                        P.op("dve", lambda e: e.reciprocal(out=st3[:, 2:3], in_=st3[:, 1:2]), reads=[st3_r], writes=[st3_r])
                        P.op("dve", (lambda s2: lambda e: e.scalar_tensor_tensor(out=yo[s2][:], in0=yo[s2][:], scalar=st3[:, 2:3], in1=GM[:],
                                                                                 op0=ALU.mult, op1=ALU.mult))(s2), reads=[yo_r[s2], st3_r, r_mod], writes=[yo_r[s2]])
                        P.op("pool", (lambda s2, sl, b: lambda e: e.tensor_tensor(out=yo[s2][:], in0=yo[s2][:], in1=xt[sl][:, b, :], op=ALU.add))(s2, sl, b),
                             reads=[yo_r[s2], xt_r[sl]], writes=[yo_r[s2]])
                        P.dma("pool", "x1w%d" % s2, (lambda lb, s2: lambda e: e.dma_start(out=X1[lb * 128:(lb + 1) * 128, :], in_=yo[s2][:]))(lb, s2),
                              reads=[yo_r[s2]])
                    p2c_ld(0)
                    p2c_nt(0)
                    p2c_A(0)
                    p2c_A(1)
                    for lb in range(16):
                        if lb % 2 == 0 and lb // 2 + 1 < 8:
                            p2c_nt(lb // 2 + 1)
                        p2c_B(lb)
                        if lb + 2 < 16:
                            p2c_A(lb + 2)
            _phase4()
        P.fence_all()
        x1_deps = []
        for key in ("x1w0", "x1w1"):
            rr = Res(key)
            rr.writer = ("dma:" + key, P.count["dma:" + key])
            x1_deps.append(rr)

        out_deps = []
        def _phase5():
            with contextlib.ExitStack() as st:
                W1 = sbuf(st, "W1", [128, 8, 8, 512], BF16); r_W1 = [Res() for _ in range(8)]
                W2 = sbuf(st, "W2", [128, 2, 32, 512], BF16); r_W2 = [Res(), Res()]
                for blk in range(8):
                    P.dma("pool", "W1_%d" % blk, (lambda blk: lambda e: e.dma_start(out=W1[:, blk, :, :], in_=w_ff1b[blk].rearrange("p (k n) -> p k n", k=8)))(blk),
                          writes=[r_W1[blk]])
                for half in range(2):
                    for q4 in range(4):
                        P.dma("pool", "W2_%d" % half, (lambda half, q4: lambda e: e.dma_start(out=W2[:, half, q4 * 8:(q4 + 1) * 8, :],
                                                                                    in_=w_ff2b[half][:, q4 * 4096:(q4 + 1) * 4096].rearrange("p (f n) -> p f n", f=8)))(half, q4),
                              writes=[r_W2[half]])
                xt = [sbuf(st, "x1t%d" % i, [128, 2, D], F32) for i in range(2)]; xt_r = [Res(), Res()]
                xn = sbuf(st, "xn3", [128, 2, D], BF16); xn_r = Res()
                hT = sbuf(st, "hT3", [128, 8, 256], BF16); hT_r = Res()
                junk = xn[:, 1, :]; junk_r = xn_r
                junkF = sbuf(st, "junkF", [128, 256], BF16); junkF_r = Res()
                st3q = sbuf(st, "st3q", [128, 4], F32)
                stat = sbuf(st, "stat4", [128, 12], F32); stat_r = Res(small=True)
                uT2 = [sbuf(st, "uT%d" % i, [128, 32, 256], BF16) for i in range(2)]; uT2_r = [Res(), Res()]
                _rl = sbuf(st, "rl0", [128, 512], F32); _rl_r = Res()
                rl = [_rl, _rl]; rl_r = [_rl_r, _rl_r]
                yo = [sbuf(st, "yo3_%d" % i, [128, D], F32) for i in range(2)]; yo_r = [Res(), Res()]
                st3 = sbuf(st, "st4", [128, 4], F32); st3_r = Res(small=True)
                x1v = X1.rearrange("(t b p) d -> t p b d", b=2, p=128)
                rctr = 0
                rctr_box = [0]
                def p3_nt(ti, part):
                    sl = ti % 2
                    norm_transpose(x1v[ti], 2, xt[sl], xt_r[sl], xn, xn_r, hT, hT_r, AFm, BFm, junk, junk_r, stat, stat_r, "x1t%d" % sl, dma_reads=x1_deps, part=part)
                def p3_F1(ti):
                    sl = ti % 2
                    uT, uT_r = uT2[ti % 2], uT2_r[ti % 2]
                    rctr = rctr_box[0]
                    for f2 in range(16):
                        ps, pr = bank()
                        for ff in range(2):
                            f = f2 * 2 + ff
                            for k in range(8):
                                P.op("pe", (lambda ps, ff, f, k: lambda e: e.matmul(ps[:, ff * 256:(ff + 1) * 256], lhsT=W1[:, f // 4, k, (f % 4) * 128:(f % 4 + 1) * 128], rhs=hT[:, k, 0:256],
                                                                                    start=(k == 0), stop=(k == 7)))(ps, ff, f, k),
                                     reads=[hT_r, r_W1[f // 4]], writes=[pr])
                        ri = rctr % 2
                        rctr += 1
                        P.op("act", (lambda ps, ri: lambda e: e.activation(out=rl[ri][:], in_=ps[:], func=AF.Relu))(ps, ri), reads=[pr], writes=[rl_r[ri]])
                        P.op("pool", (lambda f2, ri: lambda e: e.tensor_tensor(out=uT[:, f2 * 2:f2 * 2 + 2, :], in0=rl[ri][:].rearrange("p (a t) -> p a t", t=256),
                                                                               in1=rl[ri][:].rearrange("p (a t) -> p a t", t=256), op=ALU.mult))(f2, ri),
                             reads=[rl_r[ri]], writes=[uT_r])
                    rctr_box[0] = rctr
                def p3_F2(ti, b):
                    sl = ti % 2
                    uT, uT_r = uT2[ti % 2], uT2_r[ti % 2]
                    lb = ti * 2 + b
                    s2 = lb % 2
                    for half in range(2):
                        ps, pr = bank()
                        for f in range(32):
                            P.op("pe", (lambda ps, f, b, half: lambda e: e.matmul(ps[:], lhsT=uT[:, f, b * 128:(b + 1) * 128], rhs=W2[:, half, f, :],
                                                                                  start=(f == 0), stop=(f == 31)))(ps, f, b, half),
                                 reads=[uT_r, r_W2[half]], writes=[pr])
                        P.op("act", (lambda ps, half, s2: lambda e: e.activation(out=yo[s2][:, half * 512:(half + 1) * 512], in_=ps[:], func=AF.Copy))(ps, half, s2),
                             reads=[pr], writes=[yo_r[s2]])
                    P.op("dve", lambda e: e.memset(st3q[:], 0.0), writes=[st3_r])
                    for q_ in range(4):
                        P.op("act", (lambda s2, q_: lambda e: e.activation(out=junkF[:], in_=yo[s2][:, q_ * 256:(q_ + 1) * 256], func=AF.Square,
                                                                           accum_out=st3q[:, q_:q_ + 1]))(s2, q_),
                             reads=[yo_r[s2]], writes=[junkF_r, st3_r])
                    P.op("dve", lambda e: e.tensor_reduce(out=st3[:, 0:1], in_=st3q[:], axis=AX.X, op=ALU.add), reads=[st3_r], writes=[st3_r])
                    P.op("act", lambda e: e.activation(out=st3[:, 1:2], in_=st3[:, 0:1], func=AF.Sqrt, bias=EPS, scale=1.0 / D), reads=[st3_r], writes=[st3_r])
                    P.op("dve", lambda e: e.reciprocal(out=st3[:, 2:3], in_=st3[:, 1:2]), reads=[st3_r], writes=[st3_r])
                    P.op("dve", (lambda s2: lambda e: e.scalar_tensor_tensor(out=yo[s2][:], in0=yo[s2][:], scalar=st3[:, 2:3], in1=GF[:],
                                                                             op0=ALU.mult, op1=ALU.mult))(s2), reads=[yo_r[s2], st3_r, r_mod], writes=[yo_r[s2]])
                    P.op("dve", (lambda s2, sl, b: lambda e: e.tensor_tensor(out=yo[s2][:], in0=yo[s2][:], in1=xt[sl][:, b, :], op=ALU.add))(s2, sl, b),
                         reads=[yo_r[s2], xt_r[sl]], writes=[yo_r[s2]])
                    P.dma("pool", "out%d" % s2, (lambda lb, s2: lambda e: e.dma_start(out=out[lb * 128:(lb + 1) * 128, :], in_=yo[s2][:]))(lb, s2),
                          reads=[yo_r[s2]])
                p3_nt(0, "all")
                p3_F1(0)
                for ti in range(8):
                    if ti + 1 < 8:
                        p3_nt(ti + 1, "pre")
                    p3_F2(ti, 0)
                    if ti + 1 < 8:
                        p3_nt(ti + 1, "tr")
                        p3_F1(ti + 1)
                    p3_F2(ti, 1)
                for key in ("out0", "out1"):
                    rr = Res(key)
                    rr.writer = ("dma:" + key, P.count["dma:" + key])
                    out_deps.append(rr)
        _phase5()
        P.x1_deps = x1_deps
        fin = list(out_deps)
        if debug:
            for key in ("dbg", "dbg2"):
                rr = Res(key); rr.writer = ("dma:" + key, P.count["dma:" + key]); fin.append(rr)
        P.emit(final_waits=fin)
    return nc


def _prep_inputs(inp):
    f32 = np.float32
    x = np.asarray(inp["x"], f32)
    c = np.asarray(inp["c"], f32)
    L = 0
    w_in = np.asarray(inp["w_in"], f32)[L]
    splits = np.cumsum([512, 512, 1024, 1024, 4, 4, 384, 256, 64, 1024, 1024])[:-1]
    q_a, k_a, v_a, o_a, i_a, f_a, c_q, c_kv, k_pe, g_a, g_b = np.split(w_in, splits, axis=1)
    swap64 = np.concatenate([np.arange(32, 64), np.arange(0, 32)])
    wA = np.ascontiguousarray(np.concatenate([q_a, k_a, v_a, i_a, f_a, c_kv, k_pe, k_pe[:, swap64]], axis=1))
    wB = np.ascontiguousarray(np.concatenate([o_a, g_a, g_b, c_q], axis=1))
    assert wA.shape[1] == A_COLS and wB.shape[1] == B_COLS
    w_uq = np.asarray(inp["w_uq"], f32)[L].reshape(384, 8, 192)
    nope = w_uq[:, :, :128].reshape(384, 1024)
    pe = w_uq[:, :, 128:]
    pe2 = np.concatenate([pe, pe[:, :, swap64]], axis=2).reshape(384, 1024)
    w_uq2 = np.ascontiguousarray(np.concatenate([nope, pe2], axis=1))
    w_ukv = np.asarray(inp["w_ukv"], f32)[L].reshape(256, 8, 256)
    w_ukv2 = np.ascontiguousarray(np.concatenate([w_ukv[:, :, :128].reshape(256, 1024), w_ukv[:, :, 128:].reshape(256, 1024)], axis=1))
    fm = lambda v, k: np.ascontiguousarray(np.asarray(v, f32).reshape(k, 128).T)
    conv_w = np.asarray(inp["mlstm_conv_w"], f32)[L]
    conv_w_t = np.ascontiguousarray(conv_w.reshape(4, 8, 128).transpose(2, 1, 0))
    half = 32
    inv_freq = 10000.0 ** (-np.arange(half, dtype=np.float64) / half)
    pos = np.arange(S, dtype=np.float64)
    ang = pos[:, None] * inv_freq[None, :]
    cos, sin = np.cos(ang).astype(f32).T, np.sin(ang).astype(f32).T
    rope_tab = np.ascontiguousarray(np.concatenate([cos, cos, -sin, sin], axis=0))
    tri = np.triu(np.ones((128, 128), f32))
    common = dict(
        w_ada=np.ascontiguousarray(np.asarray(inp["w_ada"], f32)[L]),
        b_ada=np.ascontiguousarray(np.asarray(inp["b_ada"], f32)[L].reshape(1, -1)),
        wA=wA, wB=wB, w_uq2=w_uq2, w_ukv2=w_ukv2,
        w_out=np.ascontiguousarray(np.asarray(inp["w_out"], f32)[L]),
        w_ff1b=np.ascontiguousarray(np.asarray(inp["w_ff1"], f32)[L].reshape(8, 128, 8, 512).transpose(2, 1, 0, 3).reshape(8, 128, 4096)),
        w_ff2b=np.ascontiguousarray(np.asarray(inp["w_ff2"], f32)[L].reshape(32, 128, 2, 512).transpose(2, 1, 0, 3).reshape(2, 128, 16384)),
        wBg_b=np.ascontiguousarray(wB[:, :3072].reshape(8, 128, 6, 512).transpose(2, 1, 0, 3).reshape(6, 128, 4096)),
        npre_mix_t=fm(inp["norm_pre_mix"][L], 8), npre_mlp_t=fm(inp["norm_pre_mlp"][L], 8),
        npost_mix=np.ascontiguousarray(np.asarray(inp["norm_post_mix"], f32)[L].reshape(1, -1)),
        npost_mlp=np.ascontiguousarray(np.asarray(inp["norm_post_mlp"], f32)[L].reshape(1, -1)),
        conv_w_t=conv_w_t, conv_b_t=fm(inp["mlstm_conv_b"][L], 8),
        gate_b4=np.ascontiguousarray(np.tile(np.asarray(inp["mlstm_gate_b"], f32)[L], 4).reshape(1, 32)),
        head_norm=np.ascontiguousarray(np.asarray(inp["mlstm_head_norm"], f32)[L].reshape(1, -1)),
        qn_t=fm(inp["mla_q_norm"][L], 3), kvn_t=fm(inp["mla_kv_norm"][L], 2),
        rope_all=rope_tab,
    )
    maps = []
    for core in range(8):
        b, j = core // 2, core % 2
        xb = x[b]
        own = xb.reshape(16, 2, 128, D)[:, j].reshape(NOWN, D)
        own_pos = (np.arange(S).reshape(16, 2, 128)[:, j]).reshape(-1)
        am = np.zeros((128, 2, 128), f32)
        if j == 0:
            am[:, 0, :] = tri
        else:
            am[:, 0, :] = 1.0
            am[:, 1, :] = tri
        sm = np.zeros((128, 2), f32)
        sm[:, j] = 1.0
        m = dict(common)
        m.update(x_all=np.ascontiguousarray(xb), x_own=np.ascontiguousarray(own), c_t=fm(c[b], 8),
                 rope_own=np.ascontiguousarray(rope_tab[:, own_pos]), amask=am, selm=sm)
        maps.append(m)
    return maps


_NC_CACHE = {}


def run(inputs, debug=False, cores=8):
    maps = _prep_inputs(inputs)
    if debug not in _NC_CACHE:
        _NC_CACHE[debug] = build_program(debug)
    nc = _NC_CACHE[debug]
    res = run_bass_kernel_spmd(nc, maps[:cores], core_ids=list(range(cores)))
    return res


def kernel(**inputs):
    res = run(inputs)
    outp = np.zeros((4, S, D), np.float32)
    ov = outp.reshape(4, 16, 2, 128, D)
    for core in range(8):
        b, j = core // 2, core % 2
        ov[b, :, j] = np.asarray(res.results[core]["out"], np.float32).reshape(16, 128, D)
    return outp
```
